# Optimizing a Trainium2 kernel written in Bass

```python
import jax
import jax.numpy as jnp
from jax import lax
import numpy as np

D_MODEL = 1024
BATCH = 16
SEQ = 2048
DEPTH = 2

D_FF = 2816
NORM_EPS = 1e-6
N_BRANCH = 3

GLA_HEADS = 4
GLA_DK = 64
GLA_DV = 128
GLA_QK = GLA_HEADS * GLA_DK
GLA_V = GLA_HEADS * GLA_DV
GLA_RANK = 16
GLA_TAU = 16.0
GLA_CHUNK = 64

FNET_GROUPS = 4
FNET_GC = 128
FNET_W = FNET_GROUPS * FNET_GC

RWKV_HEADS = 8
RWKV_N = 64
RWKV_W = RWKV_HEADS * RWKV_N
RWKV_DECAY_RANK = 32
RWKV_A_RANK = 32
RWKV_GATE_RANK = 96
RWKV_LN_EPS = 64e-5

RWKV_SPLITS = (RWKV_W, RWKV_W, RWKV_W, RWKV_DECAY_RANK, RWKV_DECAY_RANK, RWKV_A_RANK, RWKV_A_RANK, RWKV_GATE_RANK)
RWKV_COLS = 3 * RWKV_W + 2 * RWKV_DECAY_RANK + 2 * RWKV_A_RANK + RWKV_GATE_RANK
IN_SPLITS = (GLA_QK, GLA_QK, GLA_V, GLA_V, GLA_RANK, GLA_RANK, FNET_W, RWKV_COLS, N_BRANCH * D_MODEL)
IN_COLS = 2 * GLA_QK + 2 * GLA_V + 2 * GLA_RANK + FNET_W + RWKV_COLS + N_BRANCH * D_MODEL

kernel_name = 'hybrid_gla_fnet_rwkv7_macaron_encoder'


def split_cols(t, sizes):
    return jnp.split(t, [int(i) for i in np.cumsum(sizes)[:-1]], axis=-1)


def flip_seq(t):
    return jnp.flip(t, axis=1)


def rms_norm(x, g):
    xf = x.astype(jnp.float32)
    y = xf * lax.rsqrt(jnp.mean(xf * xf, axis=-1, keepdims=True) + NORM_EPS)
    return (y * g.astype(jnp.float32)).astype(x.dtype)


def swiglu(h, w_gate, w_up, w_down):
    return (jax.nn.silu(h @ w_gate) * (h @ w_up)) @ w_down


def gla_chunked(q, k, v, log_a, strict):
    B, S, H, dk = q.shape
    dv = v.shape[-1]
    L = GLA_CHUNK
    nc = S // L

    def chunk(t):
        return t.reshape(B, nc, L, H, t.shape[-1])

    q, k, v, log_a = chunk(q), chunk(k), chunk(v), chunk(log_a)
    b = jnp.cumsum(log_a, axis=2)
    b_last = b[:, :, -1:]
    q_dec = q * jnp.exp(b)
    k_dec = k * jnp.exp(-b)
    k_tail = k * jnp.exp(b_last - b)
    scores = jnp.einsum('bclhk,bcmhk->bchlm', q_dec, k_dec)
    idx = jnp.arange(L)
    mask = (idx[:, None] > idx[None, :]) if strict else (idx[:, None] >= idx[None, :])
    scores = jnp.where(mask, scores, 0.0)
    o_intra = jnp.einsum('bchlm,bcmhv->bclhv', scores, v)
    u = jnp.einsum('bclhk,bclhv->bchkv', k_tail, v)
    chunk_decay = jnp.exp(b_last[:, :, 0])

    def step(state, inp):
        u_c, d_c = inp
        return state * d_c[..., None] + u_c, state

    init = jnp.zeros((B, H, dk, dv), jnp.float32)
    _, prev = lax.scan(step, init, (jnp.moveaxis(u, 1, 0), jnp.moveaxis(chunk_decay, 1, 0)))
    prev = jnp.moveaxis(prev, 0, 1)
    o_inter = jnp.einsum('bclhk,bchkv->bclhv', q_dec, prev)
    return (o_intra + o_inter).reshape(B, S, H, dv)


def gla_branch(q, k, v, r, dn_f, dn_b, up_f, bias_f, up_b, bias_b, norm_g):
    B, S, _ = q.shape
    f32 = jnp.float32

    def heads(t, d):
        return t.astype(f32).reshape(B, S, GLA_HEADS, d)

    q = heads(q, GLA_DK) * (GLA_DK ** -0.5)
    k = heads(k, GLA_DK)
    v = heads(v, GLA_DV)

    def log_decay(dn, up, bias):
        z = dn.astype(f32) @ up.astype(f32) + bias.astype(f32)
        return heads(jax.nn.log_sigmoid(z) / GLA_TAU, GLA_DK)

    la_f = log_decay(dn_f, up_f, bias_f)
    la_b = log_decay(dn_b, up_b, bias_b)
    o_f = gla_chunked(q, k, v, la_f, strict=False)
    o_b = flip_seq(gla_chunked(flip_seq(q), flip_seq(k), flip_seq(v), flip_seq(la_b), strict=True))
    o = o_f + o_b
    o = o * lax.rsqrt(jnp.mean(o * o, axis=-1, keepdims=True) + NORM_EPS)
    o = o.reshape(B, S, GLA_V) * norm_g.astype(f32)
    return o * jax.nn.silu(r.astype(f32))


def fnet_branch(u):
    B, S, _ = u.shape
    z = u.astype(jnp.float32).reshape(B, S, FNET_GROUPS, FNET_GC)
    y = jnp.fft.fft2(z, axes=(1, 3), norm='ortho').real
    return y.astype(jnp.float32).reshape(B, S, FNET_W)


def centred_shift(u, mu):
    prev = jnp.pad(u[:, :-1], ((0, 0), (1, 0), (0, 0)))
    nxt = jnp.pad(u[:, 1:], ((0, 0), (0, 1), (0, 0)))
    return u + mu * (0.5 * (prev + nxt) - u)


def rwkv_scan(r, w, k, v, kk, a, strict):
    B, S, H, N = r.shape

    def step(state, inp):
        r_t, w_t, k_t, v_t, kk_t, a_t = inp
        sa = jnp.einsum('bhvk,bhk->bhv', state, -kk_t)
        new = (state * w_t[:, :, None, :]
               + sa[..., None] * (kk_t * a_t)[:, :, None, :]
               + v_t[..., None] * k_t[:, :, None, :])
        read = state if strict else new
        return new, jnp.einsum('bhvk,bhk->bhv', read, r_t)

    xs = tuple(jnp.moveaxis(t, 1, 0) for t in (r, w, k, v, kk, a))
    _, y = lax.scan(step, jnp.zeros((B, H, N, N), jnp.float32), xs)
    return jnp.moveaxis(y, 0, 1)


def rwkv7_branch(u, mu, w0_f, w2_f, w0_b, w2_b, a0_f, a2_f, a0_b, a2_b, g2, k_k, k_a, r_k, ln_g, ln_b):
    B, S, _ = u.shape
    f32 = jnp.float32
    u = centred_shift(u.astype(f32), mu.astype(f32))
    r, k, v, wd_f, wd_b, ad_f, ad_b, gd = split_cols(u, RWKV_SPLITS)

    def heads(t):
        return t.reshape(B, S, RWKV_HEADS, RWKV_N)

    def decay(wd, w0, w2):
        w = -jax.nn.softplus(-(w0.astype(f32) + jnp.tanh(wd) @ w2.astype(f32))) - 0.5
        return heads(jnp.exp(-jnp.exp(w)))

    def icl_rate(ad, a0, a2):
        return heads(jax.nn.sigmoid(a0.astype(f32) + ad @ a2.astype(f32)))

    w_f, w_b = decay(wd_f, w0_f, w2_f), decay(wd_b, w0_b, w2_b)
    a_f, a_b = icl_rate(ad_f, a0_f, a2_f), icl_rate(ad_b, a0_b, a2_b)
    g = jax.nn.sigmoid(gd) @ g2.astype(f32)
    r, k, v = heads(r), heads(k), heads(v)
    k_k = k_k.astype(f32).reshape(RWKV_HEADS, RWKV_N)
    k_a = k_a.astype(f32).reshape(RWKV_HEADS, RWKV_N)
    r_k = r_k.astype(f32).reshape(RWKV_HEADS, RWKV_N)
    kk = k * k_k
    kk = kk * lax.rsqrt(jnp.sum(kk * kk, axis=-1, keepdims=True) + 1e-12)
    k_f = k * (1.0 + (a_f - 1.0) * k_a)
    k_b = k * (1.0 + (a_b - 1.0) * k_a)
    y_f = rwkv_scan(r, w_f, k_f, v, kk, a_f, strict=False)
    y_b = flip_seq(rwkv_scan(flip_seq(r), flip_seq(w_b), flip_seq(k_b), flip_seq(v),
                             flip_seq(kk), flip_seq(a_b), strict=True))
    y = y_f + y_b
    mean = jnp.mean(y, axis=-1, keepdims=True)
    var = jnp.mean((y - mean) ** 2, axis=-1, keepdims=True)
    y = ((y - mean) * lax.rsqrt(var + RWKV_LN_EPS)).reshape(B, S, RWKV_W)
    y = y * ln_g.astype(f32) + ln_b.astype(f32)
    bonus = (jnp.sum(r * k_f * r_k, axis=-1, keepdims=True) * v).reshape(B, S, RWKV_W)
    return (y + bonus) * g


def hybrid_layer(x, ffn1_norm, ffn1_gate, ffn1_up, ffn1_down, mix_norm, w_in,
                 gla_up_f, gla_bias_f, gla_up_b, gla_bias_b, gla_norm,
                 rwkv_mu, rwkv_w0_f, rwkv_w2_f, rwkv_w0_b, rwkv_w2_b,
                 rwkv_a0_f, rwkv_a2_f, rwkv_a0_b, rwkv_a2_b, rwkv_g2,
                 rwkv_k_k, rwkv_k_a, rwkv_r_k, rwkv_ln_g, rwkv_ln_b,
                 proj_gla, proj_fnet, proj_rwkv, w_out,
                 ffn2_norm, ffn2_gate, ffn2_up, ffn2_down):
    B, S, D = x.shape
    x = x + 0.5 * swiglu(rms_norm(x, ffn1_norm), ffn1_gate, ffn1_up, ffn1_down)
    h = rms_norm(x, mix_norm)
    z = h @ w_in
    q, k, v, r, dn_f, dn_b, u_fnet, u_rwkv, gate = split_cols(z, IN_SPLITS)
    y_a = gla_branch(q, k, v, r, dn_f, dn_b, gla_up_f, gla_bias_f, gla_up_b, gla_bias_b, gla_norm).astype(x.dtype)
    y_b = fnet_branch(u_fnet).astype(x.dtype)
    y_c = rwkv7_branch(u_rwkv, rwkv_mu, rwkv_w0_f, rwkv_w2_f, rwkv_w0_b, rwkv_w2_b,
                       rwkv_a0_f, rwkv_a2_f, rwkv_a0_b, rwkv_a2_b, rwkv_g2,
                       rwkv_k_k, rwkv_k_a, rwkv_r_k, rwkv_ln_g, rwkv_ln_b).astype(x.dtype)
    gates = jax.nn.sigmoid(gate).reshape(B, S, N_BRANCH, D)
    merged = (gates[:, :, 0] * (y_a @ proj_gla)
              + gates[:, :, 1] * (y_b @ proj_fnet)
              + gates[:, :, 2] * (y_c @ proj_rwkv))
    x = x + merged @ w_out
    x = x + 0.5 * swiglu(rms_norm(x, ffn2_norm), ffn2_gate, ffn2_up, ffn2_down)
    return x


def setup_inputs(seed: int = 0) -> dict:
    key = jax.random.key(seed)
    ks = iter(jax.random.split(key, 64))

    def nrm(shape, scale=1.0):
        return scale * jax.random.normal(next(ks), shape, jnp.float32)

    def uni(shape, lo, hi):
        return jax.random.uniform(next(ks), shape, jnp.float32, lo, hi)

    L, D, F = DEPTH, D_MODEL, D_FF
    return {
        'x': nrm((BATCH, SEQ, D)),
        'ffn1_norm': 1.0 + nrm((L, D), 0.02),
        'ffn1_gate': nrm((L, D, F), D ** -0.5),
        'ffn1_up': nrm((L, D, F), D ** -0.5),
        'ffn1_down': nrm((L, F, D), F ** -0.5),
        'mix_norm': 1.0 + nrm((L, D), 0.02),
        'w_in': nrm((L, D, IN_COLS), D ** -0.5),
        'gla_up_f': nrm((L, GLA_RANK, GLA_QK), GLA_RANK ** -0.5),
        'gla_bias_f': nrm((L, GLA_QK), 0.1),
        'gla_up_b': nrm((L, GLA_RANK, GLA_QK), GLA_RANK ** -0.5),
        'gla_bias_b': nrm((L, GLA_QK), 0.1),
        'gla_norm': 1.0 + nrm((L, GLA_V), 0.02),
        'rwkv_mu': uni((L, RWKV_COLS), 0.0, 1.0),
        'rwkv_w0_f': -1.0 + nrm((L, RWKV_W), 0.5),
        'rwkv_w2_f': nrm((L, RWKV_DECAY_RANK, RWKV_W), RWKV_DECAY_RANK ** -0.5),
        'rwkv_w0_b': -1.0 + nrm((L, RWKV_W), 0.5),
        'rwkv_w2_b': nrm((L, RWKV_DECAY_RANK, RWKV_W), RWKV_DECAY_RANK ** -0.5),
        'rwkv_a0_f': nrm((L, RWKV_W), 0.1),
        'rwkv_a2_f': nrm((L, RWKV_A_RANK, RWKV_W), RWKV_A_RANK ** -0.5),
        'rwkv_a0_b': nrm((L, RWKV_W), 0.1),
        'rwkv_a2_b': nrm((L, RWKV_A_RANK, RWKV_W), RWKV_A_RANK ** -0.5),
        'rwkv_g2': nrm((L, RWKV_GATE_RANK, RWKV_W), RWKV_GATE_RANK ** -0.5),
        'rwkv_k_k': 0.85 + nrm((L, RWKV_W), 0.02),
        'rwkv_k_a': 1.0 + nrm((L, RWKV_W), 0.02),
        'rwkv_r_k': nrm((L, RWKV_W), 0.1),
        'rwkv_ln_g': 1.0 + nrm((L, RWKV_W), 0.02),
        'rwkv_ln_b': nrm((L, RWKV_W), 0.02),
        'proj_gla': nrm((L, GLA_V, D), GLA_V ** -0.5),
        'proj_fnet': nrm((L, FNET_W, D), FNET_W ** -0.5),
        'proj_rwkv': nrm((L, RWKV_W, D), RWKV_W ** -0.5),
        'w_out': nrm((L, D, D), D ** -0.5),
        'ffn2_norm': 1.0 + nrm((L, D), 0.02),
        'ffn2_gate': nrm((L, D, F), D ** -0.5),
        'ffn2_up': nrm((L, D, F), D ** -0.5),
        'ffn2_down': nrm((L, F, D), F ** -0.5),
        'final_norm': 1.0 + nrm((D,), 0.02),
    }


def reference(x, ffn1_norm, ffn1_gate, ffn1_up, ffn1_down, mix_norm, w_in,
              gla_up_f, gla_bias_f, gla_up_b, gla_bias_b, gla_norm,
              rwkv_mu, rwkv_w0_f, rwkv_w2_f, rwkv_w0_b, rwkv_w2_b,
              rwkv_a0_f, rwkv_a2_f, rwkv_a0_b, rwkv_a2_b, rwkv_g2,
              rwkv_k_k, rwkv_k_a, rwkv_r_k, rwkv_ln_g, rwkv_ln_b,
              proj_gla, proj_fnet, proj_rwkv, w_out,
              ffn2_norm, ffn2_gate, ffn2_up, ffn2_down, final_norm):
    for l in range(DEPTH):
        x = hybrid_layer(x, ffn1_norm[l], ffn1_gate[l], ffn1_up[l], ffn1_down[l], mix_norm[l], w_in[l],
                         gla_up_f[l], gla_bias_f[l], gla_up_b[l], gla_bias_b[l], gla_norm[l],
                         rwkv_mu[l], rwkv_w0_f[l], rwkv_w2_f[l], rwkv_w0_b[l], rwkv_w2_b[l],
                         rwkv_a0_f[l], rwkv_a2_f[l], rwkv_a0_b[l], rwkv_a2_b[l], rwkv_g2[l],
                         rwkv_k_k[l], rwkv_k_a[l], rwkv_r_k[l], rwkv_ln_g[l], rwkv_ln_b[l],
                         proj_gla[l], proj_fnet[l], proj_rwkv[l], w_out[l],
                         ffn2_norm[l], ffn2_gate[l], ffn2_up[l], ffn2_down[l])
    return rms_norm(x, final_norm)
```

```python
import contextlib
import numpy as np
import ml_dtypes
import concourse.bass as bass
import concourse.mybir as mybir
from concourse.bass_utils import run_bass_kernel_spmd

F32 = mybir.dt.float32
BF16 = mybir.dt.bfloat16
AF = mybir.ActivationFunctionType
ALU = mybir.AluOpType
AX = mybir.AxisListType

ENGS = ("pe", "act", "dve", "pool", "sp")
N_DMA_SEMS = 24


class Buf:
    def __init__(self, t, name):
        self.t = t
        self.name = name
        self.st = {}
        self.is_psum = False

    def _parts(self, part):
        if part is None:
            return list(self.st.keys()) or [None]
        ks = [part]
        if None in self.st:
            ks.append(None)
        return ks

    def deps_for_read(self, part):
        out = set()
        for k in self._parts(part):
            s = self.st.get(k)
            if s and s[0] is not None:
                out.add(s[0])
            if s and self.is_psum:
                out.update(s[1])
        return out

    def deps_for_write(self, part):
        out = set()
        for k in self._parts(part):
            s = self.st.get(k)
            if s:
                if s[0] is not None:
                    out.add(s[0])
                out.update(s[1])
        return out

    def note_read(self, part, ev):
        s = self.st.setdefault(part, [None, []])
        s[1].append(ev)
        if len(s[1]) > 12:
            best = {}
            for (sm, v) in s[1]:
                if best.get(sm, -1) < v:
                    best[sm] = v
            s[1] = [(sm, v) for sm, v in best.items()]

    def note_write(self, part, ev):
        if part is None:
            self.st = {None: [ev, []]}
        else:
            self.st[part] = [ev, []]

    def __getitem__(self, idx):
        return self.t[idx]

    def ap(self):
        return self.t.ap() if hasattr(self.t, "ap") and callable(self.t.ap) else self.t


class Sched:
    def __init__(self, nc, stack):
        self.nc = nc
        self.stack = stack
        self.eng = {"pe": nc.tensor, "act": nc.scalar, "dve": nc.vector,
                    "pool": nc.gpsimd, "sp": nc.sync}
        self.prog = {e: [] for e in ENGS}
        self.cnt = {e: 0 for e in ENGS}
        self.sems = {}
        for e in ENGS:
            self.sems["E" + e] = stack.enter_context(nc.semaphore("sem_" + e))
        self.dma_sems = []
        for i in range(N_DMA_SEMS):
            k = "D%d" % i
            self.sems[k] = stack.enter_context(nc.semaphore("sem_d%d" % i))
            self.dma_sems.append([k, 0])
        self.dma_rrs = {}
        self.waited = {e: {} for e in ENGS}
        self.nwaits = 0
        self.out_events = []

    def _nm(self, name):
        self.uid = getattr(self, "uid", 0) + 1
        return "%s_u%d" % (name, self.uid)

    def sbuf(self, name, shape, dt):
        name = self._nm(name)
        t = self.stack.enter_context(self.nc.sbuf_tensor(name, list(shape), dt))
        return Buf(t, name)

    def psum(self, name, shape, dt=F32):
        name = self._nm(name)
        t = self.stack.enter_context(self.nc.psum_tensor(name, list(shape), dt))
        b = Buf(t, name)
        b.is_psum = True
        return b

    def dram(self, name, shape, dt, kind="Internal"):
        t = self.nc.dram_tensor(name, list(shape), dt, kind=kind)
        return Buf(t, name)

    def _wait(self, e, ev):
        sm, v = ev
        if self.waited[e].get(sm, -1) >= v:
            return
        if sm == "Epe" and e == "pe":
            return
        self.waited[e][sm] = v
        self.prog[e].append(("w", sm, v))
        self.nwaits += 1

    @staticmethod
    def _norm(lst):
        out = []
        for x in lst:
            if isinstance(x, Buf):
                out.append((x, None))
            else:
                out.append(x)
        return out

    def op(self, e, fn, reads=(), writes=(), **kw):
        if isinstance(fn, str):
            name = fn

            def fn(eng, name=name, kw=kw):
                return getattr(eng, name)(**kw)
        reads = self._norm(reads)
        writes = self._norm(writes)
        deps = set()
        for b, p in reads:
            deps |= b.deps_for_read(p)
        for b, p in writes:
            deps |= b.deps_for_write(p)
        for ev in sorted(deps):
            self._wait(e, ev)
        self.cnt[e] += 1
        ev = ("E" + e, self.cnt[e])
        self.prog[e].append(("o", fn, "E" + e, 1))
        for b, p in reads:
            b.note_read(p, ev)
        for b, p in writes:
            b.note_write(p, ev)
        return ev

    def dma(self, e, out_ap, in_ap, reads=(), writes=(), **kw):
        reads = self._norm(reads)
        writes = self._norm(writes)
        deps = set()
        for b, p in reads:
            deps |= b.deps_for_read(p)
        for b, p in writes:
            deps |= b.deps_for_write(p)
        half = N_DMA_SEMS // 2
        base = 0 if e == "pool" else half
        rr = self.dma_rrs.get(e == "pool", 0)
        slot = self.dma_sems[base + rr]
        self.dma_rrs[e == "pool"] = (rr + 1) % half
        if slot[1] > 0:
            deps.add((slot[0], slot[1]))
        for ev in sorted(deps):
            self._wait(e, ev)
        slot[1] += 16
        ev = (slot[0], slot[1])

        def fn(eng, out_ap=out_ap, in_ap=in_ap, kw=kw):
            return eng.dma_start(out=out_ap, in_=in_ap, **kw)

        self.prog[e].append(("o", fn, slot[0], 16))
        for b, p in reads:
            b.note_read(p, ev)
        for b, p in writes:
            b.note_write(p, ev)
        return ev

    def wait_all(self, e, events):
        for ev in events:
            self._wait(e, ev)

    def emit(self):
        nc = self.nc
        with nc.Block() as block:
            def mk(e):
                def body(eng):
                    for it in self.prog[e]:
                        if it[0] == "w":
                            eng.wait_ge(self.sems[it[1]], it[2])
                        else:
                            ins = it[1](eng)
                            ins.then_inc(self.sems[it[2]], it[3])
                return body
            block.tensor(mk("pe"))
            block.scalar(mk("act"))
            block.vector(mk("dve"))
            block.gpsimd(mk("pool"))
            block.sync(mk("sp"))


def barrier(S):
    evs = [("E" + e, S.cnt[e]) for e in ENGS if S.cnt[e] > 0]
    evs += [(k, v) for k, v in S.dma_sems if v > 0]
    for e in ENGS:
        for ev in evs:
            S._wait(e, ev)


Sched.barrier = barrier


@contextlib.contextmanager
def _phase(S):
    S.barrier()
    old = S.stack
    with contextlib.ExitStack() as ps:
        S.stack = ps
        try:
            yield
        finally:
            S.barrier()
            S.stack = old


Sched.phase = _phase


def setup_consts(S, C, consts):
    C.rr = 0
    C.ident = S.sbuf("ident", [128, 128], BF16)
    C.ones_f = S.sbuf("ones_f", [128, 512], F32)
    C.mhalf = S.sbuf("mhalf", [128, 1], F32)
    C.junk = S.sbuf("junk", [128, 1024], BF16)
    C.ss = [S.sbuf("ss%d" % i, [128, 1], F32) for i in range(4)]
    C.rstd = [S.sbuf("rstd%d" % i, [128, 1], F32) for i in range(4)]
    C.xs = [S.sbuf("xs%d" % i, [128, 1024], BF16) for i in range(2)]
    C.psT = [S.psum("psT%d" % i, [128, 8, 128], BF16) for i in range(2)]
    C.gB = [S.sbuf("gB%d" % i, [128, 8, 128], F32) for i in range(1)]
    C.gcol = S.sbuf("gcol", [128, 8], F32)
    S.dma("sp", C.ident[:], consts["ident_bf"], writes=[C.ident])
    C.one_c = S.sbuf("one_c", [128, 1], F32)
    C.eps_c = S.sbuf("eps_c", [128, 1], F32)
    S.op("dve", "memset", [], [C.one_c], ap=C.one_c[:], constant=1.0)
    S.op("dve", "memset", [], [C.eps_c], ap=C.eps_c[:], constant=1e-6)
    C.consts = consts

    S.op("dve", lambda e: e.memset(C.ones_f[:], 1.0), writes=[C.ones_f])
    S.op("dve", lambda e: e.memset(C.mhalf[:], -0.5), writes=[C.mhalf])


def load_rwkv_consts(S, C):
    consts = C.consts
    C.rmask128 = S.sbuf("rmask128", [128, 1024], F32)
    C.onesblk = S.sbuf("onesblk", [128, 128], F32)
    C.pmask = S.sbuf("pmask", [128, 2], F32)
    C.ident_f = S.sbuf("ident_f", [128, 1, 128], F32)
    C.m4_f = S.sbuf("m4_f", [128, 4, 128], F32)
    C.m4_b = S.sbuf("m4_b", [128, 4, 128], F32)
    C.mB_f = S.sbuf("mB_f", [128, 1, 128], F32)
    C.mB_b = S.sbuf("mB_b", [128, 1, 128], F32)
    S.dma("sp", C.rmask128[:], consts["rmask128"], writes=[C.rmask128])
    S.dma("sp", C.onesblk[:], consts["onesblk"], writes=[C.onesblk])
    S.dma("sp", C.pmask[:], consts["pmask"], writes=[C.pmask])
    S.dma("sp", C.ident_f[:, 0, :], consts["ident_f"], writes=[C.ident_f])
    S.dma("sp", C.m4_f[:], consts["m4_f"], writes=[C.m4_f])
    S.dma("sp", C.m4_b[:], consts["m4_b"], writes=[C.m4_b])
    S.dma("sp", C.mB_f[:, 0, :], consts["mB_f"], writes=[C.mB_f])
    S.dma("sp", C.mB_b[:, 0, :], consts["mB_b"], writes=[C.mB_b])


def load_gla_consts(S, C):
    consts = C.consts
    C.rmask = S.sbuf("rmask", [128, 2048], F32)
    C.mask_f = S.sbuf("mask_f", [128, 1, 128], F32)
    C.mask_b = S.sbuf("mask_b", [128, 1, 128], F32)
    S.dma("sp", C.rmask[:], consts["rmask"], writes=[C.rmask])
    S.dma("sp", C.mask_f[:, 0, :], consts["mask_f"], writes=[C.mask_f])
    S.dma("sp", C.mask_b[:, 0, :], consts["mask_b"], writes=[C.mask_b])


D = 1024
DFF = 2816
NJ = DFF // 128
T = 4096
SEQ = 2048
TT = 256
EPS = 1e-6

ZR_RR, ZR_RK, ZR_LORA, ZR_GD = 0, 512, 1024, 1152
ZR_GQ, ZR_GK, ZR_GR, ZR_DN, ZR_FN = 1280, 1536, 1792, 2304, 2432
ZROWS = 2944
NZC = ZROWS // 128
RW0 = 2080
WZ_SEGS = [(RW0, 512, ZR_RR), (RW0 + 512, 512, ZR_RK), (RW0 + 1536, 128, ZR_LORA), (RW0 + 1664, 96, ZR_GD),
           (0, 256, ZR_GQ), (256, 256, ZR_GK), (1024, 512, ZR_GR), (1536, 32, ZR_DN), (1568, 512, ZR_FN)]
WV_SEGS = [(512, 512, 0), (RW0 + 1024, 512, 512)]


class Ctx:
    pass


def load_w_cast(S, dst, part, dst_ap, src_ap, inner):
    S.dma("pool", dst_ap.rearrange("p (a b) -> p a b", b=inner),
          src_ap.rearrange("p (a b) -> p a b", b=inner), writes=[(dst, part)])


def norm_tile(S, C, xd, t0, g_idx, xt, hT, col0):
    nsub = TT // 128
    S.dma("sp", xt[:], xd.ap()[t0:t0 + TT, :].rearrange("(s p) d -> p s d", p=128),
          reads=[xd], writes=[xt])
    for s in range(nsub):
        ss = C.ss[C.rr % 4]
        rstd = C.rstd[C.rr % 4]
        xs = C.xs[C.rr % 2]
        psT = C.psT[C.rr % 2]
        C.rr += 1
        S.op("act", "activation", [xt], [C.junk, ss], out=C.junk[:], in_=xt[:, s, :], func=AF.Square,
             accum_out=ss[:])
        S.op("dve", "tensor_scalar", [ss], [rstd], out=rstd[:], in0=ss[:], scalar1=1.0 / D, scalar2=EPS,
             op0=ALU.mult, op1=ALU.add)
        S.op("pool", "tensor_tensor", [rstd, C.mhalf], [rstd], out=rstd[:], in0=rstd[:], in1=C.mhalf[:],
             op=ALU.pow)
        S.op("act", "activation", [xt, rstd], [xs], out=xs[:], in_=xt[:, s, :], func=AF.Copy,
             scale=rstd[:, 0:1])
        for c in range(8):
            S.op("pe", "transpose", [xs, C.ident], [psT], out=psT[:, c, :], in_=xs[:, c * 128:(c + 1) * 128],
                 identity=C.ident[:])
        S.op("dve", "tensor_tensor", [psT, C.gB[g_idx]], [hT],
             out=hT[:, :, col0 + s * 128:col0 + (s + 1) * 128], in0=psT[:], in1=C.gB[g_idx][:], op=ALU.mult)


def load_gain(S, C, g_idx, gcol_ap):
    S.dma("sp", C.gcol[:], gcol_ap, writes=[C.gcol])
    for c in range(8):
        S.op("dve", "tensor_scalar", [C.gcol, C.ones_f], [C.gB[g_idx]], out=C.gB[g_idx][:, c, :],
             in0=C.ones_f[:, 0:128], scalar1=C.gcol[:, c:c + 1], scalar2=None, op0=ALU.mult)


def ffn_phase(S, C, xin, xout, gcol_ap, wg_ap, wu_ap, wd_ap, ntiles=T // TT):
    with S.phase():
        wg = S.sbuf("wg", [128, 8, DFF], BF16)
        wu = S.sbuf("wu", [128, 8, DFF], BF16)
        wd = S.sbuf("wd", [128, NJ, D], BF16)
        xt = [S.sbuf("xt%d" % i, [128, TT // 128, D], F32) for i in range(2)]
        ot = [S.sbuf("ot%d" % i, [128, TT // 128, D], F32) for i in range(2)]
        hT = [S.sbuf("hT%d" % i, [128, 8, TT], BF16) for i in range(2)]
        sg = [S.sbuf("sg%d" % i, [128, TT], BF16) for i in range(2)]
        act = [S.sbuf("act%d" % i, [128, TT], BF16) for i in range(3)]
        pgu = [S.psum("pgu%d" % i, [128, 2, TT], F32) for i in range(2)]
        pacc = [S.psum("pacc%d" % i, [128, 512], F32) for i in range(4)]
        load_gain(S, C, 0, gcol_ap)
        for c in range(8):
            load_w_cast(S, wg, c, wg[:, c, :], wg_ap[c * 128:(c + 1) * 128, :], 704)
            load_w_cast(S, wu, c, wu[:, c, :], wu_ap[c * 128:(c + 1) * 128, :], 704)
        for j in range(NJ):
            load_w_cast(S, wd, j, wd[:, j, :], wd_ap[j * 128:(j + 1) * 128, :], 1024)
        nsub = TT // 128
        for it in range(ntiles):
            t0 = it * TT
            x_t = xt[it % 2]
            h_t = hT[it % 2]
            o_t = ot[it % 2]
            norm_tile(S, C, xin, t0, 0, x_t, h_t, 0)
            for j in range(NJ):
                pg = pgu[j % 2]
                for (w, gi) in ((wg, 0), (wu, 1)):
                    for c in range(8):
                        S.op("pe", "matmul", [(w, c), h_t], [pg], out=pg[:, gi, :],
                             lhsT=w[:, c, j * 128:(j + 1) * 128], rhs=h_t[:, c, :], start=(c == 0), stop=(c == 7))
                sgj = sg[j % 2]
                aj = act[j % 3]
                S.op("act", "activation", [pg], [sgj], out=sgj[:], in_=pg[:, 0, :], func=AF.Silu)
                S.op("dve", "tensor_tensor", [pg, sgj], [aj], out=aj[:], in0=pg[:, 1, :], in1=sgj[:], op=ALU.mult)
                for s in range(nsub):
                    for hf in range(2):
                        pa = pacc[s * 2 + hf]
                        S.op("pe", "matmul", [aj, (wd, j)], [pa], out=pa[:], lhsT=aj[:, s * 128:(s + 1) * 128],
                             rhs=wd[:, j, hf * 512:(hf + 1) * 512], start=(j == 0), stop=(j == NJ - 1))
            for s in range(nsub):
                for hf in range(2):
                    pa = pacc[s * 2 + hf]
                    S.op("dve", "scalar_tensor_tensor", [pa, x_t], [o_t], out=o_t[:, s, hf * 512:(hf + 1) * 512],
                         in0=pa[:], scalar=0.5, in1=x_t[:, s, hf * 512:(hf + 1) * 512], op0=ALU.mult, op1=ALU.add)
            S.dma("sp", xout.ap()[t0:t0 + TT, :].rearrange("(s p) d -> p s d", p=128), o_t[:],
                  reads=[o_t], writes=[xout])


def mixin_phase(S, C, xin, zT, zv, gcol_ap, win_ap, mub_ap, nseq=2):
    with S.phase():
        wz = S.sbuf("wz", [128, 8, ZROWS], BF16)
        wv = S.sbuf("wv", [128, 8, 1024], BF16)
        wzb = S.sbuf("wzb", [128, 8, 1280], BF16)
        wvb = S.sbuf("wvb", [128, 8, 512], BF16)
        mub = S.sbuf("mub", [128, 1792], F32)
        hseq = S.sbuf("hseq", [128, 8, SEQ], BF16)
        hshs = [S.sbuf("hsh%d" % i, [128, 8, TT], BF16) for i in range(2)]
        xt = [S.sbuf("xt%d" % i, [128, TT // 128, D], F32) for i in range(2)]
        zst = [S.sbuf("zst%d" % i, [128, NZC, TT], BF16) for i in range(2)]
        zvst = [S.sbuf("zvst%d" % i, [128, TT // 128, 1024], BF16) for i in range(2)]
        pz = [S.psum("pz%d" % i, [128, 512], F32) for i in range(3)]
        pv = [S.psum("pv%d" % i, [128, 512], F32) for i in range(2)]
        load_gain(S, C, 0, gcol_ap)
        S.op("pool", "memset", [], [wz], ap=wz[:], constant=0.0)
        S.dma("sp", mub[:], mub_ap, writes=[mub])
        for i, (sc, n, dc) in enumerate(WZ_SEGS):
            S.dma("pool", wz[:, :, dc:dc + n], win_ap[:, sc:sc + n].rearrange("(c p) n -> p c n", p=128),
                  writes=[(wz, "s%d" % i)])
        for i, (sc, n, dc) in enumerate(WV_SEGS):
            S.dma("pool", wv[:, :, dc:dc + n], win_ap[:, sc:sc + n].rearrange("(c p) n -> p c n", p=128),
                  writes=[(wv, "s%d" % i)])
        for c in range(8):
            S.op("dve", "scalar_tensor_tensor", [wz, mub], [wzb], out=wzb[:, c, :], in0=wz[:, c, 0:1280],
                 scalar=0.5, in1=mub[:, 0:1280], op0=ALU.mult, op1=ALU.mult)
            S.op("dve", "scalar_tensor_tensor", [wv, mub], [wvb], out=wvb[:, c, :], in0=wv[:, c, 512:1024],
                 scalar=0.5, in1=mub[:, 1280:1792], op0=ALU.mult, op1=ALU.mult)
        S.op("dve", "tensor_scalar", [mub], [mub], out=mub[:], in0=mub[:], scalar1=-1.0, scalar2=1.0,
             op0=ALU.mult, op1=ALU.add)
        for c in range(8):
            S.op("pool", "tensor_tensor", [wz, mub], [wz], out=wz[:, c, 0:1280], in0=wz[:, c, 0:1280],
                 in1=mub[:, 0:1280], op=ALU.mult)
            S.op("pool", "tensor_tensor", [wv, mub], [wv], out=wv[:, c, 512:1024], in0=wv[:, c, 512:1024],
                 in1=mub[:, 1280:1792], op=ALU.mult)
        nev = 0
        for b in range(nseq):
            for it in range(SEQ // TT):
                norm_tile(S, C, xin, b * SEQ + it * TT, 0, xt[it % 2], hseq, it * TT)
            for it in range(SEQ // TT):
                c0 = it * TT
                hsh = hshs[it % 2]
                a0 = 1 if it == 0 else 0
                a1 = TT - 1 if it == SEQ // TT - 1 else TT
                S.op("pool", "tensor_tensor", [hseq], [hsh], out=hsh[:, :, a0:a1], in0=hseq[:, :, c0 + a0 - 1:c0 + a1 - 1],
                     in1=hseq[:, :, c0 + a0 + 1:c0 + a1 + 1], op=ALU.add)
                if it == 0:
                    S.op("pool", "tensor_copy", [hseq], [hsh], out=hsh[:, :, 0:1], in_=hseq[:, :, 1:2])
                if it == SEQ // TT - 1:
                    S.op("pool", "tensor_copy", [hseq], [hsh], out=hsh[:, :, TT - 1:TT], in_=hseq[:, :, SEQ - 2:SEQ - 1])
                zs = zst[it % 2]
                zvs = zvst[it % 2]
                for ck in range(NZC):
                    p = pz[ck % 3]
                    nmm = 16 if ck < 10 else 8
                    k = 0
                    for c in range(8):
                        S.op("pe", "matmul", [wz, hseq], [p], out=p[:, 0:TT], lhsT=wz[:, c, ck * 128:(ck + 1) * 128],
                             rhs=hseq[:, c, c0:c0 + TT], start=(k == 0), stop=(k == nmm - 1))
                        k += 1
                    if ck < 10:
                        for c in range(8):
                            S.op("pe", "matmul", [wzb, hsh], [p], out=p[:, 0:TT],
                                 lhsT=wzb[:, c, ck * 128:(ck + 1) * 128], rhs=hsh[:, c, :],
                                 start=False, stop=(k == nmm - 1))
                            k += 1
                    nev += 1
                    if nev % 2:
                        S.op("act", "activation", [p], [zs], out=zs[:, ck, :], in_=p[:, 0:TT], func=AF.Copy)
                    else:
                        S.op("dve", "tensor_copy", [p], [zs], out=zs[:, ck, :], in_=p[:, 0:TT])
                for s in range(TT // 128):
                    cs = c0 + s * 128
                    for vi in range(2):
                        p = pv[vi]
                        nmm = 16 if vi == 1 else 8
                        k = 0
                        for c in range(8):
                            S.op("pe", "matmul", [wv, hseq], [p], out=p[:], lhsT=hseq[:, c, cs:cs + 128],
                                 rhs=wv[:, c, vi * 512:(vi + 1) * 512], start=(k == 0), stop=(k == nmm - 1))
                            k += 1
                        if vi == 1:
                            for c in range(8):
                                S.op("pe", "matmul", [wvb, hsh], [p], out=p[:], lhsT=hsh[:, c, s * 128:(s + 1) * 128],
                                     rhs=wvb[:, c, :], start=False, stop=(k == nmm - 1))
                                k += 1
                        if vi == 0:
                            S.op("act", "activation", [p], [zvs], out=zvs[:, s, 0:512], in_=p[:], func=AF.Copy)
                        else:
                            S.op("dve", "tensor_copy", [p], [zvs], out=zvs[:, s, 512:1024], in_=p[:])
                tg = b * SEQ + c0
                S.dma("sp", zT.ap()[:, tg:tg + TT].rearrange("(c p) t -> p c t", p=128), zs[:],
                      reads=[zs], writes=[(zT, "w%d" % (tg // TT))])
                S.dma("sp", zv.ap()[tg:tg + TT, :].rearrange("(s p) d -> p s d", p=128), zvs[:],
                      reads=[zvs], writes=[(zv, "w%d" % (tg // TT))])


GLA_C = -1.0 / 16.0


def v3(ap, b=128):
    return ap.rearrange("p (a b) -> p a b", b=b)


def gla_phase(S, C, zT, zv, yT, A, nseq=2):
    with S.phase():
        load_gla_consts(S, C)
        qT = S.sbuf("qT", [128, 2, SEQ], BF16)
        kT = S.sbuf("kT", [128, 2, SEQ], BF16)
        rT = S.sbuf("rT", [128, 4, SEQ], BF16)
        vt = S.sbuf("vt", [128, SEQ // 128, 512], BF16)
        dn = [S.sbuf("dn%d" % i, [16, SEQ], BF16) for i in range(2)]
        up = [S.sbuf("up%d" % i, [16, 256], BF16) for i in range(2)]
        gvec = S.sbuf("gvec", [128, 8], F32)
        negb = S.sbuf("negb", [128, 4], F32)
        oacc = S.sbuf("oacc", [128, 4, SEQ], F32)
        qd = S.sbuf("qd", [128, 2, SEQ], BF16)
        kd = S.sbuf("kd", [128, 2, SEQ], BF16)
        ktT = S.sbuf("ktT", [128, 2, SEQ], BF16)
        kt = S.sbuf("kt", [128, SEQ // 128, 256], BF16)
        gam = S.sbuf("gam", [128, 2, 32], F32)
        L = S.sbuf("L", [128, SEQ], F32)
        cs = S.sbuf("cs", [128, SEQ], F32)
        t1 = S.sbuf("t1", [128, SEQ], F32)
        E = S.sbuf("E", [128, SEQ], F32)
        Hf = S.sbuf("Hf", [128, 2, 128], F32)
        Ht = S.sbuf("Ht", [128, 2, 128], F32)
        Hb = S.sbuf("Hb", [128, 2, 128], BF16)
        sTs = [S.sbuf("sTs%d" % i, [128, 2, 2, 128], BF16) for i in range(2)]
        sq = S.sbuf("sq", [128, 512], F32)
        rs = S.sbuf("rs", [128, 512], F32)
        sr = S.sbuf("sr", [128, 512], BF16)
        yst = [S.sbuf("yst%d" % i, [128, 512], BF16) for i in range(2)]
        bS = [S.psum("bS%d" % i, [128, 512], F32) for i in range(2)]
        bO = [S.psum("bO%d" % i, [128, 512], F32) for i in range(2)]
        bU = [S.psum("bU%d" % i, [128, 512], F32) for i in range(2)]
        psL = bU[0]
        oacc4 = oacc[:].rearrange("p (hp par) t -> p hp par t", par=2)
        S.dma("pool", up[0][:], A["up_f"], writes=[up[0]])
        S.dma("pool", up[1][:], A["up_b"], writes=[up[1]])
        S.dma("sp", gvec[:], A["gvec"], writes=[gvec])
        S.op("dve", "tensor_scalar", [gvec], [negb], out=negb[:], in0=gvec[:, 0:4], scalar1=-1.0, scalar2=None,
             op0=ALU.mult)
        nb = SEQ // 128
        for b in range(nseq):
            tb = b * SEQ
            za = zT.ap()
            S.dma("sp", qT[:], za[ZR_GQ:ZR_GQ + 256, tb:tb + SEQ].rearrange("(c p) t -> p c t", p=128),
                  reads=[zT], writes=[qT])
            S.dma("sp", kT[:], za[ZR_GK:ZR_GK + 256, tb:tb + SEQ].rearrange("(c p) t -> p c t", p=128),
                  reads=[zT], writes=[kT])
            S.dma("sp", rT[:], za[ZR_GR:ZR_GR + 512, tb:tb + SEQ].rearrange("(c p) t -> p c t", p=128),
                  reads=[zT], writes=[rT])
            S.dma("sp", dn[0][:], za[ZR_DN:ZR_DN + 16, tb:tb + SEQ], reads=[zT], writes=[dn[0]])
            S.dma("sp", dn[1][:], za[ZR_DN + 16:ZR_DN + 32, tb:tb + SEQ], reads=[zT], writes=[dn[1]])
            S.dma("sp", vt[:], zv.ap()[tb:tb + SEQ, 0:512].rearrange("(n p) d -> p n d", p=128),
                  reads=[zv], writes=[vt])
            for d in range(2):
                for hp in range(2):
                    for q4 in range(4):
                        S.op("pe", "matmul", [up[d], dn[d]], [psL], out=psL[:], lhsT=up[d][:, hp * 128:(hp + 1) * 128],
                             rhs=dn[d][:, q4 * 512:(q4 + 1) * 512], start=True, stop=True)
                        S.op("act", "activation", [psL, negb], [E], out=E[:, q4 * 512:(q4 + 1) * 512], in_=psL[:],
                             func=AF.Exp, scale=-1.0, bias=negb[:, 2 * d + hp:2 * d + hp + 1])
                    S.op("act", "activation", [E, C.one_c], [L], out=L[:], in_=E[:], func=AF.Ln, bias=C.one_c[:, 0:1])
                    S.op("dve", "tensor_tensor_scan", [C.rmask, L], [cs], out=cs[:], data0=C.rmask[:], data1=L[:],
                         initial=0.0, op0=ALU.mult, op1=ALU.add)
                    cs3 = v3(cs[:], 64)
                    tot = cs3[:, :, 63:64]
                    S.op("dve", "tensor_tensor", [cs], [t1], out=v3(t1[:], 64),
                         in0=tot.to_broadcast([128, 32, 64]), in1=cs3, op=ALU.subtract)
                    S.op("act", "activation", [cs], [gam], out=gam[:, hp, :], in_=cs3[:, :, 63], func=AF.Exp,
                         scale=GLA_C)
                    if d == 0:
                        bI, tail = cs, t1
                    else:
                        S.op("dve", "tensor_tensor", [t1, L], [t1], out=t1[:], in0=t1[:], in1=L[:], op=ALU.add)
                        S.op("dve", "tensor_tensor", [cs, L], [cs], out=cs[:], in0=cs[:], in1=L[:], op=ALU.subtract)
                        bI, tail = t1, cs
                    S.op("act", "activation", [bI], [E], out=E[:], in_=bI[:], func=AF.Exp, scale=GLA_C)
                    S.op("dve", "scalar_tensor_tensor", [qT, E], [qd], out=qd[:, hp, :], in0=qT[:, hp, :], scalar=0.125,
                         in1=E[:], op0=ALU.mult, op1=ALU.mult)
                    S.op("act", "activation", [bI], [E], out=E[:], in_=bI[:], func=AF.Exp, scale=-GLA_C)
                    S.op("dve", "tensor_tensor", [kT, E], [kd], out=kd[:, hp, :], in0=kT[:, hp, :], in1=E[:], op=ALU.mult)
                    S.op("act", "activation", [tail], [E], out=E[:], in_=tail[:], func=AF.Exp, scale=GLA_C)
                    S.op("dve", "tensor_tensor", [kT, E], [ktT], out=ktT[:, hp, :], in0=kT[:, hp, :], in1=E[:],
                         op=ALU.mult)
                for blk in range(nb):
                    psT = C.psT[C.rr % 2]
                    C.rr += 1
                    for hp in range(2):
                        S.op("pe", "transpose", [ktT, C.ident], [psT], out=psT[:, hp, :],
                             in_=ktT[:, hp, blk * 128:(blk + 1) * 128], identity=C.ident[:])
                    S.op("act", "activation", [psT], [kt], out=v3(kt[:, blk, :]), in_=psT[:, 0:2, :], func=AF.Copy)
                S.op("dve", "memset", [], [Hf], ap=Hf[:], constant=0.0)
                S.op("dve", "memset", [], [Hb], ap=Hb[:], constant=0.0)
                mask = C.mask_f if d == 0 else C.mask_b
                blocks = range(nb) if d == 0 else range(nb - 1, -1, -1)
                for bi, blk in enumerate(blocks):
                    bc = slice(blk * 128, (blk + 1) * 128)
                    bf_ = bi % 2
                    sT = sTs[bf_]
                    for h in range(4):
                        hp, par, base = h // 2, h % 2, (h % 2) * 64
                        pS = v3(bS[par][:, 0:256])
                        S.op("pe", "matmul", [kd, qd], [bS[par]], out=pS[:, hp, :], lhsT=kd[base:base + 64, hp, bc],
                             rhs=qd[base:base + 64, hp, bc], start=True, stop=True)
                    for par in range(2):
                        pS = v3(bS[par][:, 0:256])
                        S.op("dve", "tensor_tensor", [bS[par], mask], [(sT, par)], out=sT[:, par, :, :], in0=pS,
                             in1=mask[:].to_broadcast([128, 2, 128]), op=ALU.mult)
                    for h in range(4):
                        hp, par = h // 2, h % 2
                        pO = v3(bO[par][:, 0:256])
                        S.op("pe", "matmul", [vt, (sT, par)], [bO[par]], out=pO[:, hp, :],
                             lhsT=vt[:, blk, h * 128:(h + 1) * 128], rhs=sT[:, par, hp, :], start=(hp == 0), stop=False,
                             skip_group_check=True)
                    halves = (0, 1) if d == 0 else (1, 0)
                    for half in halves:
                        c = blk * 2 + half
                        cc = slice(c * 64, (c + 1) * 64)
                        pb = slice(half * 64, (half + 1) * 64)
                        pU = v3(bU[half][:, 0:256])
                        for h in range(4):
                            hp, par, base = h // 2, h % 2, (h % 2) * 64
                            pO = v3(bO[par][:, 0:256])
                            S.op("pe", "matmul", [Hb, qd], [bO[par]], out=pO[:, hp, half * 64:(half + 1) * 64],
                                 lhsT=Hb[base:base + 64, hp, :], rhs=qd[base:base + 64, hp, cc],
                                 start=False, stop=True, skip_group_check=True)
                        for h in range(4):
                            hp, base = h // 2, (h % 2) * 64
                            S.op("pe", "matmul", [kt, vt], [bU[half]], out=pU[base:base + 64, hp, :],
                                 lhsT=kt[pb, blk, h * 64:(h + 1) * 64], rhs=vt[pb, blk, h * 128:(h + 1) * 128],
                                 start=True, stop=True)
                        S.op("pool", "tensor_tensor", [Hf, gam], [Ht], out=Ht[:], in0=Hf[:],
                             in1=gam[:, :, c:c + 1].to_broadcast([128, 2, 128]), op=ALU.mult)
                        S.op("dve", "tensor_tensor", [Ht, bU[half]], [Hf], out=Hf[:], in0=Ht[:], in1=pU, op=ALU.add)
                        S.op("act", "activation", [Hf], [Hb], out=Hb[:], in_=Hf[:], func=AF.Copy)
                    for par in range(2):
                        pO = v3(bO[par][:, 0:256])
                        if d == 0:
                            S.op("act", "activation", [bO[par]], [oacc], out=oacc4[:, :, par, bc], in_=pO,
                                 func=AF.Copy)
                        else:
                            S.op("dve", "tensor_tensor", [bO[par], oacc], [oacc], out=oacc4[:, :, par, bc], in0=pO,
                                 in1=oacc4[:, :, par, bc], op=ALU.add)
            k = 0
            for h in range(4):
                for q4 in range(4):
                    c4 = slice(q4 * 512, (q4 + 1) * 512)
                    S.op("act", "activation", [oacc], [sq], out=sq[:], in_=oacc[:, h, c4], func=AF.Square)
                    S.op("pe", "matmul", [C.ones_f, sq], [psL], out=psL[:], lhsT=C.ones_f[:, 0:128], rhs=sq[:],
                         start=True, stop=True)
                    S.op("act", "activation", [psL, C.eps_c], [rs], out=rs[:], in_=psL[:], func=AF.Sqrt,
                         scale=1.0 / 128, bias=C.eps_c[:, 0:1])
                    S.op("dve", "reciprocal", [rs], [rs], out=rs[:], in_=rs[:])
                    S.op("act", "activation", [rT], [sr], out=sr[:], in_=rT[:, h, c4], func=AF.Silu)
                    S.op("dve", "scalar_tensor_tensor", [oacc, gvec, rs], [rs], out=rs[:], in0=oacc[:, h, c4],
                         scalar=gvec[:, 4 + h:5 + h], in1=rs[:], op0=ALU.mult, op1=ALU.mult)
                    ys = yst[k % 2]
                    k += 1
                    S.op("dve", "tensor_tensor", [rs, sr], [ys], out=ys[:], in0=rs[:], in1=sr[:], op=ALU.mult)
                    S.dma("sp", yT.ap()[h * 128:(h + 1) * 128, tb + q4 * 512:tb + (q4 + 1) * 512], ys[:],
                          reads=[ys], writes=[(yT, "g%d_%d_%d" % (b, h, q4))])


def fnet_phase(S, C, zT, yT, A, nseq=2):
    with S.phase():
        Ws = S.sbuf("Ws", [128, 2, 16, SEQ], BF16)
        Wc = S.sbuf("Wc", [128, 256], BF16)
        uT = S.sbuf("uT", [128, 4, SEQ], BF16)
        As = [S.sbuf("As%d" % i, [128, 16, 256], BF16) for i in range(2)]
        yst = [S.sbuf("ystf%d" % i, [128, 512], BF16) for i in range(2)]
        pA = [S.psum("pA%d" % i, [128, 512], F32) for i in range(2)]
        pY = [S.psum("pY%d" % i, [128, 512], F32) for i in range(2)]
        S.dma("sp", Wc[:], A["dftC"], writes=[Wc])
        for t in range(2):
            for hlf in range(2):
                S.dma("sp", Ws[:, t, hlf * 8:(hlf + 1) * 8, :],
                      A["dftS"][t, hlf * 1024:(hlf + 1) * 1024, :].rearrange("(n p) s -> p n s", p=128),
                      writes=[(Ws, "%d_%d" % (t, hlf))])
        k = 0
        for b in range(nseq):
            tb = b * SEQ
            S.dma("sp", uT[:], zT.ap()[ZR_FN:ZR_FN + 512, tb:tb + SEQ].rearrange("(c p) t -> p c t", p=128),
                  reads=[zT], writes=[uT])
            for g in range(4):
                a_s = As[g % 2]
                for n in range(16):
                    p = pA[n % 2]
                    S.op("pe", "matmul", [uT, Wc], [p], out=p[:, 0:256], lhsT=uT[:, g, n * 128:(n + 1) * 128], rhs=Wc[:],
                         start=True, stop=True)
                    if n % 2:
                        S.op("act", "activation", [p], [a_s], out=a_s[:, n, :], in_=p[:, 0:256], func=AF.Copy)
                    else:
                        S.op("dve", "tensor_copy", [p], [a_s], out=a_s[:, n, :], in_=p[:, 0:256])
                for q4 in range(4):
                    p = pY[q4 % 2]
                    i = 0
                    for t in range(2):
                        for n in range(16):
                            S.op("pe", "matmul", [a_s, Ws], [p], out=p[:], lhsT=a_s[:, n, t * 128:(t + 1) * 128],
                                 rhs=Ws[:, t, n, q4 * 512:(q4 + 1) * 512], start=(i == 0), stop=(i == 31))
                            i += 1
                    ys = yst[k % 2]
                    k += 1
                    S.op("act", "activation", [p], [ys], out=ys[:], in_=p[:], func=AF.Copy)
                    S.dma("sp", yT.ap()[512 + g * 128:512 + (g + 1) * 128, tb + q4 * 512:tb + (q4 + 1) * 512], ys[:],
                          reads=[ys], writes=[(yT, "f%d_%d_%d" % (b, g, q4))])


def merge_phase(S, C, xin, xout, yT, gcol_ap, win_ap, pw_aps, wout_ap, ntiles=T // TT):
    with S.phase():
        wgt = S.sbuf("wgt", [128, 8, 3072], BF16)
        pw = [S.sbuf("pw%d" % i, [128, 4, D], BF16) for i in range(3)]
        wo = S.sbuf("wo", [128, 8, D], BF16)
        xt = [S.sbuf("xt%d" % i, [128, TT // 128, D], F32) for i in range(2)]
        ot = [S.sbuf("ot%d" % i, [128, TT // 128, D], F32) for i in range(2)]
        hT = [S.sbuf("hT%d" % i, [128, 8, TT], BF16) for i in range(2)]
        ysb = [S.sbuf("ysb%d" % i, [128, 12, TT], BF16) for i in range(2)]
        sg = S.sbuf("sgm", [128, 24, TT], BF16)
        macc = S.sbuf("macc", [128, 8, TT], F32)
        mtmp = [S.sbuf("mtmp%d" % i, [128, TT], F32) for i in range(2)]
        mT = S.sbuf("mT", [128, 8, TT], BF16)
        pG = [S.psum("pG%d" % i, [128, 512], F32) for i in range(2)]
        pP = [S.psum("pP%d" % i, [128, 512], F32) for i in range(2)]
        pX = [S.psum("pX%d" % i, [128, 512], F32) for i in range(2)]
        load_gain(S, C, 0, gcol_ap)
        for i in range(3):
            S.dma("pool", wgt[:, :, i * 1024:(i + 1) * 1024],
                  win_ap[:, 3840 + i * 1024:3840 + (i + 1) * 1024].rearrange("(c p) n -> p c n", p=128),
                  writes=[(wgt, i)])
            S.dma("pool", pw[i][:], pw_aps[i].rearrange("(c p) n -> p c n", p=128), writes=[pw[i]])
        S.dma("pool", wo[:], wout_ap.rearrange("(c p) n -> p c n", p=128), writes=[wo])
        nsub = TT // 128
        k = 0
        for it in range(ntiles):
            t0 = it * TT
            x_t, h_t, o_t, y_t = xt[it % 2], hT[it % 2], ot[it % 2], ysb[it % 2]
            norm_tile(S, C, xin, t0, 0, x_t, h_t, 0)
            S.dma("sp", y_t[:], yT.ap()[:, t0:t0 + TT].rearrange("(c p) t -> p c t", p=128), reads=[yT], writes=[y_t])
            for gi in range(24):
                p = pG[gi % 2]
                for c in range(8):
                    S.op("pe", "matmul", [wgt, h_t], [p], out=p[:, 0:TT], lhsT=wgt[:, c, gi * 128:(gi + 1) * 128],
                         rhs=h_t[:, c, :], start=(c == 0), stop=(c == 7))
                S.op("act", "activation", [p], [(sg, gi)], out=sg[:, gi, :], in_=p[:, 0:TT], func=AF.Sigmoid)
            for dc in range(8):
                for i in range(3):
                    p = pP[k % 2]
                    k += 1
                    for kc in range(4):
                        S.op("pe", "matmul", [pw[i], y_t], [p], out=p[:, 0:TT], lhsT=pw[i][:, kc, dc * 128:(dc + 1) * 128],
                             rhs=y_t[:, i * 4 + kc, :], start=(kc == 0), stop=(kc == 3))
                    if i == 0:
                        S.op("dve", "tensor_tensor", [p, (sg, dc)], [(macc, dc)], out=macc[:, dc, :], in0=p[:, 0:TT],
                             in1=sg[:, dc, :], op=ALU.mult)
                    else:
                        mt = mtmp[i % 2]
                        S.op("dve", "tensor_tensor", [p, (sg, i * 8 + dc)], [mt], out=mt[:], in0=p[:, 0:TT],
                             in1=sg[:, i * 8 + dc, :], op=ALU.mult)
                        if i == 1:
                            S.op("pool", "tensor_tensor", [(macc, dc), mt], [(macc, dc)], out=macc[:, dc, :],
                                 in0=macc[:, dc, :], in1=mt[:], op=ALU.add)
                        else:
                            S.op("dve", "tensor_tensor", [(macc, dc), mt], [(mT, dc)], out=mT[:, dc, :],
                                 in0=macc[:, dc, :], in1=mt[:], op=ALU.add)
            for s in range(nsub):
                for hf in range(2):
                    p = pX[hf]
                    for dc in range(8):
                        S.op("pe", "matmul", [(mT, dc), wo], [p], out=p[:], lhsT=mT[:, dc, s * 128:(s + 1) * 128],
                             rhs=wo[:, dc, hf * 512:(hf + 1) * 512], start=(dc == 0), stop=(dc == 7))
                    S.op("dve", "tensor_tensor", [p, x_t], [o_t], out=o_t[:, s, hf * 512:(hf + 1) * 512], in0=p[:],
                         in1=x_t[:, s, hf * 512:(hf + 1) * 512], op=ALU.add)
            S.dma("sp", xout.ap()[t0:t0 + TT, :].rearrange("(s p) d -> p s d", p=128), o_t[:],
                  reads=[o_t], writes=[xout])


def final_norm_phase(S, C, xin, out, grow_ap):
    with S.phase():
        gb = S.sbuf("gbf", [128, D], F32)
        xt = [S.sbuf("xt%d" % i, [128, TT // 128, D], F32) for i in range(2)]
        ot = [S.sbuf("ot%d" % i, [128, TT // 128, D], F32) for i in range(2)]
        S.dma("sp", gb[:], grow_ap, writes=[gb])
        for it in range(T // TT):
            t0 = it * TT
            x_t, o_t = xt[it % 2], ot[it % 2]
            S.dma("sp", x_t[:], xin.ap()[t0:t0 + TT, :].rearrange("(s p) d -> p s d", p=128), reads=[xin], writes=[x_t])
            for s in range(TT // 128):
                ss = C.ss[C.rr % 4]
                rstd = C.rstd[C.rr % 4]
                C.rr += 1
                S.op("act", "activation", [x_t], [C.junk, ss], out=C.junk[:], in_=x_t[:, s, :], func=AF.Square,
                     accum_out=ss[:])
                S.op("dve", "tensor_scalar", [ss], [rstd], out=rstd[:], in0=ss[:], scalar1=1.0 / D, scalar2=EPS,
                     op0=ALU.mult, op1=ALU.add)
                S.op("pool", "tensor_tensor", [rstd, C.mhalf], [rstd], out=rstd[:], in0=rstd[:], in1=C.mhalf[:],
                     op=ALU.pow)
                S.op("dve", "scalar_tensor_tensor", [x_t, rstd, gb], [o_t], out=o_t[:, s, :], in0=x_t[:, s, :],
                     scalar=rstd[:, 0:1], in1=gb[:], op0=ALU.mult, op1=ALU.mult)
            S.dma("sp", out.ap()[t0:t0 + TT, :].rearrange("(s p) d -> p s d", p=128), o_t[:], reads=[o_t], writes=[out])


RW_C = -0.6065306597126334
import os as _os
RW_STAGE = int(_os.environ.get("K_RW_STAGE", "9"))
RW_CUT = int(_os.environ.get("K_RW_CUT", "9"))
RV = {"w0_f": 0, "w0_b": 1, "a0_f": 2, "a0_b": 3, "k_k": 4, "k_a": 5, "r_k": 6, "ln_g": 7, "ln_b": 8}


def rwkv_phase(S, C, zT, zv, yT, A, nseq=2):
    HS = SEQ // 2
    NB = SEQ // 128
    with S.phase():
        load_rwkv_consts(S, C)
        w2 = [S.sbuf("w2_%d" % i, [32, 512], BF16) for i in range(2)]
        a2 = [S.sbuf("a2_%d" % i, [32, 512], BF16) for i in range(2)]
        g2 = S.sbuf("g2", [96, 512], BF16)
        rvec = S.sbuf("rvec", [128, 36], F32)
        omka = S.sbuf("omka", [128, 4], F32)
        lnb_eps = S.sbuf("lneps", [128, 1], F32)
        lo = S.sbuf("lo", [96, SEQ], BF16)
        th = [S.sbuf("th%d" % i, [32, SEQ], BF16) for i in range(2)]
        ad = [S.sbuf("ad%d" % i, [32, SEQ], BF16) for i in range(2)]
        sgd = S.sbuf("sgd", [96, SEQ], BF16)
        vt = S.sbuf("vtr", [128, NB, 256], BF16)
        rk = S.sbuf("rk", [128, 2, SEQ], BF16)
        X = [S.sbuf("X%d" % i, [128, HS], F32) for i in range(7)]
        KR = S.sbuf("KR", [128, 2, NB, 2, 128], BF16)
        BT = S.sbuf("BT", [128, 2, SEQ], BF16)
        KT = S.sbuf("KT", [128, 2, SEQ], BF16)
        tlT = [S.sbuf("tlT%d" % i, [128, SEQ], BF16) for i in range(2)]
        tl = [S.sbuf("tl%d" % i, [128, NB, 256], BF16) for i in range(2)]
        gam = S.sbuf("gamr", [128, 2, NB], F32)
        yacc = S.sbuf("yacc", [128, 2, SEQ], F32)
        bonus = S.sbuf("bonus", [128, 2, SEQ], BF16)
        krz = [S.sbuf("krz%d" % i, [128, 4, 256], BF16) for i in range(2)]
        W4 = [S.sbuf("W4_%d" % i, [128, 4, 4, 128], BF16) for i in range(2)]
        Bbm = [S.sbuf("Bbm%d" % i, [128, 4, 128], BF16) for i in range(2)]
        Mt = [S.sbuf("Mt%d" % i, [128, 4, 128], BF16) for i in range(2)]
        Nt = [S.sbuf("Nt%d" % i, [128, 4, 128], BF16) for i in range(2)]
        Sf = S.sbuf("Sf", [128, 4, 128], F32)
        Sb = [S.sbuf("Sb%d" % i, [128, 4, 128], BF16) for i in range(2)]
        Gs = S.sbuf("Gs", [128, 4, 64], BF16)
        Ps = S.sbuf("Ps", [128, 4, 64], BF16)
        Hf = S.sbuf("Hfr", [128, 2, 64], F32)
        Ht = S.sbuf("Htr", [128, 2, 64], F32)
        Hb = S.sbuf("Hbr", [128, 2, 64], BF16)
        yst = [S.sbuf("ystr%d" % i, [128, 512], BF16) for i in range(2)]
        bk = [S.psum("rb%d" % i, [128, 512], F32) for i in range(6)]
        bQ, bR, bG, bY, bH = (bk[0], bk[1]), bk[2], bk[3], bk[4], bk[5]
        for i, nm in enumerate(("w2_f", "w2_b")):
            S.dma("pool", w2[i][:], A[nm], writes=[w2[i]])
        for i, nm in enumerate(("a2_f", "a2_b")):
            S.dma("pool", a2[i][:], A[nm], writes=[a2[i]])
        S.dma("pool", g2[:], A["g2"], writes=[g2])
        S.dma("sp", rvec[:], A["rvec"], writes=[rvec])
        S.op("dve", "tensor_scalar", [rvec], [omka], out=omka[:], in0=rvec[:, RV["k_a"] * 4:RV["k_a"] * 4 + 4],
             scalar1=-1.0, scalar2=1.0, op0=ALU.mult, op1=ALU.add)
        S.op("dve", "memset", [], [lnb_eps], ap=lnb_eps[:], constant=64e-5)

        def rv(name, hp):
            c = RV[name] * 4 + hp
            return rvec[:, c:c + 1]

        for b in range(nseq):
            if RW_CUT < 1:
                continue
            tb = b * SEQ
            za = zT.ap()
            for i in range(2):
                S.dma("sp", lo[0:32, :], za[ZR_LORA + 32 * i:ZR_LORA + 32 * i + 32, tb:tb + SEQ], reads=[zT], writes=[lo])
                S.op("act", "activation", [lo], [th[i]], out=th[i][:], in_=lo[0:32, :], func=AF.Tanh)
            for i in range(2):
                S.dma("sp", ad[i][:], za[ZR_LORA + 64 + 32 * i:ZR_LORA + 96 + 32 * i, tb:tb + SEQ], reads=[zT],
                      writes=[ad[i]])
            S.dma("sp", lo[:], za[ZR_GD:ZR_GD + 96, tb:tb + SEQ], reads=[zT], writes=[lo])
            S.op("act", "activation", [lo], [sgd], out=sgd[:], in_=lo[:], func=AF.Sigmoid)
            for grp in range(2):
                if RW_CUT < 2:
                    continue
                S.dma("sp", vt[:], zv.ap()[tb:tb + SEQ, 512 + grp * 256:512 + (grp + 1) * 256]
                      .rearrange("(n p) d -> p n d", p=128), reads=[zv], writes=[vt])
                for d in range(2):
                    for hl in range(2):
                        hp = grp * 2 + hl
                        if d == 0:
                            S.dma("sp", rk[:, 0, :], za[ZR_RR + hp * 128:ZR_RR + (hp + 1) * 128, tb:tb + SEQ], reads=[zT],
                                  writes=[(rk, 0)])
                            S.dma("sp", rk[:, 1, :], za[ZR_RK + hp * 128:ZR_RK + (hp + 1) * 128, tb:tb + SEQ], reads=[zT],
                                  writes=[(rk, 1)])
                        elif hl == 0 or True:
                            S.dma("sp", rk[:, 0, :], za[ZR_RR + hp * 128:ZR_RR + (hp + 1) * 128, tb:tb + SEQ], reads=[zT],
                                  writes=[(rk, 0)])
                            S.dma("sp", rk[:, 1, :], za[ZR_RK + hp * 128:ZR_RK + (hp + 1) * 128, tb:tb + SEQ], reads=[zT],
                                  writes=[(rk, 1)])
                        for hs in range(2):
                            c0 = hs * HS
                            hc = slice(c0, c0 + HS)
                            nbh = HS // 128
                            bsl = slice(hs * nbh, (hs + 1) * nbh)
                            SG, CS, A3, A2, KP, BE, SB = X
                            for q in range(HS // 512):
                                qc = slice(c0 + q * 512, c0 + (q + 1) * 512)
                                ql = slice(q * 512, (q + 1) * 512)
                                S.op("pe", "matmul", [w2[d], th[d]], [bG], out=bG[:], lhsT=w2[d][:, hp * 128:(hp + 1) * 128],
                                     rhs=th[d][:, qc], start=True, stop=True)
                                S.op("act", "activation", [bG, rvec], [SG], out=SG[:, ql], in_=bG[:], func=AF.Sigmoid,
                                     bias=rv("w0_f" if d == 0 else "w0_b", hp))
                            S.op("dve", "tensor_tensor_scan", [C.rmask128, SG], [CS], out=CS[:], data0=C.rmask128[:, 0:HS],
                                 data1=SG[:], initial=0.0, op0=ALU.mult, op1=ALU.add)
                            cs3 = v3(CS[:], 128)
                            S.op("dve", "tensor_tensor", [CS], [A3], out=v3(A3[:], 128),
                                 in0=cs3[:, :, 127:128].to_broadcast([128, nbh, 128]), in1=cs3, op=ALU.subtract)
                            S.op("act", "activation", [CS], [gam], out=gam[:, hl, bsl], in_=cs3[:, :, 127], func=AF.Exp,
                                 scale=RW_C)
                            S.op("dve", "tensor_tensor", [CS, SG], [A2], out=A2[:], in0=CS[:], in1=SG[:], op=ALU.subtract)
                            if d == 0:
                                bI, bE, tail = CS, A2, A3
                            else:
                                S.op("dve", "tensor_tensor", [A3, SG], [CS], out=CS[:], in0=A3[:], in1=SG[:], op=ALU.add)
                                bI, bE, tail = CS, A3, A2
                            if RW_CUT < 3:
                                continue
                            AV = SG
                            for q in range(HS // 512):
                                qc = slice(c0 + q * 512, c0 + (q + 1) * 512)
                                ql = slice(q * 512, (q + 1) * 512)
                                S.op("pe", "matmul", [a2[d], ad[d]], [bG], out=bG[:], lhsT=a2[d][:, hp * 128:(hp + 1) * 128],
                                     rhs=ad[d][:, qc], start=True, stop=True)
                                S.op("act", "activation", [bG, rvec], [AV], out=AV[:, ql], in_=bG[:], func=AF.Sigmoid,
                                     bias=rv("a0_f" if d == 0 else "a0_b", hp))
                            S.op("act", "activation", [(rk, 1), rvec], [BE], out=BE[:], in_=rk[:, 1, hc], func=AF.Square,
                                 scale=rv("k_k", hp))
                            for q in range(HS // 512):
                                ql = slice(q * 512, (q + 1) * 512)
                                S.op("pe", "matmul", [C.onesblk, BE], [bG], out=bG[:], lhsT=C.onesblk[:], rhs=BE[:, ql],
                                     start=True, stop=True)
                                S.op("act", "activation", [bG], [SB], out=SB[:, ql], in_=bG[:], func=AF.Sqrt)
                            S.op("dve", "tensor_scalar", [SB], [SB], out=SB[:], in0=SB[:], scalar1=1e-6, scalar2=None,
                                 op0=ALU.max)
                            S.op("dve", "reciprocal", [SB], [SB], out=SB[:], in_=SB[:])
                            S.op("dve", "scalar_tensor_tensor", [(rk, 1), rvec, SB], [KP], out=KP[:], in0=rk[:, 1, hc],
                                 scalar=rv("k_k", hp), in1=SB[:], op0=ALU.mult, op1=ALU.mult)
                            if RW_CUT < 4:
                                continue
                            S.op("dve", "tensor_tensor", [KP, AV], [BE], out=BE[:], in0=KP[:], in1=AV[:], op=ALU.mult)
                            S.op("dve", "tensor_scalar", [AV, rvec, omka], [AV], out=AV[:], in0=AV[:], scalar1=rv("k_a", hp),
                                 scalar2=omka[:, hp:hp + 1], op0=ALU.mult, op1=ALU.add)
                            S.op("dve", "tensor_tensor", [(rk, 1), AV], [AV], out=AV[:], in0=rk[:, 1, hc], in1=AV[:],
                                 op=ALU.mult)
                            KD = AV
                            if d == 0:
                                S.op("dve", "scalar_tensor_tensor", [(rk, 0), rvec, KD], [SB], out=SB[:], in0=rk[:, 0, hc],
                                     scalar=rv("r_k", hp), in1=KD[:], op0=ALU.mult, op1=ALU.mult)
                                for q in range(HS // 512):
                                    ql = slice(q * 512, (q + 1) * 512)
                                    S.op("pe", "matmul", [C.onesblk, SB], [bG], out=bG[:], lhsT=C.onesblk[:], rhs=SB[:, ql],
                                         start=True, stop=True)
                                    psT = C.psT[C.rr % 2]
                                    C.rr += 1
                                    for j in range(4):
                                        blk = (c0 + q * 512) // 128 + j
                                        S.op("pe", "transpose", [vt, C.ident], [psT], out=psT[:, j, :],
                                             in_=vt[:, blk, hl * 128:(hl + 1) * 128], identity=C.ident[:])
                                    S.op("act", "activation", [bG], [SB], out=SB[:, ql], in_=bG[:], func=AF.Copy)
                                    S.op("dve", "tensor_tensor", [psT, SB], [(bonus, hl)],
                                         out=v3(bonus[:, hl, c0 + q * 512:c0 + (q + 1) * 512]), in0=psT[:, 0:4, :],
                                         in1=v3(SB[:, ql]), op=ALU.mult)
                            if RW_CUT < 5:
                                continue
                            S.op("act", "activation", [bE], [bE], out=bE[:], in_=bE[:], func=AF.Exp, scale=RW_C)
                            S.op("act", "activation", [bI], [bI], out=bI[:], in_=bI[:], func=AF.Exp, scale=RW_C)
                            S.op("act", "activation", [tail], [tail], out=tail[:], in_=tail[:], func=AF.Exp, scale=RW_C)
                            S.op("dve", "tensor_tensor", [KP, bE], [(KR, hl)], out=KR[:, hl, bsl, 0, :], in0=v3(KP[:]),
                                 in1=v3(bE[:]), op=ALU.mult)
                            rdec = bI if d == 0 else bE
                            S.op("dve", "tensor_tensor", [(rk, 0), rdec], [(KR, hl)], out=KR[:, hl, bsl, 1, :],
                                 in0=v3(rk[:, 0, hc]), in1=v3(rdec[:]), op=ALU.mult)
                            S.op("dve", "tensor_tensor", [BE, tail], [tlT[0]], out=tlT[0][:, hc], in0=BE[:], in1=tail[:],
                                 op=ALU.mult)
                            S.op("dve", "tensor_tensor", [KD, tail], [tlT[1]], out=tlT[1][:, hc], in0=KD[:], in1=tail[:],
                                 op=ALU.mult)
                            S.op("dve", "reciprocal", [bI], [bI], out=bI[:], in_=bI[:])
                            S.op("dve", "tensor_tensor", [BE, bI], [(BT, hl)], out=BT[:, hl, hc], in0=BE[:], in1=bI[:],
                                 op=ALU.mult)
                            S.op("dve", "tensor_tensor", [KD, bI], [(KT, hl)], out=KT[:, hl, hc], in0=KD[:], in1=bI[:],
                                 op=ALU.mult)
                        if RW_CUT < 6:
                            continue
                        for blk in range(NB):
                            psT = C.psT[C.rr % 2]
                            C.rr += 1
                            for i in range(2):
                                S.op("pe", "transpose", [tlT[i], C.ident], [psT], out=psT[:, i, :],
                                     in_=tlT[i][:, blk * 128:(blk + 1) * 128], identity=C.ident[:])
                            for i in range(2):
                                if blk % 2:
                                    S.op("act", "activation", [psT], [(tl[i], hl)], out=tl[i][:, blk, hl * 128:(hl + 1) * 128],
                                         in_=psT[:, i, :], func=AF.Copy)
                                else:
                                    S.op("dve", "tensor_copy", [psT], [(tl[i], hl)], out=tl[i][:, blk, hl * 128:(hl + 1) * 128],
                                         in_=psT[:, i, :])
                    if RW_STAGE < 2:
                        continue
                    S.op("dve", "memset", [], [Hf], ap=Hf[:], constant=0.0)
                    S.op("dve", "memset", [], [Hb], ap=Hb[:], constant=0.0)
                    m4 = C.m4_f if d == 0 else C.m4_b
                    mB = C.mB_f if d == 0 else C.mB_b
                    blocks = range(NB) if d == 0 else range(NB - 1, -1, -1)
                    for bi, blk in enumerate(blocks):
                        bc = slice(blk * 128, (blk + 1) * 128)
                        kz, w4, bbm, sb = krz[bi % 2], W4[bi % 2], Bbm[bi % 2], Sb[bi % 2]
                        for h4 in range(4):
                            hl, par = h4 // 2, h4 % 2
                            q = bQ[h4 % 2]
                            if h4 % 2:
                                S.op("act", "activation", [(KR, hl), C.pmask], [(kz, h4)], out=kz[:, h4, :],
                                     in_=KR[:, hl, blk, :, :].rearrange("p a b -> p (a b)"), func=AF.Copy,
                                     scale=C.pmask[:, par:par + 1])
                            else:
                                S.op("pool", "tensor_scalar", [(KR, hl), C.pmask], [(kz, h4)], out=kz[:, h4, :],
                                     in0=KR[:, hl, blk, :, :].rearrange("p a b -> p (a b)"), scalar1=C.pmask[:, par:par + 1],
                                     scalar2=None, op0=ALU.mult)
                            S.op("pe", "matmul", [(BT, hl), (kz, h4)], [q], out=q[:, 0:128], lhsT=BT[:, hl, bc],
                                 rhs=kz[:, h4, 0:128], start=True, stop=True)
                            S.op("pe", "matmul", [(BT, hl), (kz, h4)], [q], out=q[:, 128:256], lhsT=kz[:, h4, 0:128],
                                 rhs=BT[:, hl, bc], start=False, stop=True, skip_group_check=True)
                            S.op("pe", "matmul", [(KT, hl), (kz, h4)], [q], out=q[:, 256:512], lhsT=KT[:, hl, bc],
                                 rhs=kz[:, h4, :], start=False, stop=True, skip_group_check=True)
                            S.op("pe", "matmul", [(BT, hl), (kz, h4)], [bR], out=bR[:, h4 * 128:(h4 + 1) * 128],
                                 lhsT=BT[:, hl, bc], rhs=kz[:, h4, 128:256], start=(h4 == 0), stop=True, skip_group_check=True)
                            S.op("dve", "tensor_tensor", [q, m4], [(w4, h4)], out=w4[:, h4, :, :], in0=v3(q[:]), in1=m4[:],
                                 op=ALU.mult)
                        S.op("dve", "tensor_tensor", [bR, mB], [bbm], out=bbm[:], in0=v3(bR[:]),
                             in1=mB[:].to_broadcast([128, 4, 128]), op=ALU.mult)
                        S.op("dve", "tensor_tensor", [w4, C.ident_f], [Sf], out=Sf[:], in0=w4[:, :, 0, :],
                             in1=C.ident_f[:].to_broadcast([128, 4, 128]), op=ALU.add)
                        S.op("act", "activation", [Sf], [sb], out=sb[:], in_=Sf[:], func=AF.Copy)
                        Mp, Np = w4[:, :, 0, :], w4[:, :, 1, :]
                        Mr, Nr = [w4], [w4]
                        for lv in range(1, 7):
                            Mn, Nn = Mt[lv % 2], Nt[lv % 2]
                            for h4 in range(4):
                                if lv < 6:
                                    S.op("pe", "matmul", Mr + Nr, [bQ[0]], out=bQ[0][:, h4 * 128:(h4 + 1) * 128],
                                         lhsT=Np[:, h4, :], rhs=Mp[:, h4, :], start=(h4 == 0), stop=True,
                                         skip_group_check=True)
                                S.op("pe", "matmul", Mr + Nr, [bQ[1]], out=bQ[1][:, h4 * 128:(h4 + 1) * 128],
                                     lhsT=Mp[:, h4, :], rhs=Np[:, h4, :], start=(h4 == 0), stop=True, skip_group_check=True)
                            if lv < 6:
                                S.op("act", "activation", [bQ[0]], [Mn], out=Mn[:], in_=v3(bQ[0][:]), func=AF.Copy)
                            S.op("dve", "tensor_copy", [bQ[1]], [Nn], out=Nn[:], in_=v3(bQ[1][:]))
                            for h4 in range(4):
                                S.op("pe", "matmul", [Nn, sb], [bR], out=bR[:, h4 * 128:(h4 + 1) * 128], lhsT=Nn[:, h4, :],
                                     rhs=sb[:, h4, :], start=(h4 == 0), stop=True, skip_group_check=True)
                            S.op("dve", "tensor_tensor", [Sf, bR], [Sf], out=Sf[:], in0=Sf[:], in1=v3(bR[:]), op=ALU.add)
                            S.op("act", "activation", [Sf], [sb], out=sb[:], in_=Sf[:], func=AF.Copy)
                            Mp, Np = Mn[:], Nn[:]
                            Mr, Nr = [Mn], [Nn]
                        gG = bG[:, 0:256].rearrange("p (a b) -> p a b", b=64)
                        gP = bG[:, 256:512].rearrange("p (a b) -> p a b", b=64)
                        for h4 in range(4):
                            hl = h4 // 2
                            S.op("pe", "matmul", [(kz, h4), Hb], [bG], out=gG[:, h4, :], lhsT=kz[:, h4, 0:128], rhs=Hb[:, hl, :],
                                 start=(h4 == 0), stop=False, skip_group_check=True)
                            S.op("pe", "matmul", [(w4, h4), vt], [bG], out=gG[:, h4, :], lhsT=w4[:, h4, 2, :],
                                 rhs=vt[:, blk, h4 * 64:(h4 + 1) * 64], start=False, stop=True, skip_group_check=True)
                        S.op("act", "activation", [bG], [Gs], out=Gs[:], in_=gG, func=AF.Copy)
                        for h4 in range(4):
                            S.op("pe", "matmul", [sb, Gs], [bG], out=gP[:, h4, :], lhsT=sb[:, h4, :], rhs=Gs[:, h4, :],
                                 start=False, stop=True, skip_group_check=True)
                        S.op("act", "activation", [bG], [Ps], out=Ps[:], in_=gP, func=AF.Copy, scale=-1.0)
                        yv = bY[:, 0:256].rearrange("p (a b) -> p a b", b=128)
                        hv = bH[:, 0:128].rearrange("p (a b) -> p a b", b=64)
                        for h4 in range(4):
                            hl, par = h4 // 2, h4 % 2
                            po = slice(par * 64, (par + 1) * 64)
                            vh = vt[:, blk, h4 * 64:(h4 + 1) * 64]
                            S.op("pe", "matmul", [Hb, (kz, h4)], [bY], out=yv[po, hl, :], lhsT=Hb[:, hl, :],
                                 rhs=kz[:, h4, 128:256], start=(h4 < 2), stop=False, skip_group_check=True)
                            S.op("pe", "matmul", [vt, (w4, h4)], [bY], out=yv[po, hl, :], lhsT=vh, rhs=w4[:, h4, 3, :],
                                 start=False, stop=False, skip_group_check=True)
                            S.op("pe", "matmul", [Ps, bbm], [bY], out=yv[po, hl, :], lhsT=Ps[:, h4, :], rhs=bbm[:, h4, :],
                                 start=False, stop=True, skip_group_check=True)
                        for h4 in range(4):
                            hl, par = h4 // 2, h4 % 2
                            po = slice(par * 64, (par + 1) * 64)
                            vh = vt[:, blk, h4 * 64:(h4 + 1) * 64]
                            S.op("pe", "matmul", [(tl[1], hl), vt], [bH], out=hv[po, hl, :],
                                 lhsT=tl[1][:, blk, h4 * 64:(h4 + 1) * 64], rhs=vh, start=(h4 < 2), stop=False,
                                 skip_group_check=True)
                            S.op("pe", "matmul", [(tl[0], hl), Ps], [bH], out=hv[po, hl, :],
                                 lhsT=tl[0][:, blk, h4 * 64:(h4 + 1) * 64], rhs=Ps[:, h4, :], start=False, stop=True,
                                 skip_group_check=True)
                        S.op("pool", "tensor_tensor", [Hf, gam], [Ht], out=Ht[:], in0=Hf[:],
                             in1=gam[:, :, blk:blk + 1].to_broadcast([128, 2, 64]), op=ALU.mult)
                        S.op("dve", "tensor_tensor", [Ht, bH], [Hf], out=Hf[:], in0=Ht[:], in1=hv, op=ALU.add)
                        S.op("act", "activation", [Hf], [Hb], out=Hb[:], in_=Hf[:], func=AF.Copy)
                        if d == 0:
                            S.op("act", "activation", [bY], [yacc], out=yacc[:, :, bc], in_=yv, func=AF.Copy)
                        else:
                            S.op("dve", "tensor_tensor", [bY, yacc], [yacc], out=yacc[:, :, bc], in0=yv, in1=yacc[:, :, bc],
                                 op=ALU.add)
                if RW_STAGE < 3:
                    continue
                k = 0
                for hl in range(2):
                    hp = grp * 2 + hl
                    YC, SQ, RS = X[0], X[1], X[2]
                    for q in range(4):
                        qc = slice(q * 512, (q + 1) * 512)
                        S.op("pe", "matmul", [C.onesblk, yacc], [bG], out=bG[:], lhsT=C.onesblk[:], rhs=yacc[:, hl, qc],
                             start=True, stop=True)
                        S.op("dve", "scalar_tensor_tensor", [bG, yacc], [YC], out=YC[:, 0:512], in0=bG[:], scalar=-1.0 / 64,
                             in1=yacc[:, hl, qc], op0=ALU.mult, op1=ALU.add)
                        S.op("act", "activation", [YC], [SQ], out=SQ[:, 0:512], in_=YC[:, 0:512], func=AF.Square)
                        S.op("pe", "matmul", [C.onesblk, SQ], [bY], out=bY[:], lhsT=C.onesblk[:], rhs=SQ[:, 0:512],
                             start=True, stop=True)
                        S.op("act", "activation", [bY, lnb_eps], [RS], out=RS[:, 0:512], in_=bY[:], func=AF.Sqrt,
                             scale=1.0 / 64, bias=lnb_eps[:, 0:1])
                        S.op("dve", "reciprocal", [RS], [RS], out=RS[:, 0:512], in_=RS[:, 0:512])
                        S.op("dve", "tensor_tensor", [YC, RS], [YC], out=YC[:, 0:512], in0=YC[:, 0:512], in1=RS[:, 0:512],
                             op=ALU.mult)
                        S.op("dve", "tensor_scalar", [YC, rvec], [YC], out=YC[:, 0:512], in0=YC[:, 0:512],
                             scalar1=rv("ln_g", hp), scalar2=rv("ln_b", hp), op0=ALU.mult, op1=ALU.add)
                        S.op("pool", "tensor_tensor", [YC, (bonus, hl)], [YC], out=YC[:, 0:512], in0=YC[:, 0:512],
                             in1=bonus[:, hl, qc], op=ALU.add)
                        S.op("pe", "matmul", [g2, sgd], [bH], out=bH[:], lhsT=g2[:, hp * 128:(hp + 1) * 128], rhs=sgd[:, qc],
                             start=True, stop=True)
                        ys = yst[k % 2]
                        k += 1
                        S.op("dve", "tensor_tensor", [YC, bH], [ys], out=ys[:], in0=YC[:, 0:512], in1=bH[:], op=ALU.mult)
                        S.dma("sp", yT.ap()[1024 + hp * 128:1024 + (hp + 1) * 128, tb + q * 512:tb + (q + 1) * 512], ys[:],
                              reads=[ys], writes=[(yT, "r%d_%d_%d" % (b, hp, q))])


NCORES = 8
DEPTH = 2
_bf = ml_dtypes.bfloat16

CONST_SPECS = (("ident_bf", [128, 128], BF16), ("rmask", [128, 2048], F32), ("mask_f", [128, 128], F32),
               ("mask_b", [128, 128], F32), ("rmask128", [128, 1024], F32), ("onesblk", [128, 128], F32),
               ("pmask", [128, 2], F32), ("ident_f", [128, 128], F32), ("m4_f", [128, 4, 128], F32),
               ("m4_b", [128, 4, 128], F32), ("mB_f", [128, 128], F32), ("mB_b", [128, 128], F32),
               ("dftS", [2, 2048, 2048], BF16), ("dftC", [128, 256], BF16))
W_SPECS = (("ffn1_gate", [DEPTH, D, DFF]), ("ffn1_up", [DEPTH, D, DFF]), ("ffn1_down", [DEPTH, DFF, D]),
           ("w_in", [DEPTH, D, 6912]), ("gla_up_f", [DEPTH, 16, 256]), ("gla_up_b", [DEPTH, 16, 256]),
           ("rwkv_w2_f", [DEPTH, 32, 512]), ("rwkv_w2_b", [DEPTH, 32, 512]), ("rwkv_a2_f", [DEPTH, 32, 512]),
           ("rwkv_a2_b", [DEPTH, 32, 512]), ("rwkv_g2", [DEPTH, 96, 512]), ("proj_gla", [DEPTH, 512, D]),
           ("proj_fnet", [DEPTH, 512, D]), ("proj_rwkv", [DEPTH, 512, D]), ("w_out", [DEPTH, D, D]),
           ("ffn2_gate", [DEPTH, D, DFF]), ("ffn2_up", [DEPTH, D, DFF]), ("ffn2_down", [DEPTH, DFF, D]),
           ("gcols", [DEPTH, 3, 128, 8]), ("mub", [DEPTH, 128, 1792]), ("gvec", [DEPTH, 128, 8]),
           ("rvec", [DEPTH, 128, 36]), ("grow", [128, D]))


def host_constants():
    ii = np.arange(128)
    same = (ii[:, None] // 64) == (ii[None, :] // 64)
    LT = (ii[:, None] < ii[None, :]).astype(np.float32)
    GT = (ii[:, None] > ii[None, :]).astype(np.float32)
    LE = (ii[:, None] <= ii[None, :]).astype(np.float32)
    rm = np.ones((128, 2048), np.float32)
    rm[:, ::64] = 0
    rm128 = np.ones((128, 1024), np.float32)
    rm128[:, ::128] = 0
    ob = np.zeros((128, 128), np.float32)
    ob[:64, :64] = 1
    ob[64:, 64:] = 1
    pm = np.zeros((128, 2), np.float32)
    pm[:64, 0] = 1
    pm[64:, 1] = 1
    s = np.arange(2048)
    ang = 2 * np.pi * ((np.outer(s, s) % 2048).astype(np.float64)) / 2048
    c = np.arange(128)
    angc = 2 * np.pi * ((np.outer(c, c) % 128).astype(np.float64)) / 128
    sc = 1.0 / np.sqrt(2048 * 128)
    return {
        "ident_bf": np.eye(128, dtype=_bf), "rmask": rm,
        "mask_f": (same & (ii[:, None] <= ii[None, :])).astype(np.float32),
        "mask_b": (same & (ii[:, None] > ii[None, :])).astype(np.float32),
        "rmask128": rm128, "onesblk": ob, "pmask": pm, "ident_f": np.eye(128, dtype=np.float32),
        "m4_f": np.ascontiguousarray(np.stack([-LT, -GT, LT, LE], axis=1)),
        "m4_b": np.ascontiguousarray(np.stack([-GT, -LT, GT, GT], axis=1)),
        "mB_f": LE, "mB_b": GT,
        "dftS": np.stack([np.cos(ang), -np.sin(ang)]).astype(_bf),
        "dftC": (np.concatenate([np.cos(angc), np.sin(angc)], axis=1) * sc).astype(_bf),
    }


def host_layouts(inp):
    L = DEPTH
    f32 = np.float32
    gcols = np.zeros((L, 3, 128, 8), f32)
    mub = np.zeros((L, 128, 1792), f32)
    gvec = np.zeros((L, 128, 8), f32)
    rvec = np.zeros((L, 128, 36), f32)
    for l in range(L):
        for i, nm in enumerate(("ffn1_norm", "mix_norm", "ffn2_norm")):
            gcols[l, i] = np.asarray(inp[nm][l], f32).reshape(8, 128).T
        mu = np.asarray(inp["rwkv_mu"][l], f32)
        muz = np.zeros(1792, f32)
        muz[0:512] = mu[0:512]
        muz[512:1024] = mu[512:1024]
        muz[1024:1152] = mu[1536:1664]
        muz[1152:1248] = mu[1664:1760]
        muz[1280:1792] = mu[1024:1536]
        mub[l] = np.broadcast_to(muz, (128, 1792))
        gvec[l, :, 0:2] = np.asarray(inp["gla_bias_f"][l], f32).reshape(2, 128).T
        gvec[l, :, 2:4] = np.asarray(inp["gla_bias_b"][l], f32).reshape(2, 128).T
        gvec[l, :, 4:8] = np.asarray(inp["gla_norm"][l], f32).reshape(4, 128).T
        for i, nm in enumerate(("rwkv_w0_f", "rwkv_w0_b", "rwkv_a0_f", "rwkv_a0_b", "rwkv_k_k", "rwkv_k_a", "rwkv_r_k",
                                "rwkv_ln_g", "rwkv_ln_b")):
            rvec[l, :, i * 4:(i + 1) * 4] = np.asarray(inp[nm][l], f32).reshape(4, 128).T
    grow = np.ascontiguousarray(np.broadcast_to(np.asarray(inp["final_norm"], f32), (128, D)))
    return {"gcols": gcols, "mub": mub, "gvec": gvec, "rvec": rvec, "grow": grow}


def build_program():
    nc = bass.Bass("TRN2", target_bir_lowering=False)
    x = nc.dram_tensor("x", [T, D], F32, kind="ExternalInput")
    out = nc.dram_tensor("out", [T, D], F32, kind="ExternalOutput")
    cn = {}
    for name, shp, dt in CONST_SPECS:
        cn[name] = nc.dram_tensor(name, shp, dt, kind="ExternalInput").ap()
    W = {}
    for name, shp in W_SPECS:
        W[name] = nc.dram_tensor(name, shp, F32, kind="ExternalInput").ap()
    xs = [Buf(nc.dram_tensor("xres%d" % i, [T, D], F32), "xres%d" % i) for i in range(3)]
    import os
    dk = {"kind": "ExternalOutput"} if os.environ.get("K_DUMP") else {}
    zT = Buf(nc.dram_tensor("zT", [ZROWS, T], BF16, **dk), "zT")
    zv = Buf(nc.dram_tensor("zv", [T, 1024], BF16, **dk), "zv")
    yT = Buf(nc.dram_tensor("yT", [1536, T], BF16, **dk), "yT")
    xin = Buf(x, "x")
    outb = Buf(out, "out")
    with contextlib.ExitStack() as st:
        S = Sched(nc, st)
        C = Ctx()
        setup_consts(S, C, cn)
        cur = xin
        import os
        NPH = int(os.environ.get("K_NPH", "99"))
        for l in range(DEPTH):
            if l * 7 + 0 >= NPH:
                break
            ffn_phase(S, C, cur, xs[0], W["gcols"][l, 0], W["ffn1_gate"][l], W["ffn1_up"][l], W["ffn1_down"][l])
            if l * 7 + 1 >= NPH:
                break
            mixin_phase(S, C, xs[0], zT, zv, W["gcols"][l, 1], W["w_in"][l], W["mub"][l])
            if l * 7 + 2 >= NPH:
                break
            gla_phase(S, C, zT, zv, yT, {"up_f": W["gla_up_f"][l], "up_b": W["gla_up_b"][l], "gvec": W["gvec"][l]})
            if l * 7 + 3 >= NPH:
                break
            fnet_phase(S, C, zT, yT, cn)
            if l * 7 + 4 >= NPH:
                break
            if os.environ.get("K_SKIP_R0") and l == 0:
                pass
            else:
              rwkv_phase(S, C, zT, zv, yT, {"w2_f": W["rwkv_w2_f"][l], "w2_b": W["rwkv_w2_b"][l],
                                          "a2_f": W["rwkv_a2_f"][l], "a2_b": W["rwkv_a2_b"][l],
                                          "g2": W["rwkv_g2"][l], "rvec": W["rvec"][l]})
            if l * 7 + 5 >= NPH:
                break
            merge_phase(S, C, xs[0], xs[1], yT, W["gcols"][l, 1], W["w_in"][l],
                        [W["proj_gla"][l], W["proj_fnet"][l], W["proj_rwkv"][l]], W["w_out"][l])
            if l * 7 + 6 >= NPH:
                break
            ffn_phase(S, C, xs[1], xs[2], W["gcols"][l, 2], W["ffn2_gate"][l], W["ffn2_up"][l], W["ffn2_down"][l])
            cur = xs[2]
        final_norm_phase(S, C, cur, outb, W["grow"])
        S.emit()
    return nc


_CACHE = {}


def kernel(**inputs):
    inp = {k: np.asarray(v) for k, v in inputs.items()}
    if "nc" not in _CACHE:
        _CACHE["nc"] = build_program()
        _CACHE["consts"] = host_constants()
    nc = _CACHE["nc"]
    shared = dict(_CACHE["consts"])
    shared.update(host_layouts(inp))
    for name, _ in W_SPECS:
        if name not in shared:
            shared[name] = np.ascontiguousarray(inp[name], dtype=np.float32)
    x = np.ascontiguousarray(inp["x"], dtype=np.float32).reshape(NCORES, T, D)
    in_maps = []
    for c in range(NCORES):
        m = dict(shared)
        m["x"] = x[c]
        in_maps.append(m)
    res = run_bass_kernel_spmd(nc, in_maps, core_ids=list(range(NCORES)))
    out = np.stack([np.asarray(r["out"]) for r in res.results], axis=0)
    return out.reshape(16, 2048, D).astype(np.float32)
```

```python
import contextlib
import numpy as np
import ml_dtypes
import concourse.bass as bass
import concourse.mybir as mybir
from concourse.bass_utils import run_bass_kernel_spmd

F32 = mybir.dt.float32
BF16 = mybir.dt.bfloat16
AF = mybir.ActivationFunctionType
ALU = mybir.AluOpType
AX = mybir.AxisListType

ENGS = ("pe", "act", "dve", "pool", "sp")
N_DMA_SEMS = 24


class Buf:
    def __init__(self, t, name):
        self.t = t
        self.name = name
        self.st = {}
        self.is_psum = False

    def _parts(self, part):
        if part is None:
            return list(self.st.keys()) or [None]
        ks = [part]
        if None in self.st:
            ks.append(None)
        return ks

    def deps_for_read(self, part):
        out = set()
        for k in self._parts(part):
            s = self.st.get(k)
            if s and s[0] is not None:
                out.add(s[0])
            if s and self.is_psum:
                out.update(s[1])
        return out

    def deps_for_write(self, part):
        out = set()
        for k in self._parts(part):
            s = self.st.get(k)
            if s:
                if s[0] is not None:
                    out.add(s[0])
                out.update(s[1])
        return out

    def note_read(self, part, ev):
        s = self.st.setdefault(part, [None, []])
        s[1].append(ev)
        if len(s[1]) > 12:
            best = {}
            for (sm, v) in s[1]:
                if best.get(sm, -1) < v:
                    best[sm] = v
            s[1] = [(sm, v) for sm, v in best.items()]

    def note_write(self, part, ev):
        if part is None:
            self.st = {None: [ev, []]}
        else:
            self.st[part] = [ev, []]

    def __getitem__(self, idx):
        return self.t[idx]

    def ap(self):
        return self.t.ap() if hasattr(self.t, "ap") and callable(self.t.ap) else self.t


class Sched:
    def __init__(self, nc, stack):
        self.nc = nc
        self.stack = stack
        self.eng = {"pe": nc.tensor, "act": nc.scalar, "dve": nc.vector,
                    "pool": nc.gpsimd, "sp": nc.sync}
        self.prog = {e: [] for e in ENGS}
        self.cnt = {e: 0 for e in ENGS}
        self.sems = {}
        for e in ENGS:
            self.sems["E" + e] = stack.enter_context(nc.semaphore("sem_" + e))
        self.dma_sems = []
        for i in range(N_DMA_SEMS):
            k = "D%d" % i
            self.sems[k] = stack.enter_context(nc.semaphore("sem_d%d" % i))
            self.dma_sems.append([k, 0])
        self.dma_rrs = {}
        self.waited = {e: {} for e in ENGS}
        self.nwaits = 0
        self.out_events = []

    def _nm(self, name):
        self.uid = getattr(self, "uid", 0) + 1
        return "%s_u%d" % (name, self.uid)

    def sbuf(self, name, shape, dt):
        name = self._nm(name)
        t = self.stack.enter_context(self.nc.sbuf_tensor(name, list(shape), dt))
        return Buf(t, name)

    def psum(self, name, shape, dt=F32):
        name = self._nm(name)
        t = self.stack.enter_context(self.nc.psum_tensor(name, list(shape), dt))
        b = Buf(t, name)
        b.is_psum = True
        return b

    def dram(self, name, shape, dt, kind="Internal"):
        t = self.nc.dram_tensor(name, list(shape), dt, kind=kind)
        return Buf(t, name)

    def _wait(self, e, ev):
        sm, v = ev
        if self.waited[e].get(sm, -1) >= v:
            return
        if sm == "Epe" and e == "pe":
            return
        self.waited[e][sm] = v
        self.prog[e].append(("w", sm, v))
        self.nwaits += 1

    @staticmethod
    def _norm(lst):
        out = []
        for x in lst:
            if isinstance(x, Buf):
                out.append((x, None))
            else:
                out.append(x)
        return out

    def op(self, e, fn, reads=(), writes=(), **kw):
        if isinstance(fn, str):
            name = fn

            def fn(eng, name=name, kw=kw):
                return getattr(eng, name)(**kw)
        reads = self._norm(reads)
        writes = self._norm(writes)
        deps = set()
        for b, p in reads:
            deps |= b.deps_for_read(p)
        for b, p in writes:
            deps |= b.deps_for_write(p)
        for ev in sorted(deps):
            self._wait(e, ev)
        self.cnt[e] += 1
        ev = ("E" + e, self.cnt[e])
        self.prog[e].append(("o", fn, "E" + e, 1))
        for b, p in reads:
            b.note_read(p, ev)
        for b, p in writes:
            b.note_write(p, ev)
        return ev

    def dma(self, e, out_ap, in_ap, reads=(), writes=(), **kw):
        reads = self._norm(reads)
        writes = self._norm(writes)
        deps = set()
        for b, p in reads:
            deps |= b.deps_for_read(p)
        for b, p in writes:
            deps |= b.deps_for_write(p)
        half = N_DMA_SEMS // 2
        base = 0 if e == "pool" else half
        rr = self.dma_rrs.get(e == "pool", 0)
        slot = self.dma_sems[base + rr]
        self.dma_rrs[e == "pool"] = (rr + 1) % half
        if slot[1] > 0:
            deps.add((slot[0], slot[1]))
        for ev in sorted(deps):
            self._wait(e, ev)
        slot[1] += 16
        ev = (slot[0], slot[1])

        def fn(eng, out_ap=out_ap, in_ap=in_ap, kw=kw):
            return eng.dma_start(out=out_ap, in_=in_ap, **kw)

        self.prog[e].append(("o", fn, slot[0], 16))
        for b, p in reads:
            b.note_read(p, ev)
        for b, p in writes:
            b.note_write(p, ev)
        return ev

    def wait_all(self, e, events):
        for ev in events:
            self._wait(e, ev)

    def emit(self):
        nc = self.nc
        with nc.Block() as block:
            def mk(e):
                def body(eng):
                    for it in self.prog[e]:
                        if it[0] == "w":
                            eng.wait_ge(self.sems[it[1]], it[2])
                        else:
                            ins = it[1](eng)
                            ins.then_inc(self.sems[it[2]], it[3])
                return body
            block.tensor(mk("pe"))
            block.scalar(mk("act"))
            block.vector(mk("dve"))
            block.gpsimd(mk("pool"))
            block.sync(mk("sp"))


def barrier(S):
    evs = [("E" + e, S.cnt[e]) for e in ENGS if S.cnt[e] > 0]
    evs += [(k, v) for k, v in S.dma_sems if v > 0]
    for e in ENGS:
        for ev in evs:
            S._wait(e, ev)


Sched.barrier = barrier


@contextlib.contextmanager
def _phase(S):
    S.barrier()
    old = S.stack
    with contextlib.ExitStack() as ps:
        S.stack = ps
        try:
            yield
        finally:
            S.barrier()
            S.stack = old


Sched.phase = _phase


def setup_consts(S, C, consts):
    C.rr = 0
    C.rt = 0
    C.ident = S.sbuf("ident", [128, 128], BF16)
    C.ones_f = S.sbuf("ones_f", [128, 512], F32)
    C.mhalf = S.sbuf("mhalf", [128, 1], F32)
    C.junk = S.sbuf("junk", [128, 1024], BF16)
    C.ss = [S.sbuf("ss%d" % i, [128, 1], F32) for i in range(4)]
    C.rstd = [S.sbuf("rstd%d" % i, [128, 1], F32) for i in range(4)]
    C.xs = [S.sbuf("xs%d" % i, [128, 1024], BF16) for i in range(2)]
    C.psT = [S.psum("psT%d" % i, [128, 8, 128], BF16) for i in range(2)]
    C.gB = [S.sbuf("gB%d" % i, [128, 8, 128], F32) for i in range(1)]
    C.gcol = S.sbuf("gcol", [128, 8], F32)
    S.dma("sp", C.ident[:], consts["ident_bf"], writes=[C.ident])
    C.one_c = S.sbuf("one_c", [128, 1], F32)
    C.eps_c = S.sbuf("eps_c", [128, 1], F32)
    S.op("dve", "memset", [], [C.one_c], ap=C.one_c[:], constant=1.0)
    S.op("dve", "memset", [], [C.eps_c], ap=C.eps_c[:], constant=1e-6)
    C.consts = consts

    S.op("dve", lambda e: e.memset(C.ones_f[:], 1.0), writes=[C.ones_f])
    S.op("dve", lambda e: e.memset(C.mhalf[:], -0.5), writes=[C.mhalf])


def load_rwkv_consts(S, C):
    consts = C.consts
    C.rmask128 = S.sbuf("rmask128", [128, 1024], F32)
    C.onesblk = S.sbuf("onesblk", [128, 128], F32)
    C.pmask = S.sbuf("pmask", [128, 2], F32)
    C.ident_f = S.sbuf("ident_f", [128, 1, 128], F32)
    C.m4_f = S.sbuf("m4_f", [128, 4, 128], F32)
    C.m4_b = S.sbuf("m4_b", [128, 4, 128], F32)
    C.mB_f = S.sbuf("mB_f", [128, 1, 128], F32)
    C.mB_b = S.sbuf("mB_b", [128, 1, 128], F32)
    S.dma("sp", C.rmask128[:], consts["rmask128"], writes=[C.rmask128])
    S.dma("sp", C.onesblk[:], consts["onesblk"], writes=[C.onesblk])
    S.dma("sp", C.pmask[:], consts["pmask"], writes=[C.pmask])
    S.dma("sp", C.ident_f[:, 0, :], consts["ident_f"], writes=[C.ident_f])
    S.dma("sp", C.m4_f[:], consts["m4_f"], writes=[C.m4_f])
    S.dma("sp", C.m4_b[:], consts["m4_b"], writes=[C.m4_b])
    S.dma("sp", C.mB_f[:, 0, :], consts["mB_f"], writes=[C.mB_f])
    S.dma("sp", C.mB_b[:, 0, :], consts["mB_b"], writes=[C.mB_b])


def load_gla_consts(S, C):
    consts = C.consts
    C.rmask = S.sbuf("rmask", [128, 2048], F32)
    C.mask_f = S.sbuf("mask_f", [128, 1, 128], F32)
    C.mask_b = S.sbuf("mask_b", [128, 1, 128], F32)
    S.dma("sp", C.rmask[:], consts["rmask"], writes=[C.rmask])
    S.dma("sp", C.mask_f[:, 0, :], consts["mask_f"], writes=[C.mask_f])
    S.dma("sp", C.mask_b[:, 0, :], consts["mask_b"], writes=[C.mask_b])


D = 1024
DFF = 2816
NJ = DFF // 128
T = 4096
SEQ = 2048
TT = 256
EPS = 1e-6

ZR_RR, ZR_RK, ZR_LORA, ZR_GD = 0, 512, 1024, 1152
ZR_GQ, ZR_GK, ZR_GR, ZR_DN, ZR_FN = 1280, 1536, 1792, 2304, 2432
ZROWS = 2944
NZC = ZROWS // 128
RW0 = 2080
WZ_SEGS = [(RW0, 512, ZR_RR), (RW0 + 512, 512, ZR_RK), (RW0 + 1536, 128, ZR_LORA), (RW0 + 1664, 96, ZR_GD),
           (0, 256, ZR_GQ), (256, 256, ZR_GK), (1024, 512, ZR_GR), (1536, 32, ZR_DN), (1568, 512, ZR_FN)]
WV_SEGS = [(512, 512, 0), (RW0 + 1024, 512, 512)]


class Ctx:
    pass


def load_w_cast(S, dst, part, dst_ap, src_ap, inner):
    S.dma("pool", dst_ap.rearrange("p (a b) -> p a b", b=inner),
          src_ap.rearrange("p (a b) -> p a b", b=inner), writes=[(dst, part)])


def load_x(S, xd, t0, xt):
    S.dma("sp", xt[:], xd.ap()[t0:t0 + TT, :].rearrange("(s p) d -> p s d", p=128),
          reads=[xd], writes=[xt])


def norm_stats(S, C, xt):
    outs = []
    for s in range(TT // 128):
        ss = C.ss[C.rr % 4]
        rstd = C.rstd[C.rr % 4]
        xs = C.xs[C.rr % 2]
        C.rr += 1
        S.op("act", "activation", [xt], [C.junk, ss], out=C.junk[:], in_=xt[:, s, :], func=AF.Square,
             accum_out=ss[:])
        S.op("dve", "tensor_scalar", [ss], [rstd], out=rstd[:], in0=ss[:], scalar1=1.0 / D, scalar2=EPS,
             op0=ALU.mult, op1=ALU.add)
        S.op("pool", "tensor_tensor", [rstd, C.mhalf], [rstd], out=rstd[:], in0=rstd[:], in1=C.mhalf[:],
             op=ALU.pow)
        S.op("act", "activation", [xt, rstd], [xs], out=xs[:], in_=xt[:, s, :], func=AF.Copy,
             scale=rstd[:, 0:1])
        outs.append(xs)
    return outs


def norm_transpose(S, C, xss, g_idx, hT, col0):
    for s, xs in enumerate(xss):
        psT = C.psT[C.rt % 2]
        C.rt += 1
        for c in range(8):
            S.op("pe", "transpose", [xs, C.ident], [psT], out=psT[:, c, :], in_=xs[:, c * 128:(c + 1) * 128],
                 identity=C.ident[:])
        S.op("dve", "tensor_tensor", [psT, C.gB[g_idx]], [hT],
             out=hT[:, :, col0 + s * 128:col0 + (s + 1) * 128], in0=psT[:], in1=C.gB[g_idx][:], op=ALU.mult)


def norm_tile(S, C, xd, t0, g_idx, xt, hT, col0):
    if xd is not None:
        load_x(S, xd, t0, xt)
    norm_transpose(S, C, norm_stats(S, C, xt), g_idx, hT, col0)


def load_gain(S, C, g_idx, gcol_ap):
    S.dma("sp", C.gcol[:], gcol_ap, writes=[C.gcol])
    for c in range(8):
        S.op("dve", "tensor_scalar", [C.gcol, C.ones_f], [C.gB[g_idx]], out=C.gB[g_idx][:, c, :],
             in0=C.ones_f[:, 0:128], scalar1=C.gcol[:, c:c + 1], scalar2=None, op0=ALU.mult)


def ffn_phase(S, C, xin, xout, gcol_ap, wg_ap, wu_ap, wd_ap, ntiles=T // TT):
    with S.phase():
        wg = S.sbuf("wg", [128, 8, DFF], BF16)
        wu = S.sbuf("wu", [128, 8, DFF], BF16)
        wd = S.sbuf("wd", [128, NJ, D], BF16)
        xt = [S.sbuf("xt%d" % i, [128, TT // 128, D], F32) for i in range(2)]
        ot = [S.sbuf("ot%d" % i, [128, TT // 128, D], F32) for i in range(2)]
        hT = [S.sbuf("hT%d" % i, [128, 8, TT], BF16) for i in range(2)]
        sg = [S.sbuf("sg%d" % i, [128, TT], BF16) for i in range(2)]
        act = [S.sbuf("act%d" % i, [128, TT], BF16) for i in range(3)]
        pgu = [S.psum("pgu%d" % i, [128, 2, TT], F32) for i in range(2)]
        pacc = [S.psum("pacc%d" % i, [128, 512], F32) for i in range(4)]
        load_gain(S, C, 0, gcol_ap)
        for c in range(8):
            load_w_cast(S, wg, c, wg[:, c, :], wg_ap[c * 128:(c + 1) * 128, :], 704)
            load_w_cast(S, wu, c, wu[:, c, :], wu_ap[c * 128:(c + 1) * 128, :], 704)
        for j in range(NJ):
            load_w_cast(S, wd, j, wd[:, j, :], wd_ap[j * 128:(j + 1) * 128, :], 1024)
        nsub = TT // 128
        load_x(S, xin, 0, xt[0])
        norm_tile(S, C, None, 0, 0, xt[0], hT[0], 0)
        for it in range(ntiles):
            t0 = it * TT
            x_t = xt[it % 2]
            h_t = hT[it % 2]
            o_t = ot[it % 2]
            if it + 1 < ntiles:
                load_x(S, xin, t0 + TT, xt[(it + 1) % 2])
            nxt = {}

            def gu(j):
                pg = pgu[j % 2]
                for (w, gi) in ((wg, 0), (wu, 1)):
                    for c in range(8):
                        S.op("pe", "matmul", [(w, c), h_t], [pg], out=pg[:, gi, :],
                             lhsT=w[:, c, j * 128:(j + 1) * 128], rhs=h_t[:, c, :], start=(c == 0), stop=(c == 7))
                sgj = sg[j % 2]
                aj = act[j % 3]
                S.op("act", "activation", [pg], [sgj], out=sgj[:], in_=pg[:, 0, :], func=AF.Silu)
                S.op("dve", "tensor_tensor", [pg, sgj], [aj], out=aj[:], in0=pg[:, 1, :], in1=sgj[:], op=ALU.mult)

            def down(j):
                aj = act[j % 3]
                for s in range(nsub):
                    for hf in range(2):
                        pa = pacc[s * 2 + hf]
                        S.op("pe", "matmul", [aj, (wd, j)], [pa], out=pa[:], lhsT=aj[:, s * 128:(s + 1) * 128],
                             rhs=wd[:, j, hf * 512:(hf + 1) * 512], start=(j == 0), stop=(j == NJ - 1))

            gu(0)
            for j in range(NJ):
                if j + 1 < NJ:
                    gu(j + 1)
                down(j)
                if it + 1 < ntiles:
                    if j == 3:
                        nxt["xs"] = norm_stats(S, C, xt[(it + 1) % 2])
                    if j == 12:
                        norm_transpose(S, C, nxt["xs"], 0, hT[(it + 1) % 2], 0)
            for s in range(nsub):
                for hf in range(2):
                    pa = pacc[s * 2 + hf]
                    S.op("dve", "scalar_tensor_tensor", [pa, x_t], [o_t], out=o_t[:, s, hf * 512:(hf + 1) * 512],
                         in0=pa[:], scalar=0.5, in1=x_t[:, s, hf * 512:(hf + 1) * 512], op0=ALU.mult, op1=ALU.add)
            S.dma("sp", xout.ap()[t0:t0 + TT, :].rearrange("(s p) d -> p s d", p=128), o_t[:],
                  reads=[o_t], writes=[xout])


def mixin_phase(S, C, xin, zT, zv, gcol_ap, win_ap, mub_ap, nseq=2):
    with S.phase():
        wz = S.sbuf("wz", [128, 8, ZROWS], BF16)
        wv = S.sbuf("wv", [128, 8, 1024], BF16)
        wzb = S.sbuf("wzb", [128, 8, 1280], BF16)
        wvb = S.sbuf("wvb", [128, 8, 512], BF16)
        mub = S.sbuf("mub", [128, 1792], F32)
        hseq = S.sbuf("hseq", [128, 8, SEQ], BF16)
        hshs = [S.sbuf("hsh%d" % i, [128, 8, TT], BF16) for i in range(2)]
        xt = [S.sbuf("xt%d" % i, [128, TT // 128, D], F32) for i in range(2)]
        zst = [S.sbuf("zst%d" % i, [128, NZC, TT], BF16) for i in range(2)]
        zvst = [S.sbuf("zvst%d" % i, [128, TT // 128, 1024], BF16) for i in range(2)]
        pz = [S.psum("pz%d" % i, [128, 512], F32) for i in range(3)]
        pv = [S.psum("pv%d" % i, [128, 512], F32) for i in range(2)]
        load_gain(S, C, 0, gcol_ap)
        S.op("pool", "memset", [], [wz], ap=wz[:], constant=0.0)
        S.dma("sp", mub[:], mub_ap, writes=[mub])
        for i, (sc, n, dc) in enumerate(WZ_SEGS):
            S.dma("pool", wz[:, :, dc:dc + n], win_ap[:, sc:sc + n].rearrange("(c p) n -> p c n", p=128),
                  writes=[(wz, "s%d" % i)])
        for i, (sc, n, dc) in enumerate(WV_SEGS):
            S.dma("pool", wv[:, :, dc:dc + n], win_ap[:, sc:sc + n].rearrange("(c p) n -> p c n", p=128),
                  writes=[(wv, "s%d" % i)])
        for c in range(8):
            S.op("dve", "scalar_tensor_tensor", [wz, mub], [wzb], out=wzb[:, c, :], in0=wz[:, c, 0:1280],
                 scalar=0.5, in1=mub[:, 0:1280], op0=ALU.mult, op1=ALU.mult)
            S.op("dve", "scalar_tensor_tensor", [wv, mub], [wvb], out=wvb[:, c, :], in0=wv[:, c, 512:1024],
                 scalar=0.5, in1=mub[:, 1280:1792], op0=ALU.mult, op1=ALU.mult)
        S.op("dve", "tensor_scalar", [mub], [mub], out=mub[:], in0=mub[:], scalar1=-1.0, scalar2=1.0,
             op0=ALU.mult, op1=ALU.add)
        for c in range(8):
            S.op("pool", "tensor_tensor", [wz, mub], [wz], out=wz[:, c, 0:1280], in0=wz[:, c, 0:1280],
                 in1=mub[:, 0:1280], op=ALU.mult)
            S.op("pool", "tensor_tensor", [wv, mub], [wv], out=wv[:, c, 512:1024], in0=wv[:, c, 512:1024],
                 in1=mub[:, 1280:1792], op=ALU.mult)
        nev = 0
        for b in range(nseq):
            for it in range(SEQ // TT):
                norm_tile(S, C, xin, b * SEQ + it * TT, 0, xt[it % 2], hseq, it * TT)
            for it in range(SEQ // TT):
                c0 = it * TT
                hsh = hshs[it % 2]
                a0 = 1 if it == 0 else 0
                a1 = TT - 1 if it == SEQ // TT - 1 else TT
                S.op("pool", "tensor_tensor", [hseq], [hsh], out=hsh[:, :, a0:a1], in0=hseq[:, :, c0 + a0 - 1:c0 + a1 - 1],
                     in1=hseq[:, :, c0 + a0 + 1:c0 + a1 + 1], op=ALU.add)
                if it == 0:
                    S.op("pool", "tensor_copy", [hseq], [hsh], out=hsh[:, :, 0:1], in_=hseq[:, :, 1:2])
                if it == SEQ // TT - 1:
                    S.op("pool", "tensor_copy", [hseq], [hsh], out=hsh[:, :, TT - 1:TT], in_=hseq[:, :, SEQ - 2:SEQ - 1])
                zs = zst[it % 2]
                zvs = zvst[it % 2]
                for ck in range(NZC):
                    p = pz[ck % 3]
                    nmm = 16 if ck < 10 else 8
                    k = 0
                    for c in range(8):
                        S.op("pe", "matmul", [wz, hseq], [p], out=p[:, 0:TT], lhsT=wz[:, c, ck * 128:(ck + 1) * 128],
                             rhs=hseq[:, c, c0:c0 + TT], start=(k == 0), stop=(k == nmm - 1))
                        k += 1
                    if ck < 10:
                        for c in range(8):
                            S.op("pe", "matmul", [wzb, hsh], [p], out=p[:, 0:TT],
                                 lhsT=wzb[:, c, ck * 128:(ck + 1) * 128], rhs=hsh[:, c, :],
                                 start=False, stop=(k == nmm - 1))
                            k += 1
                    nev += 1
                    if nev % 2:
                        S.op("act", "activation", [p], [zs], out=zs[:, ck, :], in_=p[:, 0:TT], func=AF.Copy)
                    else:
                        S.op("dve", "tensor_copy", [p], [zs], out=zs[:, ck, :], in_=p[:, 0:TT])
                for s in range(TT // 128):
                    cs = c0 + s * 128
                    for vi in range(2):
                        p = pv[vi]
                        nmm = 16 if vi == 1 else 8
                        k = 0
                        for c in range(8):
                            S.op("pe", "matmul", [wv, hseq], [p], out=p[:], lhsT=hseq[:, c, cs:cs + 128],
                                 rhs=wv[:, c, vi * 512:(vi + 1) * 512], start=(k == 0), stop=(k == nmm - 1))
                            k += 1
                        if vi == 1:
                            for c in range(8):
                                S.op("pe", "matmul", [wvb, hsh], [p], out=p[:], lhsT=hsh[:, c, s * 128:(s + 1) * 128],
                                     rhs=wvb[:, c, :], start=False, stop=(k == nmm - 1))
                                k += 1
                        if vi == 0:
                            S.op("act", "activation", [p], [zvs], out=zvs[:, s, 0:512], in_=p[:], func=AF.Copy)
                        else:
                            S.op("dve", "tensor_copy", [p], [zvs], out=zvs[:, s, 512:1024], in_=p[:])
                tg = b * SEQ + c0
                S.dma("sp", zT.ap()[:, tg:tg + TT].rearrange("(c p) t -> p c t", p=128), zs[:],
                      reads=[zs], writes=[(zT, "w%d" % (tg // TT))])
                S.dma("sp", zv.ap()[tg:tg + TT, :].rearrange("(s p) d -> p s d", p=128), zvs[:],
                      reads=[zvs], writes=[(zv, "w%d" % (tg // TT))])


GLA_C = -1.0 / 16.0


def v3(ap, b=128):
    return ap.rearrange("p (a b) -> p a b", b=b)


def gla_phase(S, C, zT, zv, yT, A, nseq=2):
    with S.phase():
        load_gla_consts(S, C)
        qT = S.sbuf("qT", [128, 2, SEQ], BF16)
        kT = S.sbuf("kT", [128, 2, SEQ], BF16)
        rT = S.sbuf("rT", [128, 4, SEQ], BF16)
        vt = S.sbuf("vt", [128, SEQ // 128, 512], BF16)
        dn = [S.sbuf("dn%d" % i, [16, SEQ], BF16) for i in range(2)]
        up = [S.sbuf("up%d" % i, [16, 256], BF16) for i in range(2)]
        gvec = S.sbuf("gvec", [128, 8], F32)
        negb = S.sbuf("negb", [128, 4], F32)
        oacc = S.sbuf("oacc", [128, 4, SEQ], F32)
        qd = S.sbuf("qd", [128, 2, SEQ], BF16)
        kd = S.sbuf("kd", [128, 2, SEQ], BF16)
        ktT = S.sbuf("ktT", [128, 2, SEQ], BF16)
        kt = S.sbuf("kt", [128, SEQ // 128, 256], BF16)
        gam = S.sbuf("gam", [128, 2, 32], F32)
        L = S.sbuf("L", [128, SEQ], F32)
        cs = S.sbuf("cs", [128, SEQ], F32)
        t1 = S.sbuf("t1", [128, SEQ], F32)
        E = S.sbuf("E", [128, SEQ], F32)
        Hf = S.sbuf("Hf", [128, 2, 128], F32)
        Ht = S.sbuf("Ht", [128, 2, 128], F32)
        Hb = S.sbuf("Hb", [128, 2, 128], BF16)
        sTs = [S.sbuf("sTs%d" % i, [128, 2, 2, 128], BF16) for i in range(2)]
        sq = S.sbuf("sq", [128, 512], F32)
        rs = S.sbuf("rs", [128, 512], F32)
        sr = S.sbuf("sr", [128, 512], BF16)
        yst = [S.sbuf("yst%d" % i, [128, 512], BF16) for i in range(2)]
        bS = [S.psum("bS%d" % i, [128, 512], F32) for i in range(2)]
        bO = [S.psum("bO%d" % i, [128, 512], F32) for i in range(2)]
        bU = [S.psum("bU%d" % i, [128, 512], F32) for i in range(2)]
        psL = bU[0]
        oacc4 = oacc[:].rearrange("p (hp par) t -> p hp par t", par=2)
        S.dma("pool", up[0][:], A["up_f"], writes=[up[0]])
        S.dma("pool", up[1][:], A["up_b"], writes=[up[1]])
        S.dma("sp", gvec[:], A["gvec"], writes=[gvec])
        S.op("dve", "tensor_scalar", [gvec], [negb], out=negb[:], in0=gvec[:, 0:4], scalar1=-1.0, scalar2=None,
             op0=ALU.mult)
        nb = SEQ // 128
        for b in range(nseq):
            tb = b * SEQ
            za = zT.ap()
            S.dma("sp", qT[:], za[ZR_GQ:ZR_GQ + 256, tb:tb + SEQ].rearrange("(c p) t -> p c t", p=128),
                  reads=[zT], writes=[qT])
            S.dma("sp", kT[:], za[ZR_GK:ZR_GK + 256, tb:tb + SEQ].rearrange("(c p) t -> p c t", p=128),
                  reads=[zT], writes=[kT])
            S.dma("sp", rT[:], za[ZR_GR:ZR_GR + 512, tb:tb + SEQ].rearrange("(c p) t -> p c t", p=128),
                  reads=[zT], writes=[rT])
            S.dma("sp", dn[0][:], za[ZR_DN:ZR_DN + 16, tb:tb + SEQ], reads=[zT], writes=[dn[0]])
            S.dma("sp", dn[1][:], za[ZR_DN + 16:ZR_DN + 32, tb:tb + SEQ], reads=[zT], writes=[dn[1]])
            S.dma("sp", vt[:], zv.ap()[tb:tb + SEQ, 0:512].rearrange("(n p) d -> p n d", p=128),
                  reads=[zv], writes=[vt])
            for d in range(2):
                for hp in range(2):
                    for q4 in range(4):
                        S.op("pe", "matmul", [up[d], dn[d]], [psL], out=psL[:], lhsT=up[d][:, hp * 128:(hp + 1) * 128],
                             rhs=dn[d][:, q4 * 512:(q4 + 1) * 512], start=True, stop=True)
                        S.op("act", "activation", [psL, negb], [E], out=E[:, q4 * 512:(q4 + 1) * 512], in_=psL[:],
                             func=AF.Exp, scale=-1.0, bias=negb[:, 2 * d + hp:2 * d + hp + 1])
                    S.op("act", "activation", [E, C.one_c], [L], out=L[:], in_=E[:], func=AF.Ln, bias=C.one_c[:, 0:1])
                    S.op("dve", "tensor_tensor_scan", [C.rmask, L], [cs], out=cs[:], data0=C.rmask[:], data1=L[:],
                         initial=0.0, op0=ALU.mult, op1=ALU.add)
                    cs3 = v3(cs[:], 64)
                    tot = cs3[:, :, 63:64]
                    S.op("dve", "tensor_tensor", [cs], [t1], out=v3(t1[:], 64),
                         in0=tot.to_broadcast([128, 32, 64]), in1=cs3, op=ALU.subtract)
                    S.op("act", "activation", [cs], [gam], out=gam[:, hp, :], in_=cs3[:, :, 63], func=AF.Exp,
                         scale=GLA_C)
                    if d == 0:
                        bI, tail = cs, t1
                    else:
                        S.op("dve", "tensor_tensor", [t1, L], [t1], out=t1[:], in0=t1[:], in1=L[:], op=ALU.add)
                        S.op("dve", "tensor_tensor", [cs, L], [cs], out=cs[:], in0=cs[:], in1=L[:], op=ALU.subtract)
                        bI, tail = t1, cs
                    S.op("act", "activation", [bI], [E], out=E[:], in_=bI[:], func=AF.Exp, scale=GLA_C)
                    S.op("dve", "scalar_tensor_tensor", [qT, E], [qd], out=qd[:, hp, :], in0=qT[:, hp, :], scalar=0.125,
                         in1=E[:], op0=ALU.mult, op1=ALU.mult)
                    S.op("act", "activation", [bI], [E], out=E[:], in_=bI[:], func=AF.Exp, scale=-GLA_C)
                    S.op("dve", "tensor_tensor", [kT, E], [kd], out=kd[:, hp, :], in0=kT[:, hp, :], in1=E[:], op=ALU.mult)
                    S.op("act", "activation", [tail], [E], out=E[:], in_=tail[:], func=AF.Exp, scale=GLA_C)
                    S.op("dve", "tensor_tensor", [kT, E], [ktT], out=ktT[:, hp, :], in0=kT[:, hp, :], in1=E[:],
                         op=ALU.mult)
                for blk in range(nb):
                    psT = C.psT[C.rr % 2]
                    C.rr += 1
                    for hp in range(2):
                        S.op("pe", "transpose", [ktT, C.ident], [psT], out=psT[:, hp, :],
                             in_=ktT[:, hp, blk * 128:(blk + 1) * 128], identity=C.ident[:])
                    S.op("act", "activation", [psT], [kt], out=v3(kt[:, blk, :]), in_=psT[:, 0:2, :], func=AF.Copy)
                S.op("dve", "memset", [], [Hf], ap=Hf[:], constant=0.0)
                S.op("dve", "memset", [], [Hb], ap=Hb[:], constant=0.0)
                mask = C.mask_f if d == 0 else C.mask_b
                blocks = range(nb) if d == 0 else range(nb - 1, -1, -1)
                for bi, blk in enumerate(blocks):
                    bc = slice(blk * 128, (blk + 1) * 128)
                    bf_ = bi % 2
                    sT = sTs[bf_]
                    for h in range(4):
                        hp, par, base = h // 2, h % 2, (h % 2) * 64
                        pS = v3(bS[par][:, 0:256])
                        S.op("pe", "matmul", [kd, qd], [bS[par]], out=pS[:, hp, :], lhsT=kd[base:base + 64, hp, bc],
                             rhs=qd[base:base + 64, hp, bc], start=True, stop=True)
                    for par in range(2):
                        pS = v3(bS[par][:, 0:256])
                        S.op("dve", "tensor_tensor", [bS[par], mask], [(sT, par)], out=sT[:, par, :, :], in0=pS,
                             in1=mask[:].to_broadcast([128, 2, 128]), op=ALU.mult)
                    for h in range(4):
                        hp, par = h // 2, h % 2
                        pO = v3(bO[par][:, 0:256])
                        S.op("pe", "matmul", [vt, (sT, par)], [bO[par]], out=pO[:, hp, :],
                             lhsT=vt[:, blk, h * 128:(h + 1) * 128], rhs=sT[:, par, hp, :], start=(hp == 0), stop=False,
                             skip_group_check=True)
                    halves = (0, 1) if d == 0 else (1, 0)
                    for half in halves:
                        c = blk * 2 + half
                        cc = slice(c * 64, (c + 1) * 64)
                        pb = slice(half * 64, (half + 1) * 64)
                        pU = v3(bU[half][:, 0:256])
                        for h in range(4):
                            hp, par, base = h // 2, h % 2, (h % 2) * 64
                            pO = v3(bO[par][:, 0:256])
                            S.op("pe", "matmul", [Hb, qd], [bO[par]], out=pO[:, hp, half * 64:(half + 1) * 64],
                                 lhsT=Hb[base:base + 64, hp, :], rhs=qd[base:base + 64, hp, cc],
                                 start=False, stop=True, skip_group_check=True)
                        for h in range(4):
                            hp, base = h // 2, (h % 2) * 64
                            S.op("pe", "matmul", [kt, vt], [bU[half]], out=pU[base:base + 64, hp, :],
                                 lhsT=kt[pb, blk, h * 64:(h + 1) * 64], rhs=vt[pb, blk, h * 128:(h + 1) * 128],
                                 start=True, stop=True)
                        S.op("pool", "tensor_tensor", [Hf, gam], [Ht], out=Ht[:], in0=Hf[:],
                             in1=gam[:, :, c:c + 1].to_broadcast([128, 2, 128]), op=ALU.mult)
                        S.op("dve", "tensor_tensor", [Ht, bU[half]], [Hf], out=Hf[:], in0=Ht[:], in1=pU, op=ALU.add)
                        S.op("act", "activation", [Hf], [Hb], out=Hb[:], in_=Hf[:], func=AF.Copy)
                    for par in range(2):
                        pO = v3(bO[par][:, 0:256])
                        if d == 0:
                            S.op("act", "activation", [bO[par]], [oacc], out=oacc4[:, :, par, bc], in_=pO,
                                 func=AF.Copy)
                        else:
                            S.op("dve", "tensor_tensor", [bO[par], oacc], [oacc], out=oacc4[:, :, par, bc], in0=pO,
                                 in1=oacc4[:, :, par, bc], op=ALU.add)
            k = 0
            for h in range(4):
                for q4 in range(4):
                    c4 = slice(q4 * 512, (q4 + 1) * 512)
                    S.op("act", "activation", [oacc], [sq], out=sq[:], in_=oacc[:, h, c4], func=AF.Square)
                    S.op("pe", "matmul", [C.ones_f, sq], [psL], out=psL[:], lhsT=C.ones_f[:, 0:128], rhs=sq[:],
                         start=True, stop=True)
                    S.op("act", "activation", [psL, C.eps_c], [rs], out=rs[:], in_=psL[:], func=AF.Sqrt,
                         scale=1.0 / 128, bias=C.eps_c[:, 0:1])
                    S.op("dve", "reciprocal", [rs], [rs], out=rs[:], in_=rs[:])
                    S.op("act", "activation", [rT], [sr], out=sr[:], in_=rT[:, h, c4], func=AF.Silu)
                    S.op("dve", "scalar_tensor_tensor", [oacc, gvec, rs], [rs], out=rs[:], in0=oacc[:, h, c4],
                         scalar=gvec[:, 4 + h:5 + h], in1=rs[:], op0=ALU.mult, op1=ALU.mult)
                    ys = yst[k % 2]
                    k += 1
                    S.op("dve", "tensor_tensor", [rs, sr], [ys], out=ys[:], in0=rs[:], in1=sr[:], op=ALU.mult)
                    S.dma("sp", yT.ap()[h * 128:(h + 1) * 128, tb + q4 * 512:tb + (q4 + 1) * 512], ys[:],
                          reads=[ys], writes=[(yT, "g%d_%d_%d" % (b, h, q4))])


def fnet_phase(S, C, zT, yT, A, nseq=2):
    with S.phase():
        Ws = S.sbuf("Ws", [128, 2, 16, SEQ], BF16)
        Wc = S.sbuf("Wc", [128, 256], BF16)
        uT = S.sbuf("uT", [128, 4, SEQ], BF16)
        As = [S.sbuf("As%d" % i, [128, 16, 256], BF16) for i in range(2)]
        yst = [S.sbuf("ystf%d" % i, [128, 512], BF16) for i in range(2)]
        pA = [S.psum("pA%d" % i, [128, 512], F32) for i in range(2)]
        pY = [S.psum("pY%d" % i, [128, 512], F32) for i in range(2)]
        S.dma("sp", Wc[:], A["dftC"], writes=[Wc])
        for t in range(2):
            for hlf in range(2):
                S.dma("sp", Ws[:, t, hlf * 8:(hlf + 1) * 8, :],
                      A["dftS"][t, hlf * 1024:(hlf + 1) * 1024, :].rearrange("(n p) s -> p n s", p=128),
                      writes=[(Ws, "%d_%d" % (t, hlf))])
        k = 0
        for b in range(nseq):
            tb = b * SEQ
            S.dma("sp", uT[:], zT.ap()[ZR_FN:ZR_FN + 512, tb:tb + SEQ].rearrange("(c p) t -> p c t", p=128),
                  reads=[zT], writes=[uT])
            for g in range(4):
                a_s = As[g % 2]
                for n in range(16):
                    p = pA[n % 2]
                    S.op("pe", "matmul", [uT, Wc], [p], out=p[:, 0:256], lhsT=uT[:, g, n * 128:(n + 1) * 128], rhs=Wc[:],
                         start=True, stop=True)
                    if n % 2:
                        S.op("act", "activation", [p], [a_s], out=a_s[:, n, :], in_=p[:, 0:256], func=AF.Copy)
                    else:
                        S.op("dve", "tensor_copy", [p], [a_s], out=a_s[:, n, :], in_=p[:, 0:256])
                for q4 in range(4):
                    p = pY[q4 % 2]
                    i = 0
                    for t in range(2):
                        for n in range(16):
                            S.op("pe", "matmul", [a_s, Ws], [p], out=p[:], lhsT=a_s[:, n, t * 128:(t + 1) * 128],
                                 rhs=Ws[:, t, n, q4 * 512:(q4 + 1) * 512], start=(i == 0), stop=(i == 31))
                            i += 1
                    ys = yst[k % 2]
                    k += 1
                    S.op("act", "activation", [p], [ys], out=ys[:], in_=p[:], func=AF.Copy)
                    S.dma("sp", yT.ap()[512 + g * 128:512 + (g + 1) * 128, tb + q4 * 512:tb + (q4 + 1) * 512], ys[:],
                          reads=[ys], writes=[(yT, "f%d_%d_%d" % (b, g, q4))])


def merge_phase(S, C, xin, xout, yT, gcol_ap, win_ap, pw_aps, wout_ap, ntiles=T // TT):
    with S.phase():
        wgt = S.sbuf("wgt", [128, 8, 3072], BF16)
        pw = [S.sbuf("pw%d" % i, [128, 4, D], BF16) for i in range(3)]
        wo = S.sbuf("wo", [128, 8, D], BF16)
        xt = [S.sbuf("xt%d" % i, [128, TT // 128, D], F32) for i in range(2)]
        ot = [S.sbuf("ot%d" % i, [128, TT // 128, D], F32) for i in range(2)]
        hT = [S.sbuf("hT%d" % i, [128, 8, TT], BF16) for i in range(2)]
        ysb = [S.sbuf("ysb%d" % i, [128, 12, TT], BF16) for i in range(2)]
        sg = S.sbuf("sgm", [128, 24, TT], BF16)
        macc = S.sbuf("macc", [128, 8, TT], F32)
        mtmp = [S.sbuf("mtmp%d" % i, [128, TT], F32) for i in range(2)]
        mT = S.sbuf("mT", [128, 8, TT], BF16)
        pG = [S.psum("pG%d" % i, [128, 512], F32) for i in range(2)]
        pP = [S.psum("pP%d" % i, [128, 512], F32) for i in range(2)]
        pX = [S.psum("pX%d" % i, [128, 512], F32) for i in range(2)]
        load_gain(S, C, 0, gcol_ap)
        for i in range(3):
            S.dma("pool", wgt[:, :, i * 1024:(i + 1) * 1024],
                  win_ap[:, 3840 + i * 1024:3840 + (i + 1) * 1024].rearrange("(c p) n -> p c n", p=128),
                  writes=[(wgt, i)])
            S.dma("pool", pw[i][:], pw_aps[i].rearrange("(c p) n -> p c n", p=128), writes=[pw[i]])
        S.dma("pool", wo[:], wout_ap.rearrange("(c p) n -> p c n", p=128), writes=[wo])
        nsub = TT // 128
        k = 0
        def ld(it):
            load_x(S, xin, it * TT, xt[it % 2])
            S.dma("sp", ysb[it % 2][:], yT.ap()[:, it * TT:(it + 1) * TT].rearrange("(c p) t -> p c t", p=128),
                  reads=[yT], writes=[ysb[it % 2]])

        ld(0)
        for it in range(ntiles):
            t0 = it * TT
            x_t, h_t, o_t, y_t = xt[it % 2], hT[it % 2], ot[it % 2], ysb[it % 2]
            if it + 1 < ntiles:
                ld(it + 1)
            norm_tile(S, C, None, t0, 0, x_t, h_t, 0)
            for gi in range(24):
                p = pG[gi % 2]
                for c in range(8):
                    S.op("pe", "matmul", [wgt, h_t], [p], out=p[:, 0:TT], lhsT=wgt[:, c, gi * 128:(gi + 1) * 128],
                         rhs=h_t[:, c, :], start=(c == 0), stop=(c == 7))
                S.op("act", "activation", [p], [(sg, gi)], out=sg[:, gi, :], in_=p[:, 0:TT], func=AF.Sigmoid)
            for dc in range(8):
                for i in range(3):
                    p = pP[k % 2]
                    k += 1
                    for kc in range(4):
                        S.op("pe", "matmul", [pw[i], y_t], [p], out=p[:, 0:TT], lhsT=pw[i][:, kc, dc * 128:(dc + 1) * 128],
                             rhs=y_t[:, i * 4 + kc, :], start=(kc == 0), stop=(kc == 3))
                    if i == 0:
                        S.op("dve", "tensor_tensor", [p, (sg, dc)], [(macc, dc)], out=macc[:, dc, :], in0=p[:, 0:TT],
                             in1=sg[:, dc, :], op=ALU.mult)
                    else:
                        mt = mtmp[i % 2]
                        S.op("dve", "tensor_tensor", [p, (sg, i * 8 + dc)], [mt], out=mt[:], in0=p[:, 0:TT],
                             in1=sg[:, i * 8 + dc, :], op=ALU.mult)
                        if i == 1:
                            S.op("pool", "tensor_tensor", [(macc, dc), mt], [(macc, dc)], out=macc[:, dc, :],
                                 in0=macc[:, dc, :], in1=mt[:], op=ALU.add)
                        else:
                            S.op("dve", "tensor_tensor", [(macc, dc), mt], [(mT, dc)], out=mT[:, dc, :],
                                 in0=macc[:, dc, :], in1=mt[:], op=ALU.add)
            for s in range(nsub):
                for hf in range(2):
                    p = pX[hf]
                    for dc in range(8):
                        S.op("pe", "matmul", [(mT, dc), wo], [p], out=p[:], lhsT=mT[:, dc, s * 128:(s + 1) * 128],
                             rhs=wo[:, dc, hf * 512:(hf + 1) * 512], start=(dc == 0), stop=(dc == 7))
                    S.op("dve", "tensor_tensor", [p, x_t], [o_t], out=o_t[:, s, hf * 512:(hf + 1) * 512], in0=p[:],
                         in1=x_t[:, s, hf * 512:(hf + 1) * 512], op=ALU.add)
            S.dma("sp", xout.ap()[t0:t0 + TT, :].rearrange("(s p) d -> p s d", p=128), o_t[:],
                  reads=[o_t], writes=[xout])


def final_norm_phase(S, C, xin, out, grow_ap):
    with S.phase():
        gb = S.sbuf("gbf", [128, D], F32)
        xt = [S.sbuf("xt%d" % i, [128, TT // 128, D], F32) for i in range(2)]
        ot = [S.sbuf("ot%d" % i, [128, TT // 128, D], F32) for i in range(2)]
        S.dma("sp", gb[:], grow_ap, writes=[gb])
        for it in range(T // TT):
            t0 = it * TT
            x_t, o_t = xt[it % 2], ot[it % 2]
            S.dma("sp", x_t[:], xin.ap()[t0:t0 + TT, :].rearrange("(s p) d -> p s d", p=128), reads=[xin], writes=[x_t])
            for s in range(TT // 128):
                ss = C.ss[C.rr % 4]
                rstd = C.rstd[C.rr % 4]
                C.rr += 1
                S.op("act", "activation", [x_t], [C.junk, ss], out=C.junk[:], in_=x_t[:, s, :], func=AF.Square,
                     accum_out=ss[:])
                S.op("dve", "tensor_scalar", [ss], [rstd], out=rstd[:], in0=ss[:], scalar1=1.0 / D, scalar2=EPS,
                     op0=ALU.mult, op1=ALU.add)
                S.op("pool", "tensor_tensor", [rstd, C.mhalf], [rstd], out=rstd[:], in0=rstd[:], in1=C.mhalf[:],
                     op=ALU.pow)
                S.op("dve", "scalar_tensor_tensor", [x_t, rstd, gb], [o_t], out=o_t[:, s, :], in0=x_t[:, s, :],
                     scalar=rstd[:, 0:1], in1=gb[:], op0=ALU.mult, op1=ALU.mult)
            S.dma("sp", out.ap()[t0:t0 + TT, :].rearrange("(s p) d -> p s d", p=128), o_t[:], reads=[o_t], writes=[out])


RW_C = -0.6065306597126334
import os as _os
RW_STAGE = int(_os.environ.get("K_RW_STAGE", "9"))
RW_CUT = int(_os.environ.get("K_RW_CUT", "9"))
RV = {"w0_f": 0, "w0_b": 1, "a0_f": 2, "a0_b": 3, "k_k": 4, "k_a": 5, "r_k": 6, "ln_g": 7, "ln_b": 8}


def rwkv_phase(S, C, zT, zv, yT, A, nseq=2):
    HS = 512
    NB = SEQ // 128
    with S.phase():
        load_rwkv_consts(S, C)
        w2 = [S.sbuf("w2_%d" % i, [32, 512], BF16) for i in range(2)]
        a2 = [S.sbuf("a2_%d" % i, [32, 512], BF16) for i in range(2)]
        g2 = S.sbuf("g2", [96, 512], BF16)
        rvec = S.sbuf("rvec", [128, 36], F32)
        omka = S.sbuf("omka", [128, 4], F32)
        lnb_eps = S.sbuf("lneps", [128, 1], F32)
        lo = S.sbuf("lo", [96, SEQ], BF16)
        th = [S.sbuf("th%d" % i, [32, SEQ], BF16) for i in range(2)]
        ad = [S.sbuf("ad%d" % i, [32, SEQ], BF16) for i in range(2)]
        sgd = S.sbuf("sgd", [96, SEQ], BF16)
        vt = S.sbuf("vtr", [128, NB, 256], BF16)
        rk = S.sbuf("rk", [128, 2, SEQ], BF16)
        XS = [[S.sbuf("X%d_%d" % (g, i), [128, HS], F32) for i in range(7)] for g in range(2)]
        X = XS[0]
        KR = S.sbuf("KR", [128, 2, NB, 2, 128], BF16)
        BT = S.sbuf("BT", [128, 2, SEQ], BF16)
        KT = S.sbuf("KT", [128, 2, SEQ], BF16)
        tlT = [S.sbuf("tlT%d" % i, [128, SEQ], BF16) for i in range(2)]
        tl = [S.sbuf("tl%d" % i, [128, NB, 256], BF16) for i in range(2)]
        gam = S.sbuf("gamr", [128, 2, NB], F32)
        yacc = S.sbuf("yacc", [128, 2, SEQ], F32)
        bonus = S.sbuf("bonus", [128, 2, SEQ], BF16)
        krz = [S.sbuf("krz%d" % i, [128, 4, 256], BF16) for i in range(2)]
        W4 = [S.sbuf("W4_%d" % i, [128, 4, 4, 128], BF16) for i in range(2)]
        Bbm = [S.sbuf("Bbm%d" % i, [128, 4, 128], BF16) for i in range(2)]
        Mt = [S.sbuf("Mt%d" % i, [128, 4, 128], BF16) for i in range(2)]
        Nt = [S.sbuf("Nt%d" % i, [128, 4, 128], BF16) for i in range(2)]
        Sb = [S.sbuf("Sb%d" % i, [128, 4, 128], BF16) for i in range(2)]
        Gs = S.sbuf("Gs", [128, 4, 64], BF16)
        Ps = S.sbuf("Ps", [128, 4, 64], BF16)
        Hf = S.sbuf("Hfr", [128, 2, 64], F32)
        Ht = S.sbuf("Htr", [128, 2, 64], F32)
        Hb = S.sbuf("Hbr", [128, 2, 64], BF16)
        yst = [S.sbuf("ystr%d" % i, [128, 512], BF16) for i in range(2)]
        bk = [S.psum("rb%d" % i, [128, 512], F32) for i in range(6)]
        bQ, bR, bG, bY, bH = (bk[0], bk[1]), bk[2], bk[3], bk[4], bk[5]
        for i, nm in enumerate(("w2_f", "w2_b")):
            S.dma("pool", w2[i][:], A[nm], writes=[w2[i]])
        for i, nm in enumerate(("a2_f", "a2_b")):
            S.dma("pool", a2[i][:], A[nm], writes=[a2[i]])
        S.dma("pool", g2[:], A["g2"], writes=[g2])
        S.dma("sp", rvec[:], A["rvec"], writes=[rvec])
        S.op("dve", "tensor_scalar", [rvec], [omka], out=omka[:], in0=rvec[:, RV["k_a"] * 4:RV["k_a"] * 4 + 4],
             scalar1=-1.0, scalar2=1.0, op0=ALU.mult, op1=ALU.add)
        S.op("dve", "memset", [], [lnb_eps], ap=lnb_eps[:], constant=64e-5)

        def rv(name, hp):
            c = RV[name] * 4 + hp
            return rvec[:, c:c + 1]

        for b in range(nseq):
            if RW_CUT < 1:
                continue
            tb = b * SEQ
            za = zT.ap()
            for i in range(2):
                S.dma("sp", lo[0:32, :], za[ZR_LORA + 32 * i:ZR_LORA + 32 * i + 32, tb:tb + SEQ], reads=[zT], writes=[lo])
                S.op("act", "activation", [lo], [th[i]], out=th[i][:], in_=lo[0:32, :], func=AF.Tanh)
            for i in range(2):
                S.dma("sp", ad[i][:], za[ZR_LORA + 64 + 32 * i:ZR_LORA + 96 + 32 * i, tb:tb + SEQ], reads=[zT],
                      writes=[ad[i]])
            S.dma("sp", lo[:], za[ZR_GD:ZR_GD + 96, tb:tb + SEQ], reads=[zT], writes=[lo])
            S.op("act", "activation", [lo], [sgd], out=sgd[:], in_=lo[:], func=AF.Sigmoid)
            for grp in range(2):
                if RW_CUT < 2:
                    continue
                S.dma("sp", vt[:], zv.ap()[tb:tb + SEQ, 512 + grp * 256:512 + (grp + 1) * 256]
                      .rearrange("(n p) d -> p n d", p=128), reads=[zv], writes=[vt])
                for d in range(2):
                    for hl in range(2):
                        hp = grp * 2 + hl
                        if d == 0:
                            S.dma("sp", rk[:, 0, :], za[ZR_RR + hp * 128:ZR_RR + (hp + 1) * 128, tb:tb + SEQ], reads=[zT],
                                  writes=[(rk, 0)])
                            S.dma("sp", rk[:, 1, :], za[ZR_RK + hp * 128:ZR_RK + (hp + 1) * 128, tb:tb + SEQ], reads=[zT],
                                  writes=[(rk, 1)])
                        elif hl == 0 or True:
                            S.dma("sp", rk[:, 0, :], za[ZR_RR + hp * 128:ZR_RR + (hp + 1) * 128, tb:tb + SEQ], reads=[zT],
                                  writes=[(rk, 0)])
                            S.dma("sp", rk[:, 1, :], za[ZR_RK + hp * 128:ZR_RK + (hp + 1) * 128, tb:tb + SEQ], reads=[zT],
                                  writes=[(rk, 1)])
                        def prep_gen(hs, Xs, bnk):
                            c0 = hs * HS
                            hc = slice(c0, c0 + HS)
                            nbh = HS // 128
                            bsl = slice(hs * nbh, (hs + 1) * nbh)
                            SG, CS, A3, A2, KP, BE, SB = Xs
                            S.op("pe", "matmul", [w2[d], th[d]], [bnk], out=bnk[:], lhsT=w2[d][:, hp * 128:(hp + 1) * 128],
                                 rhs=th[d][:, hc], start=True, stop=True)
                            S.op("act", "activation", [bnk, rvec], [SG], out=SG[:], in_=bnk[:], func=AF.Sigmoid,
                                 bias=rv("w0_f" if d == 0 else "w0_b", hp))
                            yield
                            S.op("dve", "tensor_tensor_scan", [C.rmask128, SG], [CS], out=CS[:], data0=C.rmask128[:, 0:HS],
                                 data1=SG[:], initial=0.0, op0=ALU.mult, op1=ALU.add)
                            yield
                            cs3 = v3(CS[:], 128)
                            S.op("dve", "tensor_tensor", [CS], [A3], out=v3(A3[:], 128),
                                 in0=cs3[:, :, 127:128].to_broadcast([128, nbh, 128]), in1=cs3, op=ALU.subtract)
                            S.op("act", "activation", [CS], [gam], out=gam[:, hl, bsl], in_=cs3[:, :, 127], func=AF.Exp,
                                 scale=RW_C)
                            S.op("pool", "tensor_tensor", [CS, SG], [A2], out=A2[:], in0=CS[:], in1=SG[:], op=ALU.subtract)
                            yield
                            if d == 0:
                                bI, bE, tail = CS, A2, A3
                            else:
                                S.op("dve", "tensor_tensor", [A3, SG], [CS], out=CS[:], in0=A3[:], in1=SG[:], op=ALU.add)
                                bI, bE, tail = CS, A3, A2
                                yield
                            AV = SG
                            S.op("pe", "matmul", [a2[d], ad[d]], [bnk], out=bnk[:], lhsT=a2[d][:, hp * 128:(hp + 1) * 128],
                                 rhs=ad[d][:, hc], start=True, stop=True)
                            S.op("act", "activation", [bnk, rvec], [AV], out=AV[:], in_=bnk[:], func=AF.Sigmoid,
                                 bias=rv("a0_f" if d == 0 else "a0_b", hp))
                            yield
                            S.op("act", "activation", [(rk, 1), rvec], [BE], out=BE[:], in_=rk[:, 1, hc], func=AF.Square,
                                 scale=rv("k_k", hp))
                            yield
                            S.op("pe", "matmul", [C.onesblk, BE], [bnk], out=bnk[:], lhsT=C.onesblk[:], rhs=BE[:],
                                 start=True, stop=True)
                            S.op("act", "activation", [bnk], [SB], out=SB[:], in_=bnk[:], func=AF.Sqrt)
                            yield
                            S.op("dve", "tensor_scalar", [SB], [SB], out=SB[:], in0=SB[:], scalar1=1e-6, scalar2=None,
                                 op0=ALU.max)
                            S.op("dve", "reciprocal", [SB], [SB], out=SB[:], in_=SB[:])
                            yield
                            S.op("dve", "scalar_tensor_tensor", [(rk, 1), rvec, SB], [KP], out=KP[:], in0=rk[:, 1, hc],
                                 scalar=rv("k_k", hp), in1=SB[:], op0=ALU.mult, op1=ALU.mult)
                            yield
                            S.op("dve", "tensor_tensor", [KP, AV], [BE], out=BE[:], in0=KP[:], in1=AV[:], op=ALU.mult)
                            yield
                            S.op("dve", "tensor_scalar", [AV, rvec, omka], [AV], out=AV[:], in0=AV[:], scalar1=rv("k_a", hp),
                                 scalar2=omka[:, hp:hp + 1], op0=ALU.mult, op1=ALU.add)
                            yield
                            S.op("pool", "tensor_tensor", [(rk, 1), AV], [AV], out=AV[:], in0=rk[:, 1, hc], in1=AV[:],
                                 op=ALU.mult)
                            yield
                            KD = AV
                            if d == 0:
                                S.op("dve", "scalar_tensor_tensor", [(rk, 0), rvec, KD], [SB], out=SB[:], in0=rk[:, 0, hc],
                                     scalar=rv("r_k", hp), in1=KD[:], op0=ALU.mult, op1=ALU.mult)
                                yield
                                S.op("pe", "matmul", [C.onesblk, SB], [bnk], out=bnk[:], lhsT=C.onesblk[:], rhs=SB[:],
                                     start=True, stop=True)
                                psT = C.psT[C.rr % 2]
                                C.rr += 1
                                for j in range(4):
                                    blk = c0 // 128 + j
                                    S.op("pe", "transpose", [vt, C.ident], [psT], out=psT[:, j, :],
                                         in_=vt[:, blk, hl * 128:(hl + 1) * 128], identity=C.ident[:])
                                S.op("act", "activation", [bnk], [SB], out=SB[:], in_=bnk[:], func=AF.Copy)
                                yield
                                S.op("dve", "tensor_tensor", [psT, SB], [(bonus, hl)], out=v3(bonus[:, hl, hc]),
                                     in0=psT[:, 0:4, :], in1=v3(SB[:]), op=ALU.mult)
                                yield
                            S.op("act", "activation", [bE], [bE], out=bE[:], in_=bE[:], func=AF.Exp, scale=RW_C)
                            yield
                            S.op("act", "activation", [bI], [bI], out=bI[:], in_=bI[:], func=AF.Exp, scale=RW_C)
                            yield
                            S.op("act", "activation", [tail], [tail], out=tail[:], in_=tail[:], func=AF.Exp, scale=RW_C)
                            yield
                            S.op("dve", "tensor_tensor", [KP, bE], [(KR, hl)], out=KR[:, hl, bsl, 0, :], in0=v3(KP[:]),
                                 in1=v3(bE[:]), op=ALU.mult)
                            yield
                            rdec = bI if d == 0 else bE
                            S.op("dve", "tensor_tensor", [(rk, 0), rdec], [(KR, hl)], out=KR[:, hl, bsl, 1, :],
                                 in0=v3(rk[:, 0, hc]), in1=v3(rdec[:]), op=ALU.mult)
                            yield
                            S.op("pool", "tensor_tensor", [BE, tail], [(tlT[0], hs)], out=tlT[0][:, hc], in0=BE[:],
                                 in1=tail[:], op=ALU.mult)
                            yield
                            S.op("pool", "tensor_tensor", [KD, tail], [(tlT[1], hs)], out=tlT[1][:, hc], in0=KD[:],
                                 in1=tail[:], op=ALU.mult)
                            yield
                            S.op("dve", "reciprocal", [bI], [bI], out=bI[:], in_=bI[:])
                            yield
                            S.op("dve", "tensor_tensor", [BE, bI], [(BT, hl)], out=BT[:, hl, hc], in0=BE[:], in1=bI[:],
                                 op=ALU.mult)
                            yield
                            S.op("pool", "tensor_tensor", [KD, bI], [(KT, hl)], out=KT[:, hl, hc], in0=KD[:], in1=bI[:],
                                 op=ALU.mult)
                            yield

                        for h0 in range(0, SEQ // HS, 2):
                            gens = [prep_gen(h0 + g_, XS[g_], bk[g_]) for g_ in range(2)]
                            while gens:
                                for g in list(gens):
                                    try:
                                        next(g)
                                    except StopIteration:
                                        gens.remove(g)
                        if RW_CUT < 6:
                            continue
                        for blk in range(NB):
                            psT = C.psT[C.rr % 2]
                            C.rr += 1
                            for i in range(2):
                                S.op("pe", "transpose", [tlT[i], C.ident], [psT], out=psT[:, i, :],
                                     in_=tlT[i][:, blk * 128:(blk + 1) * 128], identity=C.ident[:])
                            for i in range(2):
                                if blk % 2:
                                    S.op("act", "activation", [psT], [(tl[i], hl)], out=tl[i][:, blk, hl * 128:(hl + 1) * 128],
                                         in_=psT[:, i, :], func=AF.Copy)
                                else:
                                    S.op("dve", "tensor_copy", [psT], [(tl[i], hl)], out=tl[i][:, blk, hl * 128:(hl + 1) * 128],
                                         in_=psT[:, i, :])
                    if RW_STAGE < 2:
                        continue
                    S.op("dve", "memset", [], [Hf], ap=Hf[:], constant=0.0)
                    S.op("dve", "memset", [], [Hb], ap=Hb[:], constant=0.0)
                    m4 = C.m4_f if d == 0 else C.m4_b
                    mB = C.mB_f if d == 0 else C.mB_b
                    blocks = range(NB) if d == 0 else range(NB - 1, -1, -1)
                    blocks = list(blocks)

                    def pre_inv(bi, blk):
                        th = []
                        bc = slice(blk * 128, (blk + 1) * 128)
                        kz, w4, bbm, sb = krz[bi % 2], W4[bi % 2], Bbm[bi % 2], Sb[bi % 2]

                        def t_pre(h4):
                            def f():
                                hl, par = h4 // 2, h4 % 2
                                q = bQ[h4 % 2]
                                if h4 % 2:
                                    S.op("act", "activation", [(KR, hl), C.pmask], [(kz, h4)], out=kz[:, h4, :],
                                         in_=KR[:, hl, blk, :, :].rearrange("p a b -> p (a b)"), func=AF.Copy,
                                         scale=C.pmask[:, par:par + 1])
                                else:
                                    S.op("pool", "tensor_scalar", [(KR, hl), C.pmask], [(kz, h4)], out=kz[:, h4, :],
                                         in0=KR[:, hl, blk, :, :].rearrange("p a b -> p (a b)"),
                                         scalar1=C.pmask[:, par:par + 1], scalar2=None, op0=ALU.mult)
                                S.op("pe", "matmul", [(BT, hl), (kz, h4)], [q], out=q[:, 0:128], lhsT=BT[:, hl, bc],
                                     rhs=kz[:, h4, 0:128], start=True, stop=True)
                                S.op("pe", "matmul", [(BT, hl), (kz, h4)], [q], out=q[:, 128:256], lhsT=kz[:, h4, 0:128],
                                     rhs=BT[:, hl, bc], start=False, stop=True, skip_group_check=True)
                                S.op("pe", "matmul", [(KT, hl), (kz, h4)], [q], out=q[:, 256:512], lhsT=KT[:, hl, bc],
                                     rhs=kz[:, h4, :], start=False, stop=True, skip_group_check=True)
                                S.op("pe", "matmul", [(BT, hl), (kz, h4)], [bR], out=bR[:, h4 * 128:(h4 + 1) * 128],
                                     lhsT=BT[:, hl, bc], rhs=kz[:, h4, 128:256], start=(h4 == 0), stop=True,
                                     skip_group_check=True)
                                S.op("dve", "tensor_tensor", [q, m4], [(w4, h4)], out=w4[:, h4, :, :], in0=v3(q[:]),
                                     in1=m4[:], op=ALU.mult)
                            return f

                        for h4 in range(4):
                            th.append(t_pre(h4))

                        def t_s0():
                            S.op("act", "activation", [bR], [bbm], out=bbm[:], in_=v3(bR[:]), func=AF.Copy)
                            S.op("pool", "tensor_tensor", [bbm, mB], [bbm], out=bbm[:], in0=bbm[:],
                                 in1=mB[:].to_broadcast([128, 4, 128]), op=ALU.mult)
                            S.op("dve", "tensor_tensor", [w4, C.ident_f], [sb], out=sb[:], in0=w4[:, :, 0, :],
                                 in1=C.ident_f[:].to_broadcast([128, 4, 128]), op=ALU.add)
                        th.append(t_s0)
                        st_ = {"Mp": w4[:, :, 0, :], "Np": w4[:, :, 1, :], "Mr": [w4], "Nr": [w4]}

                        def t_lvA(lv):
                            def f():
                                Mn, Nn = Mt[lv % 2], Nt[lv % 2]
                                Mp, Np = st_["Mp"], st_["Np"]
                                rd = st_["Mr"] + st_["Nr"]
                                for h4 in range(4):
                                    if lv < 6:
                                        S.op("pe", "matmul", rd, [bQ[0]], out=bQ[0][:, h4 * 128:(h4 + 1) * 128],
                                             lhsT=Np[:, h4, :], rhs=Mp[:, h4, :], start=(h4 == 0), stop=True,
                                             skip_group_check=True)
                                    S.op("pe", "matmul", rd, [bQ[1]], out=bQ[1][:, h4 * 128:(h4 + 1) * 128],
                                         lhsT=Mp[:, h4, :], rhs=Np[:, h4, :], start=(h4 == 0), stop=True,
                                         skip_group_check=True)
                                if lv < 6:
                                    S.op("act", "activation", [bQ[0]], [Mn], out=Mn[:], in_=v3(bQ[0][:]), func=AF.Copy)
                                S.op("dve", "tensor_copy", [bQ[1]], [Nn], out=Nn[:], in_=v3(bQ[1][:]))
                                st_["Mp"], st_["Np"], st_["Mr"], st_["Nr"] = Mn[:], Nn[:], [Mn], [Nn]
                            return f

                        def t_lvB(lv):
                            def f():
                                Nn = Nt[lv % 2]
                                for h4 in range(4):
                                    S.op("pe", "matmul", [Nn, sb], [bR], out=bR[:, h4 * 128:(h4 + 1) * 128],
                                         lhsT=Nn[:, h4, :], rhs=sb[:, h4, :], start=(h4 == 0), stop=True,
                                         skip_group_check=True)
                                S.op("dve", "tensor_tensor", [sb, bR], [sb], out=sb[:], in0=sb[:], in1=v3(bR[:]),
                                     op=ALU.add)
                            return f

                        for lv in range(1, 7):
                            th.append(t_lvA(lv))
                            th.append(t_lvB(lv))
                        return th

                    def chain(bi, blk):
                        th = []
                        bc = slice(blk * 128, (blk + 1) * 128)
                        kz, w4, bbm, sb = krz[bi % 2], W4[bi % 2], Bbm[bi % 2], Sb[bi % 2]
                        gG = bG[:, 0:256].rearrange("p (a b) -> p a b", b=64)
                        gP = bG[:, 256:512].rearrange("p (a b) -> p a b", b=64)
                        yv = bY[:, 0:256].rearrange("p (a b) -> p a b", b=128)
                        hv = bH[:, 0:128].rearrange("p (a b) -> p a b", b=64)

                        def t_g():
                            for h4 in range(4):
                                hl = h4 // 2
                                S.op("pe", "matmul", [(kz, h4), Hb], [bG], out=gG[:, h4, :], lhsT=kz[:, h4, 0:128],
                                     rhs=Hb[:, hl, :], start=(h4 == 0), stop=False, skip_group_check=True)
                                S.op("pe", "matmul", [(w4, h4), vt], [bG], out=gG[:, h4, :], lhsT=w4[:, h4, 2, :],
                                     rhs=vt[:, blk, h4 * 64:(h4 + 1) * 64], start=False, stop=True, skip_group_check=True)
                            S.op("act", "activation", [bG], [Gs], out=Gs[:], in_=gG, func=AF.Copy)

                        def t_p():
                            for h4 in range(4):
                                S.op("pe", "matmul", [sb, Gs], [bG], out=gP[:, h4, :], lhsT=sb[:, h4, :], rhs=Gs[:, h4, :],
                                     start=False, stop=True, skip_group_check=True)
                            S.op("act", "activation", [bG], [Ps], out=Ps[:], in_=gP, func=AF.Copy, scale=-1.0)

                        def t_h():
                            for h4 in range(4):
                                hl, par = h4 // 2, h4 % 2
                                po = slice(par * 64, (par + 1) * 64)
                                vh = vt[:, blk, h4 * 64:(h4 + 1) * 64]
                                S.op("pe", "matmul", [(tl[1], hl), vt], [bH], out=hv[po, hl, :],
                                     lhsT=tl[1][:, blk, h4 * 64:(h4 + 1) * 64], rhs=vh, start=(h4 < 2), stop=False,
                                     skip_group_check=True)
                                S.op("pe", "matmul", [(tl[0], hl), Ps], [bH], out=hv[po, hl, :],
                                     lhsT=tl[0][:, blk, h4 * 64:(h4 + 1) * 64], rhs=Ps[:, h4, :], start=False, stop=True,
                                     skip_group_check=True)

                        def t_y():
                            for h4 in range(4):
                                hl, par = h4 // 2, h4 % 2
                                po = slice(par * 64, (par + 1) * 64)
                                vh = vt[:, blk, h4 * 64:(h4 + 1) * 64]
                                S.op("pe", "matmul", [Hb, (kz, h4)], [bY], out=yv[po, hl, :], lhsT=Hb[:, hl, :],
                                     rhs=kz[:, h4, 128:256], start=(h4 < 2), stop=False, skip_group_check=True)
                                S.op("pe", "matmul", [vt, (w4, h4)], [bY], out=yv[po, hl, :], lhsT=vh, rhs=w4[:, h4, 3, :],
                                     start=False, stop=False, skip_group_check=True)
                                S.op("pe", "matmul", [Ps, bbm], [bY], out=yv[po, hl, :], lhsT=Ps[:, h4, :],
                                     rhs=bbm[:, h4, :], start=False, stop=True, skip_group_check=True)

                        def t_upd():
                            S.op("pool", "tensor_tensor", [Hf, gam], [Ht], out=Ht[:], in0=Hf[:],
                                 in1=gam[:, :, blk:blk + 1].to_broadcast([128, 2, 64]), op=ALU.mult)
                            S.op("dve", "tensor_tensor", [Ht, bH], [Hf], out=Hf[:], in0=Ht[:], in1=hv, op=ALU.add)
                            S.op("act", "activation", [Hf], [Hb], out=Hb[:], in_=Hf[:], func=AF.Copy)
                            if d == 0:
                                S.op("act", "activation", [bY], [yacc], out=yacc[:, :, bc], in_=yv, func=AF.Copy)
                            else:
                                S.op("dve", "tensor_tensor", [bY, yacc], [yacc], out=yacc[:, :, bc], in0=yv,
                                     in1=yacc[:, :, bc], op=ALU.add)

                        return [t_g, t_p, t_y, t_h, t_upd]

                    for f in pre_inv(0, blocks[0]):
                        f()
                    for bi, blk in enumerate(blocks):
                        ch = chain(bi, blk)
                        nx = pre_inv(bi + 1, blocks[bi + 1]) if bi + 1 < NB else []
                        pos = {1: 0, 5: 1, 9: 2, 11: 3, 13: 4}
                        for i, f in enumerate(nx):
                            f()
                            if i in pos:
                                ch[pos[i]]()
                        if not nx:
                            for f in ch:
                                f()
                if RW_STAGE < 3:
                    continue
                k = 0
                for hl in range(2):
                    hp = grp * 2 + hl
                    YC, SQ, RS = X[0], X[1], X[2]
                    for q in range(4):
                        qc = slice(q * 512, (q + 1) * 512)
                        S.op("pe", "matmul", [C.onesblk, yacc], [bG], out=bG[:], lhsT=C.onesblk[:], rhs=yacc[:, hl, qc],
                             start=True, stop=True)
                        S.op("dve", "scalar_tensor_tensor", [bG, yacc], [YC], out=YC[:, 0:512], in0=bG[:], scalar=-1.0 / 64,
                             in1=yacc[:, hl, qc], op0=ALU.mult, op1=ALU.add)
                        S.op("act", "activation", [YC], [SQ], out=SQ[:, 0:512], in_=YC[:, 0:512], func=AF.Square)
                        S.op("pe", "matmul", [C.onesblk, SQ], [bY], out=bY[:], lhsT=C.onesblk[:], rhs=SQ[:, 0:512],
                             start=True, stop=True)
                        S.op("act", "activation", [bY, lnb_eps], [RS], out=RS[:, 0:512], in_=bY[:], func=AF.Sqrt,
                             scale=1.0 / 64, bias=lnb_eps[:, 0:1])
                        S.op("dve", "reciprocal", [RS], [RS], out=RS[:, 0:512], in_=RS[:, 0:512])
                        S.op("dve", "tensor_tensor", [YC, RS], [YC], out=YC[:, 0:512], in0=YC[:, 0:512], in1=RS[:, 0:512],
                             op=ALU.mult)
                        S.op("dve", "tensor_scalar", [YC, rvec], [YC], out=YC[:, 0:512], in0=YC[:, 0:512],
                             scalar1=rv("ln_g", hp), scalar2=rv("ln_b", hp), op0=ALU.mult, op1=ALU.add)
                        S.op("pool", "tensor_tensor", [YC, (bonus, hl)], [YC], out=YC[:, 0:512], in0=YC[:, 0:512],
                             in1=bonus[:, hl, qc], op=ALU.add)
                        S.op("pe", "matmul", [g2, sgd], [bH], out=bH[:], lhsT=g2[:, hp * 128:(hp + 1) * 128], rhs=sgd[:, qc],
                             start=True, stop=True)
                        ys = yst[k % 2]
                        k += 1
                        S.op("dve", "tensor_tensor", [YC, bH], [ys], out=ys[:], in0=YC[:, 0:512], in1=bH[:], op=ALU.mult)
                        S.dma("sp", yT.ap()[1024 + hp * 128:1024 + (hp + 1) * 128, tb + q * 512:tb + (q + 1) * 512], ys[:],
                              reads=[ys], writes=[(yT, "r%d_%d_%d" % (b, hp, q))])


NCORES = 8
DEPTH = 2
_bf = ml_dtypes.bfloat16

CONST_SPECS = (("ident_bf", [128, 128], BF16), ("rmask", [128, 2048], F32), ("mask_f", [128, 128], F32),
               ("mask_b", [128, 128], F32), ("rmask128", [128, 1024], F32), ("onesblk", [128, 128], F32),
               ("pmask", [128, 2], F32), ("ident_f", [128, 128], F32), ("m4_f", [128, 4, 128], F32),
               ("m4_b", [128, 4, 128], F32), ("mB_f", [128, 128], F32), ("mB_b", [128, 128], F32),
               ("dftS", [2, 2048, 2048], BF16), ("dftC", [128, 256], BF16))
W_SPECS = (("ffn1_gate", [DEPTH, D, DFF]), ("ffn1_up", [DEPTH, D, DFF]), ("ffn1_down", [DEPTH, DFF, D]),
           ("w_in", [DEPTH, D, 6912]), ("gla_up_f", [DEPTH, 16, 256]), ("gla_up_b", [DEPTH, 16, 256]),
           ("rwkv_w2_f", [DEPTH, 32, 512]), ("rwkv_w2_b", [DEPTH, 32, 512]), ("rwkv_a2_f", [DEPTH, 32, 512]),
           ("rwkv_a2_b", [DEPTH, 32, 512]), ("rwkv_g2", [DEPTH, 96, 512]), ("proj_gla", [DEPTH, 512, D]),
           ("proj_fnet", [DEPTH, 512, D]), ("proj_rwkv", [DEPTH, 512, D]), ("w_out", [DEPTH, D, D]),
           ("ffn2_gate", [DEPTH, D, DFF]), ("ffn2_up", [DEPTH, D, DFF]), ("ffn2_down", [DEPTH, DFF, D]),
           ("gcols", [DEPTH, 3, 128, 8]), ("mub", [DEPTH, 128, 1792]), ("gvec", [DEPTH, 128, 8]),
           ("rvec", [DEPTH, 128, 36]), ("grow", [128, D]))


def host_constants():
    ii = np.arange(128)
    same = (ii[:, None] // 64) == (ii[None, :] // 64)
    LT = (ii[:, None] < ii[None, :]).astype(np.float32)
    GT = (ii[:, None] > ii[None, :]).astype(np.float32)
    LE = (ii[:, None] <= ii[None, :]).astype(np.float32)
    rm = np.ones((128, 2048), np.float32)
    rm[:, ::64] = 0
    rm128 = np.ones((128, 1024), np.float32)
    rm128[:, ::128] = 0
    ob = np.zeros((128, 128), np.float32)
    ob[:64, :64] = 1
    ob[64:, 64:] = 1
    pm = np.zeros((128, 2), np.float32)
    pm[:64, 0] = 1
    pm[64:, 1] = 1
    s = np.arange(2048)
    ang = 2 * np.pi * ((np.outer(s, s) % 2048).astype(np.float64)) / 2048
    c = np.arange(128)
    angc = 2 * np.pi * ((np.outer(c, c) % 128).astype(np.float64)) / 128
    sc = 1.0 / np.sqrt(2048 * 128)
    return {
        "ident_bf": np.eye(128, dtype=_bf), "rmask": rm,
        "mask_f": (same & (ii[:, None] <= ii[None, :])).astype(np.float32),
        "mask_b": (same & (ii[:, None] > ii[None, :])).astype(np.float32),
        "rmask128": rm128, "onesblk": ob, "pmask": pm, "ident_f": np.eye(128, dtype=np.float32),
        "m4_f": np.ascontiguousarray(np.stack([-LT, -GT, LT, LE], axis=1)),
        "m4_b": np.ascontiguousarray(np.stack([-GT, -LT, GT, GT], axis=1)),
        "mB_f": LE, "mB_b": GT,
        "dftS": np.stack([np.cos(ang), -np.sin(ang)]).astype(_bf),
        "dftC": (np.concatenate([np.cos(angc), np.sin(angc)], axis=1) * sc).astype(_bf),
    }


def host_layouts(inp):
    L = DEPTH
    f32 = np.float32
    gcols = np.zeros((L, 3, 128, 8), f32)
    mub = np.zeros((L, 128, 1792), f32)
    gvec = np.zeros((L, 128, 8), f32)
    rvec = np.zeros((L, 128, 36), f32)
    for l in range(L):
        for i, nm in enumerate(("ffn1_norm", "mix_norm", "ffn2_norm")):
            gcols[l, i] = np.asarray(inp[nm][l], f32).reshape(8, 128).T
        mu = np.asarray(inp["rwkv_mu"][l], f32)
        muz = np.zeros(1792, f32)
        muz[0:512] = mu[0:512]
        muz[512:1024] = mu[512:1024]
        muz[1024:1152] = mu[1536:1664]
        muz[1152:1248] = mu[1664:1760]
        muz[1280:1792] = mu[1024:1536]
        mub[l] = np.broadcast_to(muz, (128, 1792))
        gvec[l, :, 0:2] = np.asarray(inp["gla_bias_f"][l], f32).reshape(2, 128).T
        gvec[l, :, 2:4] = np.asarray(inp["gla_bias_b"][l], f32).reshape(2, 128).T
        gvec[l, :, 4:8] = np.asarray(inp["gla_norm"][l], f32).reshape(4, 128).T
        for i, nm in enumerate(("rwkv_w0_f", "rwkv_w0_b", "rwkv_a0_f", "rwkv_a0_b", "rwkv_k_k", "rwkv_k_a", "rwkv_r_k",
                                "rwkv_ln_g", "rwkv_ln_b")):
            rvec[l, :, i * 4:(i + 1) * 4] = np.asarray(inp[nm][l], f32).reshape(4, 128).T
    grow = np.ascontiguousarray(np.broadcast_to(np.asarray(inp["final_norm"], f32), (128, D)))
    return {"gcols": gcols, "mub": mub, "gvec": gvec, "rvec": rvec, "grow": grow}


def build_program():
    nc = bass.Bass("TRN2", target_bir_lowering=False)
    x = nc.dram_tensor("x", [T, D], F32, kind="ExternalInput")
    out = nc.dram_tensor("out", [T, D], F32, kind="ExternalOutput")
    cn = {}
    for name, shp, dt in CONST_SPECS:
        cn[name] = nc.dram_tensor(name, shp, dt, kind="ExternalInput").ap()
    W = {}
    for name, shp in W_SPECS:
        W[name] = nc.dram_tensor(name, shp, F32, kind="ExternalInput").ap()
    xs = [Buf(nc.dram_tensor("xres%d" % i, [T, D], F32), "xres%d" % i) for i in range(3)]
    import os
    dk = {"kind": "ExternalOutput"} if os.environ.get("K_DUMP") else {}
    zT = Buf(nc.dram_tensor("zT", [ZROWS, T], BF16, **dk), "zT")
    zv = Buf(nc.dram_tensor("zv", [T, 1024], BF16, **dk), "zv")
    yT = Buf(nc.dram_tensor("yT", [1536, T], BF16, **dk), "yT")
    xin = Buf(x, "x")
    outb = Buf(out, "out")
    with contextlib.ExitStack() as st:
        S = Sched(nc, st)
        C = Ctx()
        setup_consts(S, C, cn)
        cur = xin
        import os
        NPH = int(os.environ.get("K_NPH", "99"))
        for l in range(DEPTH):
            if l * 7 + 0 >= NPH:
                break
            ffn_phase(S, C, cur, xs[0], W["gcols"][l, 0], W["ffn1_gate"][l], W["ffn1_up"][l], W["ffn1_down"][l])
            if l * 7 + 1 >= NPH:
                break
            mixin_phase(S, C, xs[0], zT, zv, W["gcols"][l, 1], W["w_in"][l], W["mub"][l])
            if l * 7 + 2 >= NPH:
                break
            gla_phase(S, C, zT, zv, yT, {"up_f": W["gla_up_f"][l], "up_b": W["gla_up_b"][l], "gvec": W["gvec"][l]})
            if l * 7 + 3 >= NPH:
                break
            fnet_phase(S, C, zT, yT, cn)
            if l * 7 + 4 >= NPH:
                break
            if os.environ.get("K_SKIP_R0") and l == 0:
                pass
            else:
              rwkv_phase(S, C, zT, zv, yT, {"w2_f": W["rwkv_w2_f"][l], "w2_b": W["rwkv_w2_b"][l],
                                          "a2_f": W["rwkv_a2_f"][l], "a2_b": W["rwkv_a2_b"][l],
                                          "g2": W["rwkv_g2"][l], "rvec": W["rvec"][l]})
            if l * 7 + 5 >= NPH:
                break
            merge_phase(S, C, xs[0], xs[1], yT, W["gcols"][l, 1], W["w_in"][l],
                        [W["proj_gla"][l], W["proj_fnet"][l], W["proj_rwkv"][l]], W["w_out"][l])
            if l * 7 + 6 >= NPH:
                break
            ffn_phase(S, C, xs[1], xs[2], W["gcols"][l, 2], W["ffn2_gate"][l], W["ffn2_up"][l], W["ffn2_down"][l])
            cur = xs[2]
        final_norm_phase(S, C, cur, outb, W["grow"])
        S.emit()
    return nc


_CACHE = {}


def kernel(**inputs):
    inp = {k: np.asarray(v) for k, v in inputs.items()}
    if "nc" not in _CACHE:
        _CACHE["nc"] = build_program()
        _CACHE["consts"] = host_constants()
    nc = _CACHE["nc"]
    shared = dict(_CACHE["consts"])
    shared.update(host_layouts(inp))
    for name, _ in W_SPECS:
        if name not in shared:
            shared[name] = np.ascontiguousarray(inp[name], dtype=np.float32)
    x = np.ascontiguousarray(inp["x"], dtype=np.float32).reshape(NCORES, T, D)
    in_maps = []
    for c in range(NCORES):
        m = dict(shared)
        m["x"] = x[c]
        in_maps.append(m)
    res = run_bass_kernel_spmd(nc, in_maps, core_ids=list(range(NCORES)))
    out = np.stack([np.asarray(r["out"]) for r in res.results], axis=0)
    return out.reshape(16, 2048, D).astype(np.float32)
```

```python
import contextlib
import numpy as np
import ml_dtypes
import concourse.bass as bass
import concourse.mybir as mybir
from concourse.bass_utils import run_bass_kernel_spmd

F32 = mybir.dt.float32
BF16 = mybir.dt.bfloat16
AF = mybir.ActivationFunctionType
ALU = mybir.AluOpType
AX = mybir.AxisListType

ENGS = ("pe", "act", "dve", "pool", "sp")
N_DMA_SEMS = 24


class Buf:
    def __init__(self, t, name):
        self.t = t
        self.name = name
        self.st = {}
        self.is_psum = False

    def _parts(self, part):
        if part is None:
            return list(self.st.keys()) or [None]
        ks = [part]
        if None in self.st:
            ks.append(None)
        return ks

    def deps_for_read(self, part):
        out = set()
        for k in self._parts(part):
            s = self.st.get(k)
            if s and s[0] is not None:
                out.add(s[0])
            if s and self.is_psum:
                out.update(s[1])
        return out

    def deps_for_write(self, part):
        out = set()
        for k in self._parts(part):
            s = self.st.get(k)
            if s:
                if s[0] is not None:
                    out.add(s[0])
                out.update(s[1])
        return out

    def note_read(self, part, ev):
        s = self.st.setdefault(part, [None, []])
        s[1].append(ev)
        if len(s[1]) > 12:
            best = {}
            for (sm, v) in s[1]:
                if best.get(sm, -1) < v:
                    best[sm] = v
            s[1] = [(sm, v) for sm, v in best.items()]

    def note_write(self, part, ev):
        if part is None:
            self.st = {None: [ev, []]}
        else:
            self.st[part] = [ev, []]

    def __getitem__(self, idx):
        return self.t[idx]

    def ap(self):
        return self.t.ap() if hasattr(self.t, "ap") and callable(self.t.ap) else self.t


class Sched:
    def __init__(self, nc, stack):
        self.nc = nc
        self.stack = stack
        self.eng = {"pe": nc.tensor, "act": nc.scalar, "dve": nc.vector,
                    "pool": nc.gpsimd, "sp": nc.sync}
        self.prog = {e: [] for e in ENGS}
        self.cnt = {e: 0 for e in ENGS}
        self.sems = {}
        for e in ENGS:
            self.sems["E" + e] = stack.enter_context(nc.semaphore("sem_" + e))
        self.dma_sems = []
        for i in range(N_DMA_SEMS):
            k = "D%d" % i
            self.sems[k] = stack.enter_context(nc.semaphore("sem_d%d" % i))
            self.dma_sems.append([k, 0])
        self.dma_rrs = {}
        self.waited = {e: {} for e in ENGS}
        self.nwaits = 0
        self.out_events = []

    def _nm(self, name):
        self.uid = getattr(self, "uid", 0) + 1
        return "%s_u%d" % (name, self.uid)

    def sbuf(self, name, shape, dt):
        name = self._nm(name)
        t = self.stack.enter_context(self.nc.sbuf_tensor(name, list(shape), dt))
        return Buf(t, name)

    def psum(self, name, shape, dt=F32):
        name = self._nm(name)
        t = self.stack.enter_context(self.nc.psum_tensor(name, list(shape), dt))
        b = Buf(t, name)
        b.is_psum = True
        return b

    def dram(self, name, shape, dt, kind="Internal"):
        t = self.nc.dram_tensor(name, list(shape), dt, kind=kind)
        return Buf(t, name)

    def _wait(self, e, ev):
        sm, v = ev
        if self.waited[e].get(sm, -1) >= v:
            return
        if sm == "Epe" and e == "pe":
            return
        self.waited[e][sm] = v
        self.prog[e].append(("w", sm, v))
        self.nwaits += 1

    @staticmethod
    def _norm(lst):
        out = []
        for x in lst:
            if isinstance(x, Buf):
                out.append((x, None))
            else:
                out.append(x)
        return out

    def op(self, e, fn, reads=(), writes=(), **kw):
        if isinstance(fn, str):
            name = fn

            def fn(eng, name=name, kw=kw):
                return getattr(eng, name)(**kw)
        reads = self._norm(reads)
        writes = self._norm(writes)
        deps = set()
        for b, p in reads:
            deps |= b.deps_for_read(p)
        for b, p in writes:
            deps |= b.deps_for_write(p)
        for ev in sorted(deps):
            self._wait(e, ev)
        self.cnt[e] += 1
        ev = ("E" + e, self.cnt[e])
        self.prog[e].append(("o", fn, "E" + e, 1))
        for b, p in reads:
            b.note_read(p, ev)
        for b, p in writes:
            b.note_write(p, ev)
        return ev

    def dma(self, e, out_ap, in_ap, reads=(), writes=(), **kw):
        reads = self._norm(reads)
        writes = self._norm(writes)
        deps = set()
        for b, p in reads:
            deps |= b.deps_for_read(p)
        for b, p in writes:
            deps |= b.deps_for_write(p)
        half = N_DMA_SEMS // 2
        base = 0 if e == "pool" else half
        rr = self.dma_rrs.get(e == "pool", 0)
        slot = self.dma_sems[base + rr]
        self.dma_rrs[e == "pool"] = (rr + 1) % half
        if slot[1] > 0:
            deps.add((slot[0], slot[1]))
        for ev in sorted(deps):
            self._wait(e, ev)
        slot[1] += 16
        ev = (slot[0], slot[1])

        def fn(eng, out_ap=out_ap, in_ap=in_ap, kw=kw):
            return eng.dma_start(out=out_ap, in_=in_ap, **kw)

        self.prog[e].append(("o", fn, slot[0], 16))
        for b, p in reads:
            b.note_read(p, ev)
        for b, p in writes:
            b.note_write(p, ev)
        return ev

    def wait_all(self, e, events):
        for ev in events:
            self._wait(e, ev)

    def emit(self):
        nc = self.nc
        with nc.Block() as block:
            def mk(e):
                def body(eng):
                    for it in self.prog[e]:
                        if it[0] == "w":
                            eng.wait_ge(self.sems[it[1]], it[2])
                        else:
                            ins = it[1](eng)
                            ins.then_inc(self.sems[it[2]], it[3])
                return body
            block.tensor(mk("pe"))
            block.scalar(mk("act"))
            block.vector(mk("dve"))
            block.gpsimd(mk("pool"))
            block.sync(mk("sp"))


def barrier(S):
    evs = [("E" + e, S.cnt[e]) for e in ENGS if S.cnt[e] > 0]
    evs += [(k, v) for k, v in S.dma_sems if v > 0]
    for e in ENGS:
        for ev in evs:
            S._wait(e, ev)


Sched.barrier = barrier


@contextlib.contextmanager
def _phase(S):
    S.barrier()
    old = S.stack
    with contextlib.ExitStack() as ps:
        S.stack = ps
        try:
            yield
        finally:
            S.barrier()
            S.stack = old


Sched.phase = _phase


def setup_consts(S, C, consts):
    C.rr = 0
    C.rt = 0
    C.ident = S.sbuf("ident", [128, 128], BF16)
    C.ones_f = S.sbuf("ones_f", [128, 512], F32)
    C.mhalf = S.sbuf("mhalf", [128, 1], F32)
    C.junk = S.sbuf("junk", [128, 1024], BF16)
    C.ss = [S.sbuf("ss%d" % i, [128, 1], F32) for i in range(4)]
    C.rstd = [S.sbuf("rstd%d" % i, [128, 1], F32) for i in range(4)]
    C.xs = [S.sbuf("xs%d" % i, [128, 1024], BF16) for i in range(2)]
    C.psT = [S.psum("psT%d" % i, [128, 8, 128], BF16) for i in range(2)]
    C.gB = [S.sbuf("gB%d" % i, [128, 8, 128], F32) for i in range(1)]
    C.gcol = S.sbuf("gcol", [128, 8], F32)
    S.dma("sp", C.ident[:], consts["ident_bf"], writes=[C.ident])
    C.one_c = S.sbuf("one_c", [128, 1], F32)
    C.eps_c = S.sbuf("eps_c", [128, 1], F32)
    S.op("dve", "memset", [], [C.one_c], ap=C.one_c[:], constant=1.0)
    S.op("dve", "memset", [], [C.eps_c], ap=C.eps_c[:], constant=1e-6)
    C.consts = consts

    S.op("dve", lambda e: e.memset(C.ones_f[:], 1.0), writes=[C.ones_f])
    S.op("dve", lambda e: e.memset(C.mhalf[:], -0.5), writes=[C.mhalf])


def load_rwkv_consts(S, C):
    consts = C.consts
    C.rmask128 = S.sbuf("rmask128", [128, 1024], F32)
    C.onesblk = S.sbuf("onesblk", [128, 128], F32)
    C.pmask = S.sbuf("pmask", [128, 2], F32)
    C.ident_f = S.sbuf("ident_f", [128, 1, 128], F32)
    C.m4_f = S.sbuf("m4_f", [128, 4, 128], F32)
    C.m4_b = S.sbuf("m4_b", [128, 4, 128], F32)
    C.mB_f = S.sbuf("mB_f", [128, 1, 128], F32)
    C.mB_b = S.sbuf("mB_b", [128, 1, 128], F32)
    S.dma("sp", C.rmask128[:], consts["rmask128"], writes=[C.rmask128])
    S.dma("sp", C.onesblk[:], consts["onesblk"], writes=[C.onesblk])
    S.dma("sp", C.pmask[:], consts["pmask"], writes=[C.pmask])
    S.dma("sp", C.ident_f[:, 0, :], consts["ident_f"], writes=[C.ident_f])
    S.dma("sp", C.m4_f[:], consts["m4_f"], writes=[C.m4_f])
    S.dma("sp", C.m4_b[:], consts["m4_b"], writes=[C.m4_b])
    S.dma("sp", C.mB_f[:, 0, :], consts["mB_f"], writes=[C.mB_f])
    S.dma("sp", C.mB_b[:, 0, :], consts["mB_b"], writes=[C.mB_b])


def load_gla_consts(S, C):
    consts = C.consts
    C.rmask = S.sbuf("rmask", [128, 2048], F32)
    C.mask_f = S.sbuf("mask_f", [128, 1, 128], F32)
    C.mask_b = S.sbuf("mask_b", [128, 1, 128], F32)
    S.dma("sp", C.rmask[:], consts["rmask"], writes=[C.rmask])
    S.dma("sp", C.mask_f[:, 0, :], consts["mask_f"], writes=[C.mask_f])
    S.dma("sp", C.mask_b[:, 0, :], consts["mask_b"], writes=[C.mask_b])


D = 1024
DFF = 2816
NJ = DFF // 128
T = 4096
SEQ = 2048
TT = 256
EPS = 1e-6

ZR_RR, ZR_RK, ZR_LORA, ZR_GD = 0, 512, 1024, 1152
ZR_GQ, ZR_GK, ZR_GR, ZR_DN, ZR_FN = 1280, 1536, 1792, 2304, 2432
ZROWS = 2944
NZC = ZROWS // 128
RW0 = 2080
WZ_SEGS = [(RW0, 512, ZR_RR), (RW0 + 512, 512, ZR_RK), (RW0 + 1536, 128, ZR_LORA), (RW0 + 1664, 96, ZR_GD),
           (0, 256, ZR_GQ), (256, 256, ZR_GK), (1024, 512, ZR_GR), (1536, 32, ZR_DN), (1568, 512, ZR_FN)]
WV_SEGS = [(512, 512, 0), (RW0 + 1024, 512, 512)]


class Ctx:
    pass


def load_w_cast(S, dst, part, dst_ap, src_ap, inner):
    S.dma("pool", dst_ap.rearrange("p (a b) -> p a b", b=inner),
          src_ap.rearrange("p (a b) -> p a b", b=inner), writes=[(dst, part)])


def load_x(S, xd, t0, xt):
    S.dma("sp", xt[:], xd.ap()[t0:t0 + TT, :].rearrange("(s p) d -> p s d", p=128),
          reads=[xd], writes=[xt])


def norm_stats(S, C, xt):
    outs = []
    for s in range(TT // 128):
        ss = C.ss[C.rr % 4]
        rstd = C.rstd[C.rr % 4]
        xs = C.xs[C.rr % 2]
        C.rr += 1
        S.op("act", "activation", [xt], [C.junk, ss], out=C.junk[:], in_=xt[:, s, :], func=AF.Square,
             accum_out=ss[:])
        S.op("dve", "tensor_scalar", [ss], [rstd], out=rstd[:], in0=ss[:], scalar1=1.0 / D, scalar2=EPS,
             op0=ALU.mult, op1=ALU.add)
        S.op("pool", "tensor_tensor", [rstd, C.mhalf], [rstd], out=rstd[:], in0=rstd[:], in1=C.mhalf[:],
             op=ALU.pow)
        S.op("act", "activation", [xt, rstd], [xs], out=xs[:], in_=xt[:, s, :], func=AF.Copy,
             scale=rstd[:, 0:1])
        outs.append(xs)
    return outs


def norm_transpose(S, C, xss, g_idx, hT, col0):
    for s, xs in enumerate(xss):
        psT = C.psT[C.rt % 2]
        C.rt += 1
        for c in range(8):
            S.op("pe", "transpose", [xs, C.ident], [psT], out=psT[:, c, :], in_=xs[:, c * 128:(c + 1) * 128],
                 identity=C.ident[:])
        S.op("dve", "tensor_tensor", [psT, C.gB[g_idx]], [hT],
             out=hT[:, :, col0 + s * 128:col0 + (s + 1) * 128], in0=psT[:], in1=C.gB[g_idx][:], op=ALU.mult)


def norm_tile(S, C, xd, t0, g_idx, xt, hT, col0):
    if xd is not None:
        load_x(S, xd, t0, xt)
    norm_transpose(S, C, norm_stats(S, C, xt), g_idx, hT, col0)


def load_gain(S, C, g_idx, gcol_ap):
    S.dma("sp", C.gcol[:], gcol_ap, writes=[C.gcol])
    for c in range(8):
        S.op("dve", "tensor_scalar", [C.gcol, C.ones_f], [C.gB[g_idx]], out=C.gB[g_idx][:, c, :],
             in0=C.ones_f[:, 0:128], scalar1=C.gcol[:, c:c + 1], scalar2=None, op0=ALU.mult)


def ffn_phase(S, C, xin, xout, gcol_ap, wg_ap, wu_ap, wd_ap, ntiles=T // TT):
    with S.phase():
        wg = S.sbuf("wg", [128, 8, DFF], BF16)
        wu = S.sbuf("wu", [128, 8, DFF], BF16)
        wd = S.sbuf("wd", [128, NJ, D], BF16)
        xt = [S.sbuf("xt%d" % i, [128, TT // 128, D], F32) for i in range(2)]
        ot = [S.sbuf("ot%d" % i, [128, TT // 128, D], F32) for i in range(2)]
        hT = [S.sbuf("hT%d" % i, [128, 8, TT], BF16) for i in range(2)]
        sg = [S.sbuf("sg%d" % i, [128, TT], BF16) for i in range(2)]
        act = [S.sbuf("act%d" % i, [128, TT], BF16) for i in range(3)]
        pgu = [S.psum("pgu%d" % i, [128, 2, TT], F32) for i in range(2)]
        pacc = [S.psum("pacc%d" % i, [128, 512], F32) for i in range(4)]
        load_gain(S, C, 0, gcol_ap)
        for cg in range(4):
            for w, w_ap in ((wg, wg_ap), (wu, wu_ap)):
                S.dma("pool", w[:, :, cg * 704:(cg + 1) * 704],
                      w_ap[:, cg * 704:(cg + 1) * 704].rearrange("(c p) n -> p c n", p=128), writes=[(w, cg)])
            for j in range(cg * 6, min(NJ, cg * 6 + 6)):
                load_w_cast(S, wd, j, wd[:, j, :], wd_ap[j * 128:(j + 1) * 128, :], 1024)
        nsub = TT // 128
        load_x(S, xin, 0, xt[0])
        norm_tile(S, C, None, 0, 0, xt[0], hT[0], 0)
        for it in range(ntiles):
            t0 = it * TT
            x_t = xt[it % 2]
            h_t = hT[it % 2]
            o_t = ot[it % 2]
            if it + 1 < ntiles:
                load_x(S, xin, t0 + TT, xt[(it + 1) % 2])
            nxt = {}

            def gu(j):
                pg = pgu[j % 2]
                for (w, gi) in ((wg, 0), (wu, 1)):
                    for c in range(8):
                        S.op("pe", "matmul", [(w, (j * 128) // 704), (w, (j * 128 + 127) // 704), h_t], [pg], out=pg[:, gi, :],
                             lhsT=w[:, c, j * 128:(j + 1) * 128], rhs=h_t[:, c, :], start=(c == 0), stop=(c == 7))
                sgj = sg[j % 2]
                aj = act[j % 3]
                S.op("act", "activation", [pg], [sgj], out=sgj[:], in_=pg[:, 0, :], func=AF.Silu)
                S.op("dve", "tensor_tensor", [pg, sgj], [aj], out=aj[:], in0=pg[:, 1, :], in1=sgj[:], op=ALU.mult)

            def down(j):
                aj = act[j % 3]
                for s in range(nsub):
                    for hf in range(2):
                        pa = pacc[s * 2 + hf]
                        S.op("pe", "matmul", [aj, (wd, j)], [pa], out=pa[:], lhsT=aj[:, s * 128:(s + 1) * 128],
                             rhs=wd[:, j, hf * 512:(hf + 1) * 512], start=(j == 0), stop=(j == NJ - 1))

            gu(0)
            for j in range(NJ):
                if j + 1 < NJ:
                    gu(j + 1)
                down(j)
                if it + 1 < ntiles:
                    if j == 3:
                        nxt["xs"] = norm_stats(S, C, xt[(it + 1) % 2])
                    if j == 12:
                        norm_transpose(S, C, nxt["xs"], 0, hT[(it + 1) % 2], 0)
            for s in range(nsub):
                for hf in range(2):
                    pa = pacc[s * 2 + hf]
                    S.op("dve", "scalar_tensor_tensor", [pa, x_t], [o_t], out=o_t[:, s, hf * 512:(hf + 1) * 512],
                         in0=pa[:], scalar=0.5, in1=x_t[:, s, hf * 512:(hf + 1) * 512], op0=ALU.mult, op1=ALU.add)
            S.dma("sp", xout.ap()[t0:t0 + TT, :].rearrange("(s p) d -> p s d", p=128), o_t[:],
                  reads=[o_t], writes=[xout])


def mixin_phase(S, C, xin, zT, zv, gcol_ap, win_ap, mub_ap, nseq=2):
    with S.phase():
        wz = S.sbuf("wz", [128, 8, ZROWS], BF16)
        wv = S.sbuf("wv", [128, 8, 1024], BF16)
        wzb = S.sbuf("wzb", [128, 8, 1280], BF16)
        wvb = S.sbuf("wvb", [128, 8, 512], BF16)
        mub = S.sbuf("mub", [128, 1792], F32)
        hseq = S.sbuf("hseq", [128, 8, SEQ], BF16)
        hshs = [S.sbuf("hsh%d" % i, [128, 8, TT], BF16) for i in range(2)]
        xt = [S.sbuf("xt%d" % i, [128, TT // 128, D], F32) for i in range(2)]
        zst = [S.sbuf("zst%d" % i, [128, NZC, TT], BF16) for i in range(2)]
        zvst = [S.sbuf("zvst%d" % i, [128, TT // 128, 1024], BF16) for i in range(2)]
        pz = [S.psum("pz%d" % i, [128, 512], F32) for i in range(3)]
        pv = [S.psum("pv%d" % i, [128, 512], F32) for i in range(2)]
        load_gain(S, C, 0, gcol_ap)
        S.op("pool", "memset", [], [wz], ap=wz[:], constant=0.0)
        S.dma("sp", mub[:], mub_ap, writes=[mub])
        for i, (sc, n, dc) in enumerate(WZ_SEGS):
            S.dma("pool", wz[:, :, dc:dc + n], win_ap[:, sc:sc + n].rearrange("(c p) n -> p c n", p=128),
                  writes=[(wz, "s%d" % i)])
        for i, (sc, n, dc) in enumerate(WV_SEGS):
            S.dma("pool", wv[:, :, dc:dc + n], win_ap[:, sc:sc + n].rearrange("(c p) n -> p c n", p=128),
                  writes=[(wv, "s%d" % i)])
        for c in range(8):
            S.op("dve", "scalar_tensor_tensor", [wz, mub], [wzb], out=wzb[:, c, :], in0=wz[:, c, 0:1280],
                 scalar=0.5, in1=mub[:, 0:1280], op0=ALU.mult, op1=ALU.mult)
            S.op("dve", "scalar_tensor_tensor", [wv, mub], [wvb], out=wvb[:, c, :], in0=wv[:, c, 512:1024],
                 scalar=0.5, in1=mub[:, 1280:1792], op0=ALU.mult, op1=ALU.mult)
        S.op("dve", "tensor_scalar", [mub], [mub], out=mub[:], in0=mub[:], scalar1=-1.0, scalar2=1.0,
             op0=ALU.mult, op1=ALU.add)
        for c in range(8):
            S.op("pool", "tensor_tensor", [wz, mub], [wz], out=wz[:, c, 0:1280], in0=wz[:, c, 0:1280],
                 in1=mub[:, 0:1280], op=ALU.mult)
            S.op("pool", "tensor_tensor", [wv, mub], [wv], out=wv[:, c, 512:1024], in0=wv[:, c, 512:1024],
                 in1=mub[:, 1280:1792], op=ALU.mult)
        nev = 0
        for b in range(nseq):
            for it in range(SEQ // TT):
                norm_tile(S, C, xin, b * SEQ + it * TT, 0, xt[it % 2], hseq, it * TT)
            for it in range(SEQ // TT):
                c0 = it * TT
                hsh = hshs[it % 2]
                a0 = 1 if it == 0 else 0
                a1 = TT - 1 if it == SEQ // TT - 1 else TT
                S.op("pool", "tensor_tensor", [hseq], [hsh], out=hsh[:, :, a0:a1], in0=hseq[:, :, c0 + a0 - 1:c0 + a1 - 1],
                     in1=hseq[:, :, c0 + a0 + 1:c0 + a1 + 1], op=ALU.add)
                if it == 0:
                    S.op("pool", "tensor_copy", [hseq], [hsh], out=hsh[:, :, 0:1], in_=hseq[:, :, 1:2])
                if it == SEQ // TT - 1:
                    S.op("pool", "tensor_copy", [hseq], [hsh], out=hsh[:, :, TT - 1:TT], in_=hseq[:, :, SEQ - 2:SEQ - 1])
                zs = zst[it % 2]
                zvs = zvst[it % 2]
                for ck in range(NZC):
                    p = pz[ck % 3]
                    nmm = 16 if ck < 10 else 8
                    k = 0
                    for c in range(8):
                        S.op("pe", "matmul", [wz, hseq], [p], out=p[:, 0:TT], lhsT=wz[:, c, ck * 128:(ck + 1) * 128],
                             rhs=hseq[:, c, c0:c0 + TT], start=(k == 0), stop=(k == nmm - 1))
                        k += 1
                    if ck < 10:
                        for c in range(8):
                            S.op("pe", "matmul", [wzb, hsh], [p], out=p[:, 0:TT],
                                 lhsT=wzb[:, c, ck * 128:(ck + 1) * 128], rhs=hsh[:, c, :],
                                 start=False, stop=(k == nmm - 1))
                            k += 1
                    nev += 1
                    if nev % 2:
                        S.op("act", "activation", [p], [zs], out=zs[:, ck, :], in_=p[:, 0:TT], func=AF.Copy)
                    else:
                        S.op("dve", "tensor_copy", [p], [zs], out=zs[:, ck, :], in_=p[:, 0:TT])
                for s in range(TT // 128):
                    cs = c0 + s * 128
                    for vi in range(2):
                        p = pv[vi]
                        nmm = 16 if vi == 1 else 8
                        k = 0
                        for c in range(8):
                            S.op("pe", "matmul", [wv, hseq], [p], out=p[:], lhsT=hseq[:, c, cs:cs + 128],
                                 rhs=wv[:, c, vi * 512:(vi + 1) * 512], start=(k == 0), stop=(k == nmm - 1))
                            k += 1
                        if vi == 1:
                            for c in range(8):
                                S.op("pe", "matmul", [wvb, hsh], [p], out=p[:], lhsT=hsh[:, c, s * 128:(s + 1) * 128],
                                     rhs=wvb[:, c, :], start=False, stop=(k == nmm - 1))
                                k += 1
                        if vi == 0:
                            S.op("act", "activation", [p], [zvs], out=zvs[:, s, 0:512], in_=p[:], func=AF.Copy)
                        else:
                            S.op("dve", "tensor_copy", [p], [zvs], out=zvs[:, s, 512:1024], in_=p[:])
                tg = b * SEQ + c0
                S.dma("sp", zT.ap()[:, tg:tg + TT].rearrange("(c p) t -> p c t", p=128), zs[:],
                      reads=[zs], writes=[(zT, "w%d" % (tg // TT))])
                S.dma("sp", zv.ap()[tg:tg + TT, :].rearrange("(s p) d -> p s d", p=128), zvs[:],
                      reads=[zvs], writes=[(zv, "w%d" % (tg // TT))])


GLA_C = -1.0 / 16.0


def v3(ap, b=128):
    return ap.rearrange("p (a b) -> p a b", b=b)


def gla_phase(S, C, zT, zv, yT, A, nseq=2):
    with S.phase():
        load_gla_consts(S, C)
        qT = S.sbuf("qT", [128, 2, SEQ], BF16)
        kT = S.sbuf("kT", [128, 2, SEQ], BF16)
        rT = S.sbuf("rT", [128, 4, SEQ], BF16)
        vt = S.sbuf("vt", [128, SEQ // 128, 512], BF16)
        dn = [S.sbuf("dn%d" % i, [16, SEQ], BF16) for i in range(2)]
        up = [S.sbuf("up%d" % i, [16, 256], BF16) for i in range(2)]
        gvec = S.sbuf("gvec", [128, 8], F32)
        negb = S.sbuf("negb", [128, 4], F32)
        oacc = S.sbuf("oacc", [128, 4, SEQ], F32)
        qd = S.sbuf("qd", [128, 2, SEQ], BF16)
        kd = S.sbuf("kd", [128, 2, SEQ], BF16)
        ktT = S.sbuf("ktT", [128, 2, SEQ], BF16)
        kt = S.sbuf("kt", [128, SEQ // 128, 256], BF16)
        gam = S.sbuf("gam", [128, 2, 32], F32)
        L = S.sbuf("L", [128, SEQ], F32)
        cs = S.sbuf("cs", [128, SEQ], F32)
        t1 = S.sbuf("t1", [128, SEQ], F32)
        E = S.sbuf("E", [128, SEQ], F32)
        Hf = S.sbuf("Hf", [128, 2, 128], F32)
        Ht = S.sbuf("Ht", [128, 2, 128], F32)
        Hb = S.sbuf("Hb", [128, 2, 128], BF16)
        sTs = [S.sbuf("sTs%d" % i, [128, 2, 2, 128], BF16) for i in range(2)]
        sq = S.sbuf("sq", [128, 512], F32)
        rs = S.sbuf("rs", [128, 512], F32)
        sr = S.sbuf("sr", [128, 512], BF16)
        yst = [S.sbuf("yst%d" % i, [128, 512], BF16) for i in range(2)]
        bS = [S.psum("bS%d" % i, [128, 512], F32) for i in range(2)]
        bO = [S.psum("bO%d" % i, [128, 512], F32) for i in range(2)]
        bU = [S.psum("bU%d" % i, [128, 512], F32) for i in range(2)]
        psL = bU[0]
        oacc4 = oacc[:].rearrange("p (hp par) t -> p hp par t", par=2)
        S.dma("pool", up[0][:], A["up_f"], writes=[up[0]])
        S.dma("pool", up[1][:], A["up_b"], writes=[up[1]])
        S.dma("sp", gvec[:], A["gvec"], writes=[gvec])
        S.op("dve", "tensor_scalar", [gvec], [negb], out=negb[:], in0=gvec[:, 0:4], scalar1=-1.0, scalar2=None,
             op0=ALU.mult)
        nb = SEQ // 128
        for b in range(nseq):
            tb = b * SEQ
            za = zT.ap()
            S.dma("sp", qT[:], za[ZR_GQ:ZR_GQ + 256, tb:tb + SEQ].rearrange("(c p) t -> p c t", p=128),
                  reads=[zT], writes=[qT])
            S.dma("sp", kT[:], za[ZR_GK:ZR_GK + 256, tb:tb + SEQ].rearrange("(c p) t -> p c t", p=128),
                  reads=[zT], writes=[kT])
            S.dma("sp", rT[:], za[ZR_GR:ZR_GR + 512, tb:tb + SEQ].rearrange("(c p) t -> p c t", p=128),
                  reads=[zT], writes=[rT])
            S.dma("sp", dn[0][:], za[ZR_DN:ZR_DN + 16, tb:tb + SEQ], reads=[zT], writes=[dn[0]])
            S.dma("sp", dn[1][:], za[ZR_DN + 16:ZR_DN + 32, tb:tb + SEQ], reads=[zT], writes=[dn[1]])
            S.dma("sp", vt[:], zv.ap()[tb:tb + SEQ, 0:512].rearrange("(n p) d -> p n d", p=128),
                  reads=[zv], writes=[vt])
            for d in range(2):
                for hp in range(2):
                    for q4 in range(4):
                        S.op("pe", "matmul", [up[d], dn[d]], [psL], out=psL[:], lhsT=up[d][:, hp * 128:(hp + 1) * 128],
                             rhs=dn[d][:, q4 * 512:(q4 + 1) * 512], start=True, stop=True)
                        S.op("act", "activation", [psL, negb], [E], out=E[:, q4 * 512:(q4 + 1) * 512], in_=psL[:],
                             func=AF.Exp, scale=-1.0, bias=negb[:, 2 * d + hp:2 * d + hp + 1])
                    S.op("act", "activation", [E, C.one_c], [L], out=L[:], in_=E[:], func=AF.Ln, bias=C.one_c[:, 0:1])
                    S.op("dve", "tensor_tensor_scan", [C.rmask, L], [cs], out=cs[:], data0=C.rmask[:], data1=L[:],
                         initial=0.0, op0=ALU.mult, op1=ALU.add)
                    cs3 = v3(cs[:], 64)
                    tot = cs3[:, :, 63:64]
                    S.op("dve", "tensor_tensor", [cs], [t1], out=v3(t1[:], 64),
                         in0=tot.to_broadcast([128, 32, 64]), in1=cs3, op=ALU.subtract)
                    S.op("act", "activation", [cs], [gam], out=gam[:, hp, :], in_=cs3[:, :, 63], func=AF.Exp,
                         scale=GLA_C)
                    if d == 0:
                        bI, tail = cs, t1
                    else:
                        S.op("dve", "tensor_tensor", [t1, L], [t1], out=t1[:], in0=t1[:], in1=L[:], op=ALU.add)
                        S.op("dve", "tensor_tensor", [cs, L], [cs], out=cs[:], in0=cs[:], in1=L[:], op=ALU.subtract)
                        bI, tail = t1, cs
                    S.op("act", "activation", [bI], [E], out=E[:], in_=bI[:], func=AF.Exp, scale=GLA_C)
                    S.op("dve", "scalar_tensor_tensor", [qT, E], [qd], out=qd[:, hp, :], in0=qT[:, hp, :], scalar=0.125,
                         in1=E[:], op0=ALU.mult, op1=ALU.mult)
                    S.op("act", "activation", [bI], [E], out=E[:], in_=bI[:], func=AF.Exp, scale=-GLA_C)
                    S.op("dve", "tensor_tensor", [kT, E], [kd], out=kd[:, hp, :], in0=kT[:, hp, :], in1=E[:], op=ALU.mult)
                    S.op("act", "activation", [tail], [E], out=E[:], in_=tail[:], func=AF.Exp, scale=GLA_C)
                    S.op("dve", "tensor_tensor", [kT, E], [ktT], out=ktT[:, hp, :], in0=kT[:, hp, :], in1=E[:],
                         op=ALU.mult)
                for blk in range(nb):
                    psT = C.psT[C.rr % 2]
                    C.rr += 1
                    for hp in range(2):
                        S.op("pe", "transpose", [ktT, C.ident], [psT], out=psT[:, hp, :],
                             in_=ktT[:, hp, blk * 128:(blk + 1) * 128], identity=C.ident[:])
                    S.op("act", "activation", [psT], [kt], out=v3(kt[:, blk, :]), in_=psT[:, 0:2, :], func=AF.Copy)
                S.op("dve", "memset", [], [Hf], ap=Hf[:], constant=0.0)
                S.op("dve", "memset", [], [Hb], ap=Hb[:], constant=0.0)
                mask = C.mask_f if d == 0 else C.mask_b
                blocks = range(nb) if d == 0 else range(nb - 1, -1, -1)
                for bi, blk in enumerate(blocks):
                    bc = slice(blk * 128, (blk + 1) * 128)
                    bf_ = bi % 2
                    sT = sTs[bf_]
                    for h in range(4):
                        hp, par, base = h // 2, h % 2, (h % 2) * 64
                        pS = v3(bS[par][:, 0:256])
                        S.op("pe", "matmul", [kd, qd], [bS[par]], out=pS[:, hp, :], lhsT=kd[base:base + 64, hp, bc],
                             rhs=qd[base:base + 64, hp, bc], start=True, stop=True)
                    for par in range(2):
                        pS = v3(bS[par][:, 0:256])
                        S.op("dve", "tensor_tensor", [bS[par], mask], [(sT, par)], out=sT[:, par, :, :], in0=pS,
                             in1=mask[:].to_broadcast([128, 2, 128]), op=ALU.mult)
                    for h in range(4):
                        hp, par = h // 2, h % 2
                        pO = v3(bO[par][:, 0:256])
                        S.op("pe", "matmul", [vt, (sT, par)], [bO[par]], out=pO[:, hp, :],
                             lhsT=vt[:, blk, h * 128:(h + 1) * 128], rhs=sT[:, par, hp, :], start=(hp == 0), stop=False,
                             skip_group_check=True)
                    halves = (0, 1) if d == 0 else (1, 0)
                    for half in halves:
                        c = blk * 2 + half
                        cc = slice(c * 64, (c + 1) * 64)
                        pb = slice(half * 64, (half + 1) * 64)
                        pU = v3(bU[half][:, 0:256])
                        for h in range(4):
                            hp, par, base = h // 2, h % 2, (h % 2) * 64
                            pO = v3(bO[par][:, 0:256])
                            S.op("pe", "matmul", [Hb, qd], [bO[par]], out=pO[:, hp, half * 64:(half + 1) * 64],
                                 lhsT=Hb[base:base + 64, hp, :], rhs=qd[base:base + 64, hp, cc],
                                 start=False, stop=True, skip_group_check=True)
                        for h in range(4):
                            hp, base = h // 2, (h % 2) * 64
                            S.op("pe", "matmul", [kt, vt], [bU[half]], out=pU[base:base + 64, hp, :],
                                 lhsT=kt[pb, blk, h * 64:(h + 1) * 64], rhs=vt[pb, blk, h * 128:(h + 1) * 128],
                                 start=True, stop=True)
                        S.op("pool", "tensor_tensor", [Hf, gam], [Ht], out=Ht[:], in0=Hf[:],
                             in1=gam[:, :, c:c + 1].to_broadcast([128, 2, 128]), op=ALU.mult)
                        S.op("dve", "tensor_tensor", [Ht, bU[half]], [Hf], out=Hf[:], in0=Ht[:], in1=pU, op=ALU.add)
                        S.op("act", "activation", [Hf], [Hb], out=Hb[:], in_=Hf[:], func=AF.Copy)
                    for par in range(2):
                        pO = v3(bO[par][:, 0:256])
                        if d == 0:
                            S.op("act", "activation", [bO[par]], [oacc], out=oacc4[:, :, par, bc], in_=pO,
                                 func=AF.Copy)
                        else:
                            S.op("dve", "tensor_tensor", [bO[par], oacc], [oacc], out=oacc4[:, :, par, bc], in0=pO,
                                 in1=oacc4[:, :, par, bc], op=ALU.add)
            k = 0
            for h in range(4):
                for q4 in range(4):
                    c4 = slice(q4 * 512, (q4 + 1) * 512)
                    S.op("act", "activation", [oacc], [sq], out=sq[:], in_=oacc[:, h, c4], func=AF.Square)
                    S.op("pe", "matmul", [C.ones_f, sq], [psL], out=psL[:], lhsT=C.ones_f[:, 0:128], rhs=sq[:],
                         start=True, stop=True)
                    S.op("act", "activation", [psL, C.eps_c], [rs], out=rs[:], in_=psL[:], func=AF.Sqrt,
                         scale=1.0 / 128, bias=C.eps_c[:, 0:1])
                    S.op("dve", "reciprocal", [rs], [rs], out=rs[:], in_=rs[:])
                    S.op("act", "activation", [rT], [sr], out=sr[:], in_=rT[:, h, c4], func=AF.Silu)
                    S.op("dve", "scalar_tensor_tensor", [oacc, gvec, rs], [rs], out=rs[:], in0=oacc[:, h, c4],
                         scalar=gvec[:, 4 + h:5 + h], in1=rs[:], op0=ALU.mult, op1=ALU.mult)
                    ys = yst[k % 2]
                    k += 1
                    S.op("dve", "tensor_tensor", [rs, sr], [ys], out=ys[:], in0=rs[:], in1=sr[:], op=ALU.mult)
                    S.dma("sp", yT.ap()[h * 128:(h + 1) * 128, tb + q4 * 512:tb + (q4 + 1) * 512], ys[:],
                          reads=[ys], writes=[(yT, "g%d_%d_%d" % (b, h, q4))])


def fnet_phase(S, C, zT, yT, A, nseq=2):
    with S.phase():
        Ws = S.sbuf("Ws", [128, 2, 16, SEQ], BF16)
        Wc = S.sbuf("Wc", [128, 256], BF16)
        uT = S.sbuf("uT", [128, 4, SEQ], BF16)
        As = [S.sbuf("As%d" % i, [128, 16, 256], BF16) for i in range(2)]
        yst = [S.sbuf("ystf%d" % i, [128, 512], BF16) for i in range(2)]
        pA = [S.psum("pA%d" % i, [128, 512], F32) for i in range(2)]
        pY = [S.psum("pY%d" % i, [128, 512], F32) for i in range(2)]
        S.dma("sp", Wc[:], A["dftC"], writes=[Wc])
        for t in range(2):
            for hlf in range(2):
                S.dma("sp", Ws[:, t, hlf * 8:(hlf + 1) * 8, :],
                      A["dftS"][t, hlf * 1024:(hlf + 1) * 1024, :].rearrange("(n p) s -> p n s", p=128),
                      writes=[(Ws, "%d_%d" % (t, hlf))])
        k = 0
        for b in range(nseq):
            tb = b * SEQ
            S.dma("sp", uT[:], zT.ap()[ZR_FN:ZR_FN + 512, tb:tb + SEQ].rearrange("(c p) t -> p c t", p=128),
                  reads=[zT], writes=[uT])
            for g in range(4):
                a_s = As[g % 2]
                for n in range(16):
                    p = pA[n % 2]
                    S.op("pe", "matmul", [uT, Wc], [p], out=p[:, 0:256], lhsT=uT[:, g, n * 128:(n + 1) * 128], rhs=Wc[:],
                         start=True, stop=True)
                    if n % 2:
                        S.op("act", "activation", [p], [a_s], out=a_s[:, n, :], in_=p[:, 0:256], func=AF.Copy)
                    else:
                        S.op("dve", "tensor_copy", [p], [a_s], out=a_s[:, n, :], in_=p[:, 0:256])
                for q4 in range(4):
                    p = pY[q4 % 2]
                    i = 0
                    for t in range(2):
                        for n in range(16):
                            S.op("pe", "matmul", [a_s, Ws], [p], out=p[:], lhsT=a_s[:, n, t * 128:(t + 1) * 128],
                                 rhs=Ws[:, t, n, q4 * 512:(q4 + 1) * 512], start=(i == 0), stop=(i == 31))
                            i += 1
                    ys = yst[k % 2]
                    k += 1
                    S.op("act", "activation", [p], [ys], out=ys[:], in_=p[:], func=AF.Copy)
                    S.dma("sp", yT.ap()[512 + g * 128:512 + (g + 1) * 128, tb + q4 * 512:tb + (q4 + 1) * 512], ys[:],
                          reads=[ys], writes=[(yT, "f%d_%d_%d" % (b, g, q4))])


def merge_phase(S, C, xin, xout, yT, gcol_ap, win_ap, pw_aps, wout_ap, ntiles=T // TT):
    with S.phase():
        wgt = S.sbuf("wgt", [128, 8, 3072], BF16)
        pw = [S.sbuf("pw%d" % i, [128, 4, D], BF16) for i in range(3)]
        wo = S.sbuf("wo", [128, 8, D], BF16)
        xt = [S.sbuf("xt%d" % i, [128, TT // 128, D], F32) for i in range(2)]
        ot = [S.sbuf("ot%d" % i, [128, TT // 128, D], F32) for i in range(2)]
        hT = [S.sbuf("hT%d" % i, [128, 8, TT], BF16) for i in range(2)]
        ysb = [S.sbuf("ysb%d" % i, [128, 12, TT], BF16) for i in range(2)]
        sg = S.sbuf("sgm", [128, 24, TT], BF16)
        macc = S.sbuf("macc", [128, 8, TT], F32)
        mtmp = [S.sbuf("mtmp%d" % i, [128, TT], F32) for i in range(2)]
        mT = S.sbuf("mT", [128, 8, TT], BF16)
        pG = [S.psum("pG%d" % i, [128, 512], F32) for i in range(2)]
        pP = [S.psum("pP%d" % i, [128, 512], F32) for i in range(2)]
        pX = [S.psum("pX%d" % i, [128, 512], F32) for i in range(2)]
        load_gain(S, C, 0, gcol_ap)
        for i in range(3):
            S.dma("pool", wgt[:, :, i * 1024:(i + 1) * 1024],
                  win_ap[:, 3840 + i * 1024:3840 + (i + 1) * 1024].rearrange("(c p) n -> p c n", p=128),
                  writes=[(wgt, i)])
            S.dma("pool", pw[i][:], pw_aps[i].rearrange("(c p) n -> p c n", p=128), writes=[pw[i]])
        S.dma("pool", wo[:], wout_ap.rearrange("(c p) n -> p c n", p=128), writes=[wo])
        nsub = TT // 128
        k = 0
        def ld(it):
            load_x(S, xin, it * TT, xt[it % 2])
            S.dma("sp", ysb[it % 2][:], yT.ap()[:, it * TT:(it + 1) * TT].rearrange("(c p) t -> p c t", p=128),
                  reads=[yT], writes=[ysb[it % 2]])

        ld(0)
        norm_tile(S, C, None, 0, 0, xt[0], hT[0], 0)
        for it in range(ntiles):
            t0 = it * TT
            x_t, h_t, o_t, y_t = xt[it % 2], hT[it % 2], ot[it % 2], ysb[it % 2]
            if it + 1 < ntiles:
                ld(it + 1)
            nxt = {}
            for gi in range(24):
                p = pG[gi % 2]
                for c in range(8):
                    S.op("pe", "matmul", [(wgt, gi // 8), h_t], [p], out=p[:, 0:TT], lhsT=wgt[:, c, gi * 128:(gi + 1) * 128],
                         rhs=h_t[:, c, :], start=(c == 0), stop=(c == 7))
                S.op("act", "activation", [p], [(sg, gi)], out=sg[:, gi, :], in_=p[:, 0:TT], func=AF.Sigmoid)
            for dc in range(8):
                if it + 1 < ntiles:
                    if dc == 1:
                        nxt["xs"] = norm_stats(S, C, xt[(it + 1) % 2])
                    if dc == 5:
                        norm_transpose(S, C, nxt["xs"], 0, hT[(it + 1) % 2], 0)
                for i in range(3):
                    p = pP[k % 2]
                    k += 1
                    for kc in range(4):
                        S.op("pe", "matmul", [pw[i], y_t], [p], out=p[:, 0:TT], lhsT=pw[i][:, kc, dc * 128:(dc + 1) * 128],
                             rhs=y_t[:, i * 4 + kc, :], start=(kc == 0), stop=(kc == 3))
                    if i == 0:
                        S.op("dve", "tensor_tensor", [p, (sg, dc)], [(macc, dc)], out=macc[:, dc, :], in0=p[:, 0:TT],
                             in1=sg[:, dc, :], op=ALU.mult)
                    else:
                        mt = mtmp[i % 2]
                        S.op("dve", "tensor_tensor", [p, (sg, i * 8 + dc)], [mt], out=mt[:], in0=p[:, 0:TT],
                             in1=sg[:, i * 8 + dc, :], op=ALU.mult)
                        if i == 1:
                            S.op("pool", "tensor_tensor", [(macc, dc), mt], [(macc, dc)], out=macc[:, dc, :],
                                 in0=macc[:, dc, :], in1=mt[:], op=ALU.add)
                        else:
                            S.op("dve", "tensor_tensor", [(macc, dc), mt], [(mT, dc)], out=mT[:, dc, :],
                                 in0=macc[:, dc, :], in1=mt[:], op=ALU.add)
            for s in range(nsub):
                for hf in range(2):
                    p = pX[hf]
                    for dc in range(8):
                        S.op("pe", "matmul", [(mT, dc), wo], [p], out=p[:], lhsT=mT[:, dc, s * 128:(s + 1) * 128],
                             rhs=wo[:, dc, hf * 512:(hf + 1) * 512], start=(dc == 0), stop=(dc == 7))
                    S.op("dve", "tensor_tensor", [p, x_t], [o_t], out=o_t[:, s, hf * 512:(hf + 1) * 512], in0=p[:],
                         in1=x_t[:, s, hf * 512:(hf + 1) * 512], op=ALU.add)
            S.dma("sp", xout.ap()[t0:t0 + TT, :].rearrange("(s p) d -> p s d", p=128), o_t[:],
                  reads=[o_t], writes=[xout])


def final_norm_phase(S, C, xin, out, grow_ap):
    with S.phase():
        gb = S.sbuf("gbf", [128, D], F32)
        xt = [S.sbuf("xt%d" % i, [128, TT // 128, D], F32) for i in range(2)]
        ot = [S.sbuf("ot%d" % i, [128, TT // 128, D], F32) for i in range(2)]
        S.dma("sp", gb[:], grow_ap, writes=[gb])
        for it in range(T // TT):
            t0 = it * TT
            x_t, o_t = xt[it % 2], ot[it % 2]
            S.dma("sp", x_t[:], xin.ap()[t0:t0 + TT, :].rearrange("(s p) d -> p s d", p=128), reads=[xin], writes=[x_t])
            for s in range(TT // 128):
                ss = C.ss[C.rr % 4]
                rstd = C.rstd[C.rr % 4]
                C.rr += 1
                S.op("act", "activation", [x_t], [C.junk, ss], out=C.junk[:], in_=x_t[:, s, :], func=AF.Square,
                     accum_out=ss[:])
                S.op("dve", "tensor_scalar", [ss], [rstd], out=rstd[:], in0=ss[:], scalar1=1.0 / D, scalar2=EPS,
                     op0=ALU.mult, op1=ALU.add)
                S.op("pool", "tensor_tensor", [rstd, C.mhalf], [rstd], out=rstd[:], in0=rstd[:], in1=C.mhalf[:],
                     op=ALU.pow)
                S.op("dve", "scalar_tensor_tensor", [x_t, rstd, gb], [o_t], out=o_t[:, s, :], in0=x_t[:, s, :],
                     scalar=rstd[:, 0:1], in1=gb[:], op0=ALU.mult, op1=ALU.mult)
            S.dma("sp", out.ap()[t0:t0 + TT, :].rearrange("(s p) d -> p s d", p=128), o_t[:], reads=[o_t], writes=[out])


RW_C = -0.6065306597126334
import os as _os
RW_STAGE = int(_os.environ.get("K_RW_STAGE", "9"))
RW_CUT = int(_os.environ.get("K_RW_CUT", "9"))
RV = {"w0_f": 0, "w0_b": 1, "a0_f": 2, "a0_b": 3, "k_k": 4, "k_a": 5, "r_k": 6, "ln_g": 7, "ln_b": 8}


def rwkv_phase(S, C, zT, zv, yT, A, nseq=2):
    HS = 512
    NB = SEQ // 128
    with S.phase():
        load_rwkv_consts(S, C)
        w2 = [S.sbuf("w2_%d" % i, [32, 512], BF16) for i in range(2)]
        a2 = [S.sbuf("a2_%d" % i, [32, 512], BF16) for i in range(2)]
        g2 = S.sbuf("g2", [96, 512], BF16)
        rvec = S.sbuf("rvec", [128, 36], F32)
        omka = S.sbuf("omka", [128, 4], F32)
        lnb_eps = S.sbuf("lneps", [128, 1], F32)
        lo = S.sbuf("lo", [96, SEQ], BF16)
        th = [S.sbuf("th%d" % i, [32, SEQ], BF16) for i in range(2)]
        ad = [S.sbuf("ad%d" % i, [32, SEQ], BF16) for i in range(2)]
        sgd = S.sbuf("sgd", [96, SEQ], BF16)
        vt = S.sbuf("vtr", [128, NB, 256], BF16)
        rk = S.sbuf("rk", [128, 2, SEQ], BF16)
        XS = [[S.sbuf("X%d_%d" % (g, i), [128, HS], F32) for i in range(7)] for g in range(2)]
        X = XS[0]
        KR = S.sbuf("KR", [128, 2, NB, 2, 128], BF16)
        BT = S.sbuf("BT", [128, 2, SEQ], BF16)
        KT = S.sbuf("KT", [128, 2, SEQ], BF16)
        tlT = [S.sbuf("tlT%d" % i, [128, SEQ], BF16) for i in range(2)]
        tl = [S.sbuf("tl%d" % i, [128, NB, 256], BF16) for i in range(2)]
        gam = S.sbuf("gamr", [128, 2, NB], F32)
        yacc = S.sbuf("yacc", [128, 2, SEQ], F32)
        bonus = S.sbuf("bonus", [128, 2, SEQ], BF16)
        krz = [S.sbuf("krz%d" % i, [128, 4, 256], BF16) for i in range(2)]
        W4 = [S.sbuf("W4_%d" % i, [128, 4, 4, 128], BF16) for i in range(2)]
        Bbm = [S.sbuf("Bbm%d" % i, [128, 4, 128], BF16) for i in range(2)]
        Mt = [S.sbuf("Mt%d" % i, [128, 4, 128], BF16) for i in range(2)]
        Nt = [S.sbuf("Nt%d" % i, [128, 4, 128], BF16) for i in range(2)]
        Sb = [S.sbuf("Sb%d" % i, [128, 4, 128], BF16) for i in range(2)]
        Gs = S.sbuf("Gs", [128, 4, 64], BF16)
        Ps = S.sbuf("Ps", [128, 4, 64], BF16)
        Hf = S.sbuf("Hfr", [128, 2, 64], F32)
        Ht = S.sbuf("Htr", [128, 2, 64], F32)
        Hb = S.sbuf("Hbr", [128, 2, 64], BF16)
        yst = [S.sbuf("ystr%d" % i, [128, 512], BF16) for i in range(2)]
        bk = [S.psum("rb%d" % i, [128, 512], F32) for i in range(6)]
        bQ, bR, bG, bY, bH = (bk[0], bk[1]), bk[2], bk[3], bk[4], bk[5]
        for i, nm in enumerate(("w2_f", "w2_b")):
            S.dma("pool", w2[i][:], A[nm], writes=[w2[i]])
        for i, nm in enumerate(("a2_f", "a2_b")):
            S.dma("pool", a2[i][:], A[nm], writes=[a2[i]])
        S.dma("pool", g2[:], A["g2"], writes=[g2])
        S.dma("sp", rvec[:], A["rvec"], writes=[rvec])
        S.op("dve", "tensor_scalar", [rvec], [omka], out=omka[:], in0=rvec[:, RV["k_a"] * 4:RV["k_a"] * 4 + 4],
             scalar1=-1.0, scalar2=1.0, op0=ALU.mult, op1=ALU.add)
        S.op("dve", "memset", [], [lnb_eps], ap=lnb_eps[:], constant=64e-5)

        def rv(name, hp):
            c = RV[name] * 4 + hp
            return rvec[:, c:c + 1]

        for b in range(nseq):
            if RW_CUT < 1:
                continue
            tb = b * SEQ
            za = zT.ap()
            for i in range(2):
                S.dma("sp", lo[0:32, :], za[ZR_LORA + 32 * i:ZR_LORA + 32 * i + 32, tb:tb + SEQ], reads=[zT], writes=[lo])
                S.op("act", "activation", [lo], [th[i]], out=th[i][:], in_=lo[0:32, :], func=AF.Tanh)
            for i in range(2):
                S.dma("sp", ad[i][:], za[ZR_LORA + 64 + 32 * i:ZR_LORA + 96 + 32 * i, tb:tb + SEQ], reads=[zT],
                      writes=[ad[i]])
            S.dma("sp", lo[:], za[ZR_GD:ZR_GD + 96, tb:tb + SEQ], reads=[zT], writes=[lo])
            S.op("act", "activation", [lo], [sgd], out=sgd[:], in_=lo[:], func=AF.Sigmoid)
            for grp in range(2):
                if RW_CUT < 2:
                    continue
                S.dma("sp", vt[:], zv.ap()[tb:tb + SEQ, 512 + grp * 256:512 + (grp + 1) * 256]
                      .rearrange("(n p) d -> p n d", p=128), reads=[zv], writes=[vt])
                for d in range(2):
                    for hl in range(2):
                        hp = grp * 2 + hl
                        if d == 0:
                            S.dma("sp", rk[:, 0, :], za[ZR_RR + hp * 128:ZR_RR + (hp + 1) * 128, tb:tb + SEQ], reads=[zT],
                                  writes=[(rk, 0)])
                            S.dma("sp", rk[:, 1, :], za[ZR_RK + hp * 128:ZR_RK + (hp + 1) * 128, tb:tb + SEQ], reads=[zT],
                                  writes=[(rk, 1)])
                        elif hl == 0 or True:
                            S.dma("sp", rk[:, 0, :], za[ZR_RR + hp * 128:ZR_RR + (hp + 1) * 128, tb:tb + SEQ], reads=[zT],
                                  writes=[(rk, 0)])
                            S.dma("sp", rk[:, 1, :], za[ZR_RK + hp * 128:ZR_RK + (hp + 1) * 128, tb:tb + SEQ], reads=[zT],
                                  writes=[(rk, 1)])
                        def prep_gen(hs, Xs, bnk):
                            c0 = hs * HS
                            hc = slice(c0, c0 + HS)
                            nbh = HS // 128
                            bsl = slice(hs * nbh, (hs + 1) * nbh)
                            SG, CS, A3, A2, KP, BE, SB = Xs
                            S.op("pe", "matmul", [w2[d], th[d]], [bnk], out=bnk[:], lhsT=w2[d][:, hp * 128:(hp + 1) * 128],
                                 rhs=th[d][:, hc], start=True, stop=True)
                            S.op("act", "activation", [bnk, rvec], [SG], out=SG[:], in_=bnk[:], func=AF.Sigmoid,
                                 bias=rv("w0_f" if d == 0 else "w0_b", hp))
                            yield
                            S.op("dve", "tensor_tensor_scan", [C.rmask128, SG], [CS], out=CS[:], data0=C.rmask128[:, 0:HS],
                                 data1=SG[:], initial=0.0, op0=ALU.mult, op1=ALU.add)
                            yield
                            cs3 = v3(CS[:], 128)
                            S.op("dve", "tensor_tensor", [CS], [A3], out=v3(A3[:], 128),
                                 in0=cs3[:, :, 127:128].to_broadcast([128, nbh, 128]), in1=cs3, op=ALU.subtract)
                            S.op("act", "activation", [CS], [gam], out=gam[:, hl, bsl], in_=cs3[:, :, 127], func=AF.Exp,
                                 scale=RW_C)
                            S.op("pool", "tensor_tensor", [CS, SG], [A2], out=A2[:], in0=CS[:], in1=SG[:], op=ALU.subtract)
                            yield
                            if d == 0:
                                bI, bE, tail = CS, A2, A3
                            else:
                                S.op("dve", "tensor_tensor", [A3, SG], [CS], out=CS[:], in0=A3[:], in1=SG[:], op=ALU.add)
                                bI, bE, tail = CS, A3, A2
                                yield
                            AV = SG
                            S.op("pe", "matmul", [a2[d], ad[d]], [bnk], out=bnk[:], lhsT=a2[d][:, hp * 128:(hp + 1) * 128],
                                 rhs=ad[d][:, hc], start=True, stop=True)
                            S.op("act", "activation", [bnk, rvec], [AV], out=AV[:], in_=bnk[:], func=AF.Sigmoid,
                                 bias=rv("a0_f" if d == 0 else "a0_b", hp))
                            yield
                            S.op("act", "activation", [(rk, 1), rvec], [BE], out=BE[:], in_=rk[:, 1, hc], func=AF.Square,
                                 scale=rv("k_k", hp))
                            yield
                            S.op("pe", "matmul", [C.onesblk, BE], [bnk], out=bnk[:], lhsT=C.onesblk[:], rhs=BE[:],
                                 start=True, stop=True)
                            S.op("act", "activation", [bnk], [SB], out=SB[:], in_=bnk[:], func=AF.Sqrt)
                            yield
                            S.op("dve", "tensor_scalar", [SB], [SB], out=SB[:], in0=SB[:], scalar1=1e-6, scalar2=None,
                                 op0=ALU.max)
                            S.op("dve", "reciprocal", [SB], [SB], out=SB[:], in_=SB[:])
                            yield
                            S.op("dve", "scalar_tensor_tensor", [(rk, 1), rvec, SB], [KP], out=KP[:], in0=rk[:, 1, hc],
                                 scalar=rv("k_k", hp), in1=SB[:], op0=ALU.mult, op1=ALU.mult)
                            yield
                            S.op("dve", "tensor_tensor", [KP, AV], [BE], out=BE[:], in0=KP[:], in1=AV[:], op=ALU.mult)
                            yield
                            S.op("dve", "tensor_scalar", [AV, rvec, omka], [AV], out=AV[:], in0=AV[:], scalar1=rv("k_a", hp),
                                 scalar2=omka[:, hp:hp + 1], op0=ALU.mult, op1=ALU.add)
                            yield
                            S.op("pool", "tensor_tensor", [(rk, 1), AV], [AV], out=AV[:], in0=rk[:, 1, hc], in1=AV[:],
                                 op=ALU.mult)
                            yield
                            KD = AV
                            if d == 0:
                                S.op("dve", "scalar_tensor_tensor", [(rk, 0), rvec, KD], [SB], out=SB[:], in0=rk[:, 0, hc],
                                     scalar=rv("r_k", hp), in1=KD[:], op0=ALU.mult, op1=ALU.mult)
                                yield
                                S.op("pe", "matmul", [C.onesblk, SB], [bnk], out=bnk[:], lhsT=C.onesblk[:], rhs=SB[:],
                                     start=True, stop=True)
                                psT = C.psT[C.rr % 2]
                                C.rr += 1
                                for j in range(4):
                                    blk = c0 // 128 + j
                                    S.op("pe", "transpose", [vt, C.ident], [psT], out=psT[:, j, :],
                                         in_=vt[:, blk, hl * 128:(hl + 1) * 128], identity=C.ident[:])
                                S.op("act", "activation", [bnk], [SB], out=SB[:], in_=bnk[:], func=AF.Copy)
                                yield
                                S.op("dve", "tensor_tensor", [psT, SB], [(bonus, hl)], out=v3(bonus[:, hl, hc]),
                                     in0=psT[:, 0:4, :], in1=v3(SB[:]), op=ALU.mult)
                                yield
                            S.op("act", "activation", [bE], [bE], out=bE[:], in_=bE[:], func=AF.Exp, scale=RW_C)
                            yield
                            S.op("act", "activation", [bI], [bI], out=bI[:], in_=bI[:], func=AF.Exp, scale=RW_C)
                            yield
                            S.op("act", "activation", [tail], [tail], out=tail[:], in_=tail[:], func=AF.Exp, scale=RW_C)
                            yield
                            S.op("dve", "tensor_tensor", [KP, bE], [(KR, hl)], out=KR[:, hl, bsl, 0, :], in0=v3(KP[:]),
                                 in1=v3(bE[:]), op=ALU.mult)
                            yield
                            rdec = bI if d == 0 else bE
                            S.op("dve", "tensor_tensor", [(rk, 0), rdec], [(KR, hl)], out=KR[:, hl, bsl, 1, :],
                                 in0=v3(rk[:, 0, hc]), in1=v3(rdec[:]), op=ALU.mult)
                            yield
                            S.op("pool", "tensor_tensor", [BE, tail], [(tlT[0], hs)], out=tlT[0][:, hc], in0=BE[:],
                                 in1=tail[:], op=ALU.mult)
                            yield
                            S.op("pool", "tensor_tensor", [KD, tail], [(tlT[1], hs)], out=tlT[1][:, hc], in0=KD[:],
                                 in1=tail[:], op=ALU.mult)
                            yield
                            S.op("dve", "reciprocal", [bI], [bI], out=bI[:], in_=bI[:])
                            yield
                            S.op("dve", "tensor_tensor", [BE, bI], [(BT, hl)], out=BT[:, hl, hc], in0=BE[:], in1=bI[:],
                                 op=ALU.mult)
                            yield
                            S.op("pool", "tensor_tensor", [KD, bI], [(KT, hl)], out=KT[:, hl, hc], in0=KD[:], in1=bI[:],
                                 op=ALU.mult)
                            yield

                        for h0 in range(0, SEQ // HS, 2):
                            gens = [prep_gen(h0 + g_, XS[g_], bk[g_]) for g_ in range(2)]
                            while gens:
                                for g in list(gens):
                                    try:
                                        next(g)
                                    except StopIteration:
                                        gens.remove(g)
                        if RW_CUT < 6:
                            continue
                        for blk in range(NB):
                            psT = C.psT[C.rr % 2]
                            C.rr += 1
                            for i in range(2):
                                S.op("pe", "transpose", [tlT[i], C.ident], [psT], out=psT[:, i, :],
                                     in_=tlT[i][:, blk * 128:(blk + 1) * 128], identity=C.ident[:])
                            for i in range(2):
                                if blk % 2:
                                    S.op("act", "activation", [psT], [(tl[i], hl)], out=tl[i][:, blk, hl * 128:(hl + 1) * 128],
                                         in_=psT[:, i, :], func=AF.Copy)
                                else:
                                    S.op("dve", "tensor_copy", [psT], [(tl[i], hl)], out=tl[i][:, blk, hl * 128:(hl + 1) * 128],
                                         in_=psT[:, i, :])
                    if RW_STAGE < 2:
                        continue
                    S.op("dve", "memset", [], [Hf], ap=Hf[:], constant=0.0)
                    S.op("dve", "memset", [], [Hb], ap=Hb[:], constant=0.0)
                    m4 = C.m4_f if d == 0 else C.m4_b
                    mB = C.mB_f if d == 0 else C.mB_b
                    blocks = range(NB) if d == 0 else range(NB - 1, -1, -1)
                    blocks = list(blocks)

                    def pre_inv(bi, blk):
                        th = []
                        bc = slice(blk * 128, (blk + 1) * 128)
                        kz, w4, bbm, sb = krz[bi % 2], W4[bi % 2], Bbm[bi % 2], Sb[bi % 2]

                        def t_pre(h4):
                            def f():
                                hl, par = h4 // 2, h4 % 2
                                q = bQ[h4 % 2]
                                if h4 % 2:
                                    S.op("act", "activation", [(KR, hl), C.pmask], [(kz, h4)], out=kz[:, h4, :],
                                         in_=KR[:, hl, blk, :, :].rearrange("p a b -> p (a b)"), func=AF.Copy,
                                         scale=C.pmask[:, par:par + 1])
                                else:
                                    S.op("pool", "tensor_scalar", [(KR, hl), C.pmask], [(kz, h4)], out=kz[:, h4, :],
                                         in0=KR[:, hl, blk, :, :].rearrange("p a b -> p (a b)"),
                                         scalar1=C.pmask[:, par:par + 1], scalar2=None, op0=ALU.mult)
                                S.op("pe", "matmul", [(BT, hl), (kz, h4)], [q], out=q[:, 0:128], lhsT=BT[:, hl, bc],
                                     rhs=kz[:, h4, 0:128], start=True, stop=True)
                                S.op("pe", "matmul", [(BT, hl), (kz, h4)], [q], out=q[:, 128:256], lhsT=kz[:, h4, 0:128],
                                     rhs=BT[:, hl, bc], start=False, stop=True, skip_group_check=True)
                                S.op("pe", "matmul", [(KT, hl), (kz, h4)], [q], out=q[:, 256:512], lhsT=KT[:, hl, bc],
                                     rhs=kz[:, h4, :], start=False, stop=True, skip_group_check=True)
                                S.op("pe", "matmul", [(BT, hl), (kz, h4)], [bR], out=bR[:, h4 * 128:(h4 + 1) * 128],
                                     lhsT=BT[:, hl, bc], rhs=kz[:, h4, 128:256], start=(h4 == 0), stop=True,
                                     skip_group_check=True)
                                S.op("dve", "tensor_tensor", [q, m4], [(w4, h4)], out=w4[:, h4, :, :], in0=v3(q[:]),
                                     in1=m4[:], op=ALU.mult)
                            return f

                        for h4 in range(4):
                            th.append(t_pre(h4))

                        def t_s0():
                            S.op("act", "activation", [bR], [bbm], out=bbm[:], in_=v3(bR[:]), func=AF.Copy)
                            S.op("pool", "tensor_tensor", [bbm, mB], [bbm], out=bbm[:], in0=bbm[:],
                                 in1=mB[:].to_broadcast([128, 4, 128]), op=ALU.mult)
                            S.op("dve", "tensor_tensor", [w4, C.ident_f], [sb], out=sb[:], in0=w4[:, :, 0, :],
                                 in1=C.ident_f[:].to_broadcast([128, 4, 128]), op=ALU.add)
                        th.append(t_s0)
                        st_ = {"Mp": w4[:, :, 0, :], "Np": w4[:, :, 1, :], "Mr": [w4], "Nr": [w4]}

                        def t_lvA(lv):
                            def f():
                                Mn, Nn = Mt[lv % 2], Nt[lv % 2]
                                Mp, Np = st_["Mp"], st_["Np"]
                                rd = st_["Mr"] + st_["Nr"]
                                for h4 in range(4):
                                    if lv < 6:
                                        S.op("pe", "matmul", rd, [bQ[0]], out=bQ[0][:, h4 * 128:(h4 + 1) * 128],
                                             lhsT=Np[:, h4, :], rhs=Mp[:, h4, :], start=(h4 == 0), stop=True,
                                             skip_group_check=True)
                                    S.op("pe", "matmul", rd, [bQ[1]], out=bQ[1][:, h4 * 128:(h4 + 1) * 128],
                                         lhsT=Mp[:, h4, :], rhs=Np[:, h4, :], start=(h4 == 0), stop=True,
                                         skip_group_check=True)
                                if lv < 6:
                                    S.op("act", "activation", [bQ[0]], [Mn], out=Mn[:], in_=v3(bQ[0][:]), func=AF.Copy)
                                S.op("dve", "tensor_copy", [bQ[1]], [Nn], out=Nn[:], in_=v3(bQ[1][:]))
                                st_["Mp"], st_["Np"], st_["Mr"], st_["Nr"] = Mn[:], Nn[:], [Mn], [Nn]
                            return f

                        def t_lvB(lv):
                            def f():
                                Nn = Nt[lv % 2]
                                for h4 in range(4):
                                    S.op("pe", "matmul", [Nn, sb], [bR], out=bR[:, h4 * 128:(h4 + 1) * 128],
                                         lhsT=Nn[:, h4, :], rhs=sb[:, h4, :], start=(h4 == 0), stop=True,
                                         skip_group_check=True)
                                S.op("dve", "tensor_tensor", [sb, bR], [sb], out=sb[:], in0=sb[:], in1=v3(bR[:]),
                                     op=ALU.add)
                            return f

                        for lv in range(1, 7):
                            th.append(t_lvA(lv))
                            th.append(t_lvB(lv))
                        return th

                    def chain(bi, blk):
                        th = []
                        bc = slice(blk * 128, (blk + 1) * 128)
                        kz, w4, bbm, sb = krz[bi % 2], W4[bi % 2], Bbm[bi % 2], Sb[bi % 2]
                        gG = bG[:, 0:256].rearrange("p (a b) -> p a b", b=64)
                        gP = bG[:, 256:512].rearrange("p (a b) -> p a b", b=64)
                        yv = bY[:, 0:256].rearrange("p (a b) -> p a b", b=128)
                        hv = bH[:, 0:128].rearrange("p (a b) -> p a b", b=64)

                        def t_g():
                            for h4 in range(4):
                                hl = h4 // 2
                                S.op("pe", "matmul", [(kz, h4), Hb], [bG], out=gG[:, h4, :], lhsT=kz[:, h4, 0:128],
                                     rhs=Hb[:, hl, :], start=(h4 == 0), stop=False, skip_group_check=True)
                                S.op("pe", "matmul", [(w4, h4), vt], [bG], out=gG[:, h4, :], lhsT=w4[:, h4, 2, :],
                                     rhs=vt[:, blk, h4 * 64:(h4 + 1) * 64], start=False, stop=True, skip_group_check=True)
                            S.op("act", "activation", [bG], [Gs], out=Gs[:], in_=gG, func=AF.Copy)

                        def t_p():
                            for h4 in range(4):
                                S.op("pe", "matmul", [sb, Gs], [bG], out=gP[:, h4, :], lhsT=sb[:, h4, :], rhs=Gs[:, h4, :],
                                     start=False, stop=True, skip_group_check=True)
                            S.op("act", "activation", [bG], [Ps], out=Ps[:], in_=gP, func=AF.Copy, scale=-1.0)

                        def t_h():
                            for h4 in range(4):
                                hl, par = h4 // 2, h4 % 2
                                po = slice(par * 64, (par + 1) * 64)
                                vh = vt[:, blk, h4 * 64:(h4 + 1) * 64]
                                S.op("pe", "matmul", [(tl[1], hl), vt], [bH], out=hv[po, hl, :],
                                     lhsT=tl[1][:, blk, h4 * 64:(h4 + 1) * 64], rhs=vh, start=(h4 < 2), stop=False,
                                     skip_group_check=True)
                                S.op("pe", "matmul", [(tl[0], hl), Ps], [bH], out=hv[po, hl, :],
                                     lhsT=tl[0][:, blk, h4 * 64:(h4 + 1) * 64], rhs=Ps[:, h4, :], start=False, stop=True,
                                     skip_group_check=True)

                        def t_y():
                            for h4 in range(4):
                                hl, par = h4 // 2, h4 % 2
                                po = slice(par * 64, (par + 1) * 64)
                                vh = vt[:, blk, h4 * 64:(h4 + 1) * 64]
                                S.op("pe", "matmul", [Hb, (kz, h4)], [bY], out=yv[po, hl, :], lhsT=Hb[:, hl, :],
                                     rhs=kz[:, h4, 128:256], start=(h4 < 2), stop=False, skip_group_check=True)
                                S.op("pe", "matmul", [vt, (w4, h4)], [bY], out=yv[po, hl, :], lhsT=vh, rhs=w4[:, h4, 3, :],
                                     start=False, stop=False, skip_group_check=True)
                                S.op("pe", "matmul", [Ps, bbm], [bY], out=yv[po, hl, :], lhsT=Ps[:, h4, :],
                                     rhs=bbm[:, h4, :], start=False, stop=True, skip_group_check=True)

                        def t_upd():
                            S.op("pool", "tensor_tensor", [Hf, gam], [Ht], out=Ht[:], in0=Hf[:],
                                 in1=gam[:, :, blk:blk + 1].to_broadcast([128, 2, 64]), op=ALU.mult)
                            S.op("dve", "tensor_tensor", [Ht, bH], [Hf], out=Hf[:], in0=Ht[:], in1=hv, op=ALU.add)
                            S.op("act", "activation", [Hf], [Hb], out=Hb[:], in_=Hf[:], func=AF.Copy)
                            if d == 0:
                                S.op("act", "activation", [bY], [yacc], out=yacc[:, :, bc], in_=yv, func=AF.Copy)
                            else:
                                S.op("dve", "tensor_tensor", [bY, yacc], [yacc], out=yacc[:, :, bc], in0=yv,
                                     in1=yacc[:, :, bc], op=ALU.add)

                        return [t_g, t_p, t_y, t_h, t_upd]

                    for f in pre_inv(0, blocks[0]):
                        f()
                    for bi, blk in enumerate(blocks):
                        ch = chain(bi, blk)
                        nx = pre_inv(bi + 1, blocks[bi + 1]) if bi + 1 < NB else []
                        pos = {1: 0, 4: 1, 7: 2, 9: 3, 11: 4}
                        for i, f in enumerate(nx):
                            f()
                            if i in pos:
                                ch[pos[i]]()
                        if not nx:
                            for f in ch:
                                f()
                if RW_STAGE < 3:
                    continue
                k = 0
                for hl in range(2):
                    hp = grp * 2 + hl
                    YC, SQ, RS = X[0], X[1], X[2]
                    for q in range(4):
                        qc = slice(q * 512, (q + 1) * 512)
                        S.op("pe", "matmul", [C.onesblk, yacc], [bG], out=bG[:], lhsT=C.onesblk[:], rhs=yacc[:, hl, qc],
                             start=True, stop=True)
                        S.op("dve", "scalar_tensor_tensor", [bG, yacc], [YC], out=YC[:, 0:512], in0=bG[:], scalar=-1.0 / 64,
                             in1=yacc[:, hl, qc], op0=ALU.mult, op1=ALU.add)
                        S.op("act", "activation", [YC], [SQ], out=SQ[:, 0:512], in_=YC[:, 0:512], func=AF.Square)
                        S.op("pe", "matmul", [C.onesblk, SQ], [bY], out=bY[:], lhsT=C.onesblk[:], rhs=SQ[:, 0:512],
                             start=True, stop=True)
                        S.op("act", "activation", [bY, lnb_eps], [RS], out=RS[:, 0:512], in_=bY[:], func=AF.Sqrt,
                             scale=1.0 / 64, bias=lnb_eps[:, 0:1])
                        S.op("dve", "reciprocal", [RS], [RS], out=RS[:, 0:512], in_=RS[:, 0:512])
                        S.op("dve", "tensor_tensor", [YC, RS], [YC], out=YC[:, 0:512], in0=YC[:, 0:512], in1=RS[:, 0:512],
                             op=ALU.mult)
                        S.op("dve", "tensor_scalar", [YC, rvec], [YC], out=YC[:, 0:512], in0=YC[:, 0:512],
                             scalar1=rv("ln_g", hp), scalar2=rv("ln_b", hp), op0=ALU.mult, op1=ALU.add)
                        S.op("pool", "tensor_tensor", [YC, (bonus, hl)], [YC], out=YC[:, 0:512], in0=YC[:, 0:512],
                             in1=bonus[:, hl, qc], op=ALU.add)
                        S.op("pe", "matmul", [g2, sgd], [bH], out=bH[:], lhsT=g2[:, hp * 128:(hp + 1) * 128], rhs=sgd[:, qc],
                             start=True, stop=True)
                        ys = yst[k % 2]
                        k += 1
                        S.op("dve", "tensor_tensor", [YC, bH], [ys], out=ys[:], in0=YC[:, 0:512], in1=bH[:], op=ALU.mult)
                        S.dma("sp", yT.ap()[1024 + hp * 128:1024 + (hp + 1) * 128, tb + q * 512:tb + (q + 1) * 512], ys[:],
                              reads=[ys], writes=[(yT, "r%d_%d_%d" % (b, hp, q))])


NCORES = 8
DEPTH = 2
_bf = ml_dtypes.bfloat16

CONST_SPECS = (("ident_bf", [128, 128], BF16), ("rmask", [128, 2048], F32), ("mask_f", [128, 128], F32),
               ("mask_b", [128, 128], F32), ("rmask128", [128, 1024], F32), ("onesblk", [128, 128], F32),
               ("pmask", [128, 2], F32), ("ident_f", [128, 128], F32), ("m4_f", [128, 4, 128], F32),
               ("m4_b", [128, 4, 128], F32), ("mB_f", [128, 128], F32), ("mB_b", [128, 128], F32),
               ("dftS", [2, 2048, 2048], BF16), ("dftC", [128, 256], BF16))
W_SPECS = (("ffn1_gate", [DEPTH, D, DFF]), ("ffn1_up", [DEPTH, D, DFF]), ("ffn1_down", [DEPTH, DFF, D]),
           ("w_in", [DEPTH, D, 6912]), ("gla_up_f", [DEPTH, 16, 256]), ("gla_up_b", [DEPTH, 16, 256]),
           ("rwkv_w2_f", [DEPTH, 32, 512]), ("rwkv_w2_b", [DEPTH, 32, 512]), ("rwkv_a2_f", [DEPTH, 32, 512]),
           ("rwkv_a2_b", [DEPTH, 32, 512]), ("rwkv_g2", [DEPTH, 96, 512]), ("proj_gla", [DEPTH, 512, D]),
           ("proj_fnet", [DEPTH, 512, D]), ("proj_rwkv", [DEPTH, 512, D]), ("w_out", [DEPTH, D, D]),
           ("ffn2_gate", [DEPTH, D, DFF]), ("ffn2_up", [DEPTH, D, DFF]), ("ffn2_down", [DEPTH, DFF, D]),
           ("gcols", [DEPTH, 3, 128, 8]), ("mub", [DEPTH, 128, 1792]), ("gvec", [DEPTH, 128, 8]),
           ("rvec", [DEPTH, 128, 36]), ("grow", [128, D]))


def host_constants():
    ii = np.arange(128)
    same = (ii[:, None] // 64) == (ii[None, :] // 64)
    LT = (ii[:, None] < ii[None, :]).astype(np.float32)
    GT = (ii[:, None] > ii[None, :]).astype(np.float32)
    LE = (ii[:, None] <= ii[None, :]).astype(np.float32)
    rm = np.ones((128, 2048), np.float32)
    rm[:, ::64] = 0
    rm128 = np.ones((128, 1024), np.float32)
    rm128[:, ::128] = 0
    ob = np.zeros((128, 128), np.float32)
    ob[:64, :64] = 1
    ob[64:, 64:] = 1
    pm = np.zeros((128, 2), np.float32)
    pm[:64, 0] = 1
    pm[64:, 1] = 1
    s = np.arange(2048)
    ang = 2 * np.pi * ((np.outer(s, s) % 2048).astype(np.float64)) / 2048
    c = np.arange(128)
    angc = 2 * np.pi * ((np.outer(c, c) % 128).astype(np.float64)) / 128
    sc = 1.0 / np.sqrt(2048 * 128)
    return {
        "ident_bf": np.eye(128, dtype=_bf), "rmask": rm,
        "mask_f": (same & (ii[:, None] <= ii[None, :])).astype(np.float32),
        "mask_b": (same & (ii[:, None] > ii[None, :])).astype(np.float32),
        "rmask128": rm128, "onesblk": ob, "pmask": pm, "ident_f": np.eye(128, dtype=np.float32),
        "m4_f": np.ascontiguousarray(np.stack([-LT, -GT, LT, LE], axis=1)),
        "m4_b": np.ascontiguousarray(np.stack([-GT, -LT, GT, GT], axis=1)),
        "mB_f": LE, "mB_b": GT,
        "dftS": np.stack([np.cos(ang), -np.sin(ang)]).astype(_bf),
        "dftC": (np.concatenate([np.cos(angc), np.sin(angc)], axis=1) * sc).astype(_bf),
    }


def host_layouts(inp):
    L = DEPTH
    f32 = np.float32
    gcols = np.zeros((L, 3, 128, 8), f32)
    mub = np.zeros((L, 128, 1792), f32)
    gvec = np.zeros((L, 128, 8), f32)
    rvec = np.zeros((L, 128, 36), f32)
    for l in range(L):
        for i, nm in enumerate(("ffn1_norm", "mix_norm", "ffn2_norm")):
            gcols[l, i] = np.asarray(inp[nm][l], f32).reshape(8, 128).T
        mu = np.asarray(inp["rwkv_mu"][l], f32)
        muz = np.zeros(1792, f32)
        muz[0:512] = mu[0:512]
        muz[512:1024] = mu[512:1024]
        muz[1024:1152] = mu[1536:1664]
        muz[1152:1248] = mu[1664:1760]
        muz[1280:1792] = mu[1024:1536]
        mub[l] = np.broadcast_to(muz, (128, 1792))
        gvec[l, :, 0:2] = np.asarray(inp["gla_bias_f"][l], f32).reshape(2, 128).T
        gvec[l, :, 2:4] = np.asarray(inp["gla_bias_b"][l], f32).reshape(2, 128).T
        gvec[l, :, 4:8] = np.asarray(inp["gla_norm"][l], f32).reshape(4, 128).T
        for i, nm in enumerate(("rwkv_w0_f", "rwkv_w0_b", "rwkv_a0_f", "rwkv_a0_b", "rwkv_k_k", "rwkv_k_a", "rwkv_r_k",
                                "rwkv_ln_g", "rwkv_ln_b")):
            rvec[l, :, i * 4:(i + 1) * 4] = np.asarray(inp[nm][l], f32).reshape(4, 128).T
    grow = np.ascontiguousarray(np.broadcast_to(np.asarray(inp["final_norm"], f32), (128, D)))
    return {"gcols": gcols, "mub": mub, "gvec": gvec, "rvec": rvec, "grow": grow}


def build_program():
    nc = bass.Bass("TRN2", target_bir_lowering=False)
    x = nc.dram_tensor("x", [T, D], F32, kind="ExternalInput")
    out = nc.dram_tensor("out", [T, D], F32, kind="ExternalOutput")
    cn = {}
    for name, shp, dt in CONST_SPECS:
        cn[name] = nc.dram_tensor(name, shp, dt, kind="ExternalInput").ap()
    W = {}
    for name, shp in W_SPECS:
        W[name] = nc.dram_tensor(name, shp, F32, kind="ExternalInput").ap()
    xs = [Buf(nc.dram_tensor("xres%d" % i, [T, D], F32), "xres%d" % i) for i in range(3)]
    import os
    dk = {"kind": "ExternalOutput"} if os.environ.get("K_DUMP") else {}
    zT = Buf(nc.dram_tensor("zT", [ZROWS, T], BF16, **dk), "zT")
    zv = Buf(nc.dram_tensor("zv", [T, 1024], BF16, **dk), "zv")
    yT = Buf(nc.dram_tensor("yT", [1536, T], BF16, **dk), "yT")
    xin = Buf(x, "x")
    outb = Buf(out, "out")
    with contextlib.ExitStack() as st:
        S = Sched(nc, st)
        C = Ctx()
        setup_consts(S, C, cn)
        cur = xin
        import os
        NPH = int(os.environ.get("K_NPH", "99"))
        for l in range(DEPTH):
            if l * 7 + 0 >= NPH:
                break
            ffn_phase(S, C, cur, xs[0], W["gcols"][l, 0], W["ffn1_gate"][l], W["ffn1_up"][l], W["ffn1_down"][l])
            if l * 7 + 1 >= NPH:
                break
            mixin_phase(S, C, xs[0], zT, zv, W["gcols"][l, 1], W["w_in"][l], W["mub"][l])
            if l * 7 + 2 >= NPH:
                break
            gla_phase(S, C, zT, zv, yT, {"up_f": W["gla_up_f"][l], "up_b": W["gla_up_b"][l], "gvec": W["gvec"][l]})
            if l * 7 + 3 >= NPH:
                break
            fnet_phase(S, C, zT, yT, cn)
            if l * 7 + 4 >= NPH:
                break
            if os.environ.get("K_SKIP_R0") and l == 0:
                pass
            else:
              rwkv_phase(S, C, zT, zv, yT, {"w2_f": W["rwkv_w2_f"][l], "w2_b": W["rwkv_w2_b"][l],
                                          "a2_f": W["rwkv_a2_f"][l], "a2_b": W["rwkv_a2_b"][l],
                                          "g2": W["rwkv_g2"][l], "rvec": W["rvec"][l]})
            if l * 7 + 5 >= NPH:
                break
            merge_phase(S, C, xs[0], xs[1], yT, W["gcols"][l, 1], W["w_in"][l],
                        [W["proj_gla"][l], W["proj_fnet"][l], W["proj_rwkv"][l]], W["w_out"][l])
            if l * 7 + 6 >= NPH:
                break
            ffn_phase(S, C, xs[1], xs[2], W["gcols"][l, 2], W["ffn2_gate"][l], W["ffn2_up"][l], W["ffn2_down"][l])
            cur = xs[2]
        final_norm_phase(S, C, cur, outb, W["grow"])
        S.emit()
    return nc


_CACHE = {}


def kernel(**inputs):
    inp = {k: np.asarray(v) for k, v in inputs.items()}
    if "nc" not in _CACHE:
        _CACHE["nc"] = build_program()
        _CACHE["consts"] = host_constants()
    nc = _CACHE["nc"]
    shared = dict(_CACHE["consts"])
    shared.update(host_layouts(inp))
    for name, _ in W_SPECS:
        if name not in shared:
            shared[name] = np.ascontiguousarray(inp[name], dtype=np.float32)
    x = np.ascontiguousarray(inp["x"], dtype=np.float32).reshape(NCORES, T, D)
    in_maps = []
    for c in range(NCORES):
        m = dict(shared)
        m["x"] = x[c]
        in_maps.append(m)
    res = run_bass_kernel_spmd(nc, in_maps, core_ids=list(range(NCORES)))
    out = np.stack([np.asarray(r["out"]) for r in res.results], axis=0)
    return out.reshape(16, 2048, D).astype(np.float32)
```

```python
import contextlib
import numpy as np
import ml_dtypes
import concourse.bass as bass
import concourse.mybir as mybir
from concourse.bass_utils import run_bass_kernel_spmd

F32 = mybir.dt.float32
BF16 = mybir.dt.bfloat16
AF = mybir.ActivationFunctionType
ALU = mybir.AluOpType
AX = mybir.AxisListType

ENGS = ("pe", "act", "dve", "pool", "sp")
N_DMA_SEMS = 24


class Buf:
    def __init__(self, t, name):
        self.t = t
        self.name = name
        self.st = {}
        self.is_psum = False

    def _parts(self, part):
        if part is None:
            return list(self.st.keys()) or [None]
        ks = [part]
        if None in self.st:
            ks.append(None)
        return ks

    def deps_for_read(self, part):
        out = set()
        for k in self._parts(part):
            s = self.st.get(k)
            if s and s[0] is not None:
                out.add(s[0])
            if s and self.is_psum:
                out.update(s[1])
        return out

    def deps_for_write(self, part):
        out = set()
        for k in self._parts(part):
            s = self.st.get(k)
            if s:
                if s[0] is not None:
                    out.add(s[0])
                out.update(s[1])
        return out

    def note_read(self, part, ev):
        s = self.st.setdefault(part, [None, []])
        s[1].append(ev)
        if len(s[1]) > 12:
            best = {}
            for (sm, v) in s[1]:
                if best.get(sm, -1) < v:
                    best[sm] = v
            s[1] = [(sm, v) for sm, v in best.items()]

    def note_write(self, part, ev):
        if part is None:
            self.st = {None: [ev, []]}
        else:
            self.st[part] = [ev, []]

    def __getitem__(self, idx):
        return self.t[idx]

    def ap(self):
        return self.t.ap() if hasattr(self.t, "ap") and callable(self.t.ap) else self.t


class Sched:
    def __init__(self, nc, stack):
        self.nc = nc
        self.stack = stack
        self.eng = {"pe": nc.tensor, "act": nc.scalar, "dve": nc.vector,
                    "pool": nc.gpsimd, "sp": nc.sync}
        self.prog = {e: [] for e in ENGS}
        self.cnt = {e: 0 for e in ENGS}
        self.sems = {}
        for e in ENGS:
            self.sems["E" + e] = stack.enter_context(nc.semaphore("sem_" + e))
        self.dma_sems = []
        for i in range(N_DMA_SEMS):
            k = "D%d" % i
            self.sems[k] = stack.enter_context(nc.semaphore("sem_d%d" % i))
            self.dma_sems.append([k, 0])
        self.dma_rrs = {}
        self.waited = {e: {} for e in ENGS}
        self.nwaits = 0
        self.out_events = []

    def _nm(self, name):
        self.uid = getattr(self, "uid", 0) + 1
        return "%s_u%d" % (name, self.uid)

    def sbuf(self, name, shape, dt):
        name = self._nm(name)
        t = self.stack.enter_context(self.nc.sbuf_tensor(name, list(shape), dt))
        return Buf(t, name)

    def psum(self, name, shape, dt=F32):
        name = self._nm(name)
        t = self.stack.enter_context(self.nc.psum_tensor(name, list(shape), dt))
        b = Buf(t, name)
        b.is_psum = True
        return b

    def dram(self, name, shape, dt, kind="Internal"):
        t = self.nc.dram_tensor(name, list(shape), dt, kind=kind)
        return Buf(t, name)

    def _wait(self, e, ev):
        sm, v = ev
        if self.waited[e].get(sm, -1) >= v:
            return
        if sm == "Epe" and e == "pe":
            return
        self.waited[e][sm] = v
        self.prog[e].append(("w", sm, v))
        self.nwaits += 1

    @staticmethod
    def _norm(lst):
        out = []
        for x in lst:
            if isinstance(x, Buf):
                out.append((x, None))
            else:
                out.append(x)
        return out

    def op(self, e, fn, reads=(), writes=(), **kw):
        if isinstance(fn, str):
            name = fn

            def fn(eng, name=name, kw=kw):
                return getattr(eng, name)(**kw)
        reads = self._norm(reads)
        writes = self._norm(writes)
        deps = set()
        for b, p in reads:
            deps |= b.deps_for_read(p)
        for b, p in writes:
            deps |= b.deps_for_write(p)
        for ev in sorted(deps):
            self._wait(e, ev)
        self.cnt[e] += 1
        ev = ("E" + e, self.cnt[e])
        self.prog[e].append(("o", fn, "E" + e, 1))
        for b, p in reads:
            b.note_read(p, ev)
        for b, p in writes:
            b.note_write(p, ev)
        return ev

    def dma(self, e, out_ap, in_ap, reads=(), writes=(), **kw):
        reads = self._norm(reads)
        writes = self._norm(writes)
        deps = set()
        for b, p in reads:
            deps |= b.deps_for_read(p)
        for b, p in writes:
            deps |= b.deps_for_write(p)
        half = N_DMA_SEMS // 2
        base = 0 if e == "pool" else half
        rr = self.dma_rrs.get(e == "pool", 0)
        slot = self.dma_sems[base + rr]
        self.dma_rrs[e == "pool"] = (rr + 1) % half
        if slot[1] > 0:
            deps.add((slot[0], slot[1]))
        for ev in sorted(deps):
            self._wait(e, ev)
        slot[1] += 16
        ev = (slot[0], slot[1])

        def fn(eng, out_ap=out_ap, in_ap=in_ap, kw=kw):
            return eng.dma_start(out=out_ap, in_=in_ap, **kw)

        self.prog[e].append(("o", fn, slot[0], 16))
        for b, p in reads:
            b.note_read(p, ev)
        for b, p in writes:
            b.note_write(p, ev)
        return ev

    def wait_all(self, e, events):
        for ev in events:
            self._wait(e, ev)

    def emit(self):
        nc = self.nc
        with nc.Block() as block:
            def mk(e):
                def body(eng):
                    for it in self.prog[e]:
                        if it[0] == "w":
                            eng.wait_ge(self.sems[it[1]], it[2])
                        else:
                            ins = it[1](eng)
                            ins.then_inc(self.sems[it[2]], it[3])
                return body
            block.tensor(mk("pe"))
            block.scalar(mk("act"))
            block.vector(mk("dve"))
            block.gpsimd(mk("pool"))
            block.sync(mk("sp"))


def barrier(S):
    evs = [("E" + e, S.cnt[e]) for e in ENGS if S.cnt[e] > 0]
    evs += [(k, v) for k, v in S.dma_sems if v > 0]
    for e in ENGS:
        for ev in evs:
            S._wait(e, ev)


Sched.barrier = barrier


@contextlib.contextmanager
def _phase(S):
    S.barrier()
    old = S.stack
    with contextlib.ExitStack() as ps:
        S.stack = ps
        try:
            yield
        finally:
            S.barrier()
            S.stack = old


Sched.phase = _phase


def setup_consts(S, C, consts):
    C.rr = 0
    C.rt = 0
    C.ident = S.sbuf("ident", [128, 128], BF16)
    C.ones_f = S.sbuf("ones_f", [128, 512], F32)
    C.mhalf = S.sbuf("mhalf", [128, 1], F32)
    C.junk = S.sbuf("junk", [128, 1024], BF16)
    C.ss = [S.sbuf("ss%d" % i, [128, 1], F32) for i in range(4)]
    C.rstd = [S.sbuf("rstd%d" % i, [128, 1], F32) for i in range(4)]
    C.xs = [S.sbuf("xs%d" % i, [128, 1024], BF16) for i in range(2)]
    C.psT = [S.psum("psT%d" % i, [128, 8, 128], BF16) for i in range(2)]
    C.gB = [S.sbuf("gB%d" % i, [128, 8, 128], F32) for i in range(1)]
    C.gcol = S.sbuf("gcol", [128, 8], F32)
    S.dma("sp", C.ident[:], consts["ident_bf"], writes=[C.ident])
    C.one_c = S.sbuf("one_c", [128, 1], F32)
    C.eps_c = S.sbuf("eps_c", [128, 1], F32)
    S.op("dve", "memset", [], [C.one_c], ap=C.one_c[:], constant=1.0)
    S.op("dve", "memset", [], [C.eps_c], ap=C.eps_c[:], constant=1e-6)
    C.consts = consts

    S.op("dve", lambda e: e.memset(C.ones_f[:], 1.0), writes=[C.ones_f])
    S.op("dve", lambda e: e.memset(C.mhalf[:], -0.5), writes=[C.mhalf])


def load_rwkv_consts(S, C):
    consts = C.consts
    C.rmask128 = S.sbuf("rmask128", [128, 1024], F32)
    C.onesblk = S.sbuf("onesblk", [128, 128], F32)
    C.pmask = S.sbuf("pmask", [128, 2], F32)
    C.ident_f = S.sbuf("ident_f", [128, 1, 128], F32)
    C.m4_f = S.sbuf("m4_f", [128, 4, 128], F32)
    C.m4_b = S.sbuf("m4_b", [128, 4, 128], F32)
    C.mB_f = S.sbuf("mB_f", [128, 1, 128], F32)
    C.mB_b = S.sbuf("mB_b", [128, 1, 128], F32)
    S.dma("sp", C.rmask128[:], consts["rmask128"], writes=[C.rmask128])
    S.dma("sp", C.onesblk[:], consts["onesblk"], writes=[C.onesblk])
    S.dma("sp", C.pmask[:], consts["pmask"], writes=[C.pmask])
    S.dma("sp", C.ident_f[:, 0, :], consts["ident_f"], writes=[C.ident_f])
    S.dma("sp", C.m4_f[:], consts["m4_f"], writes=[C.m4_f])
    S.dma("sp", C.m4_b[:], consts["m4_b"], writes=[C.m4_b])
    S.dma("sp", C.mB_f[:, 0, :], consts["mB_f"], writes=[C.mB_f])
    S.dma("sp", C.mB_b[:, 0, :], consts["mB_b"], writes=[C.mB_b])


def load_gla_consts(S, C):
    consts = C.consts
    C.rmask = S.sbuf("rmask", [128, 2048], F32)
    C.mask_f = S.sbuf("mask_f", [128, 1, 128], F32)
    C.mask_b = S.sbuf("mask_b", [128, 1, 128], F32)
    S.dma("sp", C.rmask[:], consts["rmask"], writes=[C.rmask])
    S.dma("sp", C.mask_f[:, 0, :], consts["mask_f"], writes=[C.mask_f])
    S.dma("sp", C.mask_b[:, 0, :], consts["mask_b"], writes=[C.mask_b])


D = 1024
DFF = 2816
NJ = DFF // 128
T = 4096
SEQ = 2048
TT = 256
EPS = 1e-6

ZR_RR, ZR_RK, ZR_LORA, ZR_GD = 0, 512, 1024, 1152
ZR_GQ, ZR_GK, ZR_GR, ZR_DN, ZR_FN = 1280, 1536, 1792, 2304, 2432
ZROWS = 2944
NZC = ZROWS // 128
RW0 = 2080
WZ_SEGS = [(RW0, 512, ZR_RR), (RW0 + 512, 512, ZR_RK), (RW0 + 1536, 128, ZR_LORA), (RW0 + 1664, 96, ZR_GD),
           (0, 256, ZR_GQ), (256, 256, ZR_GK), (1024, 512, ZR_GR), (1536, 32, ZR_DN), (1568, 512, ZR_FN)]
WV_SEGS = [(512, 512, 0), (RW0 + 1024, 512, 512)]


class Ctx:
    pass


def load_w_cast(S, dst, part, dst_ap, src_ap, inner):
    S.dma("pool", dst_ap.rearrange("p (a b) -> p a b", b=inner),
          src_ap.rearrange("p (a b) -> p a b", b=inner), writes=[(dst, part)])


def load_x(S, xd, t0, xt):
    S.dma("sp", xt[:], xd.ap()[t0:t0 + TT, :].rearrange("(s p) d -> p s d", p=128),
          reads=[xd], writes=[xt])


def norm_stats(S, C, xt):
    outs = []
    for s in range(TT // 128):
        ss = C.ss[C.rr % 4]
        rstd = C.rstd[C.rr % 4]
        xs = C.xs[C.rr % 2]
        C.rr += 1
        S.op("act", "activation", [xt], [C.junk, ss], out=C.junk[:], in_=xt[:, s, :], func=AF.Square,
             accum_out=ss[:])
        S.op("dve", "tensor_scalar", [ss], [rstd], out=rstd[:], in0=ss[:], scalar1=1.0 / D, scalar2=EPS,
             op0=ALU.mult, op1=ALU.add)
        S.op("pool", "tensor_tensor", [rstd, C.mhalf], [rstd], out=rstd[:], in0=rstd[:], in1=C.mhalf[:],
             op=ALU.pow)
        S.op("act", "activation", [xt, rstd], [xs], out=xs[:], in_=xt[:, s, :], func=AF.Copy,
             scale=rstd[:, 0:1])
        outs.append(xs)
    return outs


def norm_transpose(S, C, xss, g_idx, hT, col0):
    for s, xs in enumerate(xss):
        psT = C.psT[C.rt % 2]
        C.rt += 1
        for c in range(8):
            S.op("pe", "transpose", [xs, C.ident], [psT], out=psT[:, c, :], in_=xs[:, c * 128:(c + 1) * 128],
                 identity=C.ident[:])
        S.op("dve", "tensor_tensor", [psT, C.gB[g_idx]], [hT],
             out=hT[:, :, col0 + s * 128:col0 + (s + 1) * 128], in0=psT[:], in1=C.gB[g_idx][:], op=ALU.mult)


def norm_tile(S, C, xd, t0, g_idx, xt, hT, col0):
    if xd is not None:
        load_x(S, xd, t0, xt)
    norm_transpose(S, C, norm_stats(S, C, xt), g_idx, hT, col0)


def load_gain(S, C, g_idx, gcol_ap):
    S.dma("sp", C.gcol[:], gcol_ap, writes=[C.gcol])
    for c in range(8):
        S.op("dve", "tensor_scalar", [C.gcol, C.ones_f], [C.gB[g_idx]], out=C.gB[g_idx][:, c, :],
             in0=C.ones_f[:, 0:128], scalar1=C.gcol[:, c:c + 1], scalar2=None, op0=ALU.mult)


def ffn_phase(S, C, xin, xout, gcol_ap, wg_ap, wu_ap, wd_ap, ntiles=T // TT):
    with S.phase():
        wg = S.sbuf("wg", [128, 8, DFF], BF16)
        wu = S.sbuf("wu", [128, 8, DFF], BF16)
        wd = S.sbuf("wd", [128, NJ, D], BF16)
        xt = [S.sbuf("xt%d" % i, [128, TT // 128, D], F32) for i in range(2)]
        ot = [S.sbuf("ot%d" % i, [128, TT // 128, D], F32) for i in range(2)]
        hT = [S.sbuf("hT%d" % i, [128, 8, TT], BF16) for i in range(2)]
        sg = [S.sbuf("sg%d" % i, [128, TT], BF16) for i in range(2)]
        act = [S.sbuf("act%d" % i, [128, TT], BF16) for i in range(3)]
        pgu = [S.psum("pgu%d" % i, [128, 2, TT], F32) for i in range(2)]
        pacc = [S.psum("pacc%d" % i, [128, 512], F32) for i in range(4)]
        load_gain(S, C, 0, gcol_ap)
        for cg in range(4):
            for w, w_ap in ((wg, wg_ap), (wu, wu_ap)):
                S.dma("pool", w[:, :, cg * 704:(cg + 1) * 704],
                      w_ap[:, cg * 704:(cg + 1) * 704].rearrange("(c p) n -> p c n", p=128), writes=[(w, cg)])
            for j in range(cg * 6, min(NJ, cg * 6 + 6)):
                load_w_cast(S, wd, j, wd[:, j, :], wd_ap[j * 128:(j + 1) * 128, :], 1024)
        nsub = TT // 128
        load_x(S, xin, 0, xt[0])
        norm_tile(S, C, None, 0, 0, xt[0], hT[0], 0)
        for it in range(ntiles):
            t0 = it * TT
            x_t = xt[it % 2]
            h_t = hT[it % 2]
            o_t = ot[it % 2]
            if it + 1 < ntiles:
                load_x(S, xin, t0 + TT, xt[(it + 1) % 2])
            nxt = {}

            def gu(j):
                pg = pgu[j % 2]
                for (w, gi) in ((wg, 0), (wu, 1)):
                    for c in range(8):
                        S.op("pe", "matmul", [(w, (j * 128) // 704), (w, (j * 128 + 127) // 704), h_t], [pg], out=pg[:, gi, :],
                             lhsT=w[:, c, j * 128:(j + 1) * 128], rhs=h_t[:, c, :], start=(c == 0), stop=(c == 7))
                sgj = sg[j % 2]
                aj = act[j % 3]
                S.op("act", "activation", [pg], [sgj], out=sgj[:], in_=pg[:, 0, :], func=AF.Silu)
                S.op("dve", "tensor_tensor", [pg, sgj], [aj], out=aj[:], in0=pg[:, 1, :], in1=sgj[:], op=ALU.mult)

            def down(j):
                aj = act[j % 3]
                for s in range(nsub):
                    for hf in range(2):
                        pa = pacc[s * 2 + hf]
                        S.op("pe", "matmul", [aj, (wd, j)], [pa], out=pa[:], lhsT=aj[:, s * 128:(s + 1) * 128],
                             rhs=wd[:, j, hf * 512:(hf + 1) * 512], start=(j == 0), stop=(j == NJ - 1))

            gu(0)
            for j in range(NJ):
                if j + 1 < NJ:
                    gu(j + 1)
                down(j)
                if it + 1 < ntiles:
                    if j == 3:
                        nxt["xs"] = norm_stats(S, C, xt[(it + 1) % 2])
                    if j == 12:
                        norm_transpose(S, C, nxt["xs"], 0, hT[(it + 1) % 2], 0)
            for s in range(nsub):
                for hf in range(2):
                    pa = pacc[s * 2 + hf]
                    S.op("dve", "scalar_tensor_tensor", [pa, x_t], [o_t], out=o_t[:, s, hf * 512:(hf + 1) * 512],
                         in0=pa[:], scalar=0.5, in1=x_t[:, s, hf * 512:(hf + 1) * 512], op0=ALU.mult, op1=ALU.add)
            S.dma("sp", xout.ap()[t0:t0 + TT, :].rearrange("(s p) d -> p s d", p=128), o_t[:],
                  reads=[o_t], writes=[xout])


def mixin_phase(S, C, xin, zT, zv, gcol_ap, win_ap, mub_ap, nseq=2):
    with S.phase():
        wz = S.sbuf("wz", [128, 8, ZROWS], BF16)
        wv = S.sbuf("wv", [128, 8, 1024], BF16)
        wzb = S.sbuf("wzb", [128, 8, 1280], BF16)
        wvb = S.sbuf("wvb", [128, 8, 512], BF16)
        mub = S.sbuf("mub", [128, 1792], F32)
        hseq = S.sbuf("hseq", [128, 8, SEQ], BF16)
        hshs = [S.sbuf("hsh%d" % i, [128, 8, TT], BF16) for i in range(2)]
        xt = [S.sbuf("xt%d" % i, [128, TT // 128, D], F32) for i in range(2)]
        zst = [S.sbuf("zst%d" % i, [128, NZC, TT], BF16) for i in range(2)]
        zvst = [S.sbuf("zvst%d" % i, [128, TT // 128, 1024], BF16) for i in range(2)]
        pz = [S.psum("pz%d" % i, [128, 512], F32) for i in range(3)]
        pv = [S.psum("pv%d" % i, [128, 512], F32) for i in range(2)]
        load_gain(S, C, 0, gcol_ap)
        S.op("pool", "memset", [], [wz], ap=wz[:], constant=0.0)
        S.dma("sp", mub[:], mub_ap, writes=[mub])
        for i, (sc, n, dc) in enumerate(WZ_SEGS):
            S.dma("pool", wz[:, :, dc:dc + n], win_ap[:, sc:sc + n].rearrange("(c p) n -> p c n", p=128),
                  writes=[(wz, "s%d" % i)])
        for i, (sc, n, dc) in enumerate(WV_SEGS):
            S.dma("pool", wv[:, :, dc:dc + n], win_ap[:, sc:sc + n].rearrange("(c p) n -> p c n", p=128),
                  writes=[(wv, "s%d" % i)])
        for c in range(8):
            S.op("dve", "scalar_tensor_tensor", [wz, mub], [wzb], out=wzb[:, c, :], in0=wz[:, c, 0:1280],
                 scalar=0.5, in1=mub[:, 0:1280], op0=ALU.mult, op1=ALU.mult)
            S.op("dve", "scalar_tensor_tensor", [wv, mub], [wvb], out=wvb[:, c, :], in0=wv[:, c, 512:1024],
                 scalar=0.5, in1=mub[:, 1280:1792], op0=ALU.mult, op1=ALU.mult)
        S.op("dve", "tensor_scalar", [mub], [mub], out=mub[:], in0=mub[:], scalar1=-1.0, scalar2=1.0,
             op0=ALU.mult, op1=ALU.add)
        for c in range(8):
            S.op("pool", "tensor_tensor", [wz, mub], [wz], out=wz[:, c, 0:1280], in0=wz[:, c, 0:1280],
                 in1=mub[:, 0:1280], op=ALU.mult)
            S.op("pool", "tensor_tensor", [wv, mub], [wv], out=wv[:, c, 512:1024], in0=wv[:, c, 512:1024],
                 in1=mub[:, 1280:1792], op=ALU.mult)
        nev = 0
        for b in range(nseq):
            for it in range(SEQ // TT):
                norm_tile(S, C, xin, b * SEQ + it * TT, 0, xt[it % 2], hseq, it * TT)
            for it in range(SEQ // TT):
                c0 = it * TT
                hsh = hshs[it % 2]
                a0 = 1 if it == 0 else 0
                a1 = TT - 1 if it == SEQ // TT - 1 else TT
                S.op("pool", "tensor_tensor", [hseq], [hsh], out=hsh[:, :, a0:a1], in0=hseq[:, :, c0 + a0 - 1:c0 + a1 - 1],
                     in1=hseq[:, :, c0 + a0 + 1:c0 + a1 + 1], op=ALU.add)
                if it == 0:
                    S.op("pool", "tensor_copy", [hseq], [hsh], out=hsh[:, :, 0:1], in_=hseq[:, :, 1:2])
                if it == SEQ // TT - 1:
                    S.op("pool", "tensor_copy", [hseq], [hsh], out=hsh[:, :, TT - 1:TT], in_=hseq[:, :, SEQ - 2:SEQ - 1])
                zs = zst[it % 2]
                zvs = zvst[it % 2]
                for ck in range(NZC):
                    p = pz[ck % 3]
                    nmm = 16 if ck < 10 else 8
                    k = 0
                    for c in range(8):
                        S.op("pe", "matmul", [wz, hseq], [p], out=p[:, 0:TT], lhsT=wz[:, c, ck * 128:(ck + 1) * 128],
                             rhs=hseq[:, c, c0:c0 + TT], start=(k == 0), stop=(k == nmm - 1))
                        k += 1
                    if ck < 10:
                        for c in range(8):
                            S.op("pe", "matmul", [wzb, hsh], [p], out=p[:, 0:TT],
                                 lhsT=wzb[:, c, ck * 128:(ck + 1) * 128], rhs=hsh[:, c, :],
                                 start=False, stop=(k == nmm - 1))
                            k += 1
                    nev += 1
                    if nev % 2:
                        S.op("act", "activation", [p], [zs], out=zs[:, ck, :], in_=p[:, 0:TT], func=AF.Copy)
                    else:
                        S.op("dve", "tensor_copy", [p], [zs], out=zs[:, ck, :], in_=p[:, 0:TT])
                for s in range(TT // 128):
                    cs = c0 + s * 128
                    for vi in range(2):
                        p = pv[vi]
                        nmm = 16 if vi == 1 else 8
                        k = 0
                        for c in range(8):
                            S.op("pe", "matmul", [wv, hseq], [p], out=p[:], lhsT=hseq[:, c, cs:cs + 128],
                                 rhs=wv[:, c, vi * 512:(vi + 1) * 512], start=(k == 0), stop=(k == nmm - 1))
                            k += 1
                        if vi == 1:
                            for c in range(8):
                                S.op("pe", "matmul", [wvb, hsh], [p], out=p[:], lhsT=hsh[:, c, s * 128:(s + 1) * 128],
                                     rhs=wvb[:, c, :], start=False, stop=(k == nmm - 1))
                                k += 1
                        if vi == 0:
                            S.op("act", "activation", [p], [zvs], out=zvs[:, s, 0:512], in_=p[:], func=AF.Copy)
                        else:
                            S.op("dve", "tensor_copy", [p], [zvs], out=zvs[:, s, 512:1024], in_=p[:])
                tg = b * SEQ + c0
                S.dma("sp", zT.ap()[:, tg:tg + TT].rearrange("(c p) t -> p c t", p=128), zs[:],
                      reads=[zs], writes=[(zT, "w%d" % (tg // TT))])
                S.dma("sp", zv.ap()[tg:tg + TT, :].rearrange("(s p) d -> p s d", p=128), zvs[:],
                      reads=[zvs], writes=[(zv, "w%d" % (tg // TT))])


GLA_C = -1.0 / 16.0


def v3(ap, b=128):
    return ap.rearrange("p (a b) -> p a b", b=b)


def gla_phase(S, C, zT, zv, yT, A, nseq=2):
    with S.phase():
        load_gla_consts(S, C)
        qT = S.sbuf("qT", [128, 2, SEQ], BF16)
        kT = S.sbuf("kT", [128, 2, SEQ], BF16)
        rT = S.sbuf("rT", [128, 4, SEQ], BF16)
        vt = S.sbuf("vt", [128, SEQ // 128, 512], BF16)
        dn = [S.sbuf("dn%d" % i, [16, SEQ], BF16) for i in range(2)]
        up = [S.sbuf("up%d" % i, [16, 256], BF16) for i in range(2)]
        gvec = S.sbuf("gvec", [128, 8], F32)
        negb = S.sbuf("negb", [128, 4], F32)
        oacc = S.sbuf("oacc", [128, 4, SEQ], F32)
        qd = S.sbuf("qd", [128, 2, SEQ], BF16)
        kd = S.sbuf("kd", [128, 2, SEQ], BF16)
        ktT = S.sbuf("ktT", [128, 2, SEQ], BF16)
        kt = S.sbuf("kt", [128, SEQ // 128, 256], BF16)
        gam = S.sbuf("gam", [128, 2, 32], F32)
        L = S.sbuf("L", [128, SEQ], F32)
        cs = S.sbuf("cs", [128, SEQ], F32)
        t1 = S.sbuf("t1", [128, SEQ], F32)
        E = S.sbuf("E", [128, SEQ], F32)
        Hf = S.sbuf("Hf", [128, 2, 128], F32)
        Ht = S.sbuf("Ht", [128, 2, 128], F32)
        Hb = S.sbuf("Hb", [128, 2, 128], BF16)
        sTs = [S.sbuf("sTs%d" % i, [128, 2, 2, 128], BF16) for i in range(2)]
        sq = S.sbuf("sq", [128, 512], F32)
        rs = S.sbuf("rs", [128, 512], F32)
        sr = S.sbuf("sr", [128, 512], BF16)
        yst = [S.sbuf("yst%d" % i, [128, 512], BF16) for i in range(2)]
        bS = [S.psum("bS%d" % i, [128, 512], F32) for i in range(2)]
        bO = [S.psum("bO%d" % i, [128, 512], F32) for i in range(2)]
        bU = [S.psum("bU%d" % i, [128, 512], F32) for i in range(2)]
        psL = bU[0]
        oacc4 = oacc[:].rearrange("p (hp par) t -> p hp par t", par=2)
        S.dma("pool", up[0][:], A["up_f"], writes=[up[0]])
        S.dma("pool", up[1][:], A["up_b"], writes=[up[1]])
        S.dma("sp", gvec[:], A["gvec"], writes=[gvec])
        S.op("dve", "tensor_scalar", [gvec], [negb], out=negb[:], in0=gvec[:, 0:4], scalar1=-1.0, scalar2=None,
             op0=ALU.mult)
        nb = SEQ // 128
        for b in range(nseq):
            tb = b * SEQ
            za = zT.ap()
            S.dma("sp", qT[:], za[ZR_GQ:ZR_GQ + 256, tb:tb + SEQ].rearrange("(c p) t -> p c t", p=128),
                  reads=[zT], writes=[qT])
            S.dma("sp", kT[:], za[ZR_GK:ZR_GK + 256, tb:tb + SEQ].rearrange("(c p) t -> p c t", p=128),
                  reads=[zT], writes=[kT])
            S.dma("sp", rT[:], za[ZR_GR:ZR_GR + 512, tb:tb + SEQ].rearrange("(c p) t -> p c t", p=128),
                  reads=[zT], writes=[rT])
            S.dma("sp", dn[0][:], za[ZR_DN:ZR_DN + 16, tb:tb + SEQ], reads=[zT], writes=[dn[0]])
            S.dma("sp", dn[1][:], za[ZR_DN + 16:ZR_DN + 32, tb:tb + SEQ], reads=[zT], writes=[dn[1]])
            S.dma("sp", vt[:], zv.ap()[tb:tb + SEQ, 0:512].rearrange("(n p) d -> p n d", p=128),
                  reads=[zv], writes=[vt])
            for d in range(2):
                for hp in range(2):
                    for q4 in range(4):
                        S.op("pe", "matmul", [up[d], dn[d]], [psL], out=psL[:], lhsT=up[d][:, hp * 128:(hp + 1) * 128],
                             rhs=dn[d][:, q4 * 512:(q4 + 1) * 512], start=True, stop=True)
                        S.op("act", "activation", [psL, negb], [E], out=E[:, q4 * 512:(q4 + 1) * 512], in_=psL[:],
                             func=AF.Exp, scale=-1.0, bias=negb[:, 2 * d + hp:2 * d + hp + 1])
                    S.op("act", "activation", [E, C.one_c], [L], out=L[:], in_=E[:], func=AF.Ln, bias=C.one_c[:, 0:1])
                    S.op("dve", "tensor_tensor_scan", [C.rmask, L], [cs], out=cs[:], data0=C.rmask[:], data1=L[:],
                         initial=0.0, op0=ALU.mult, op1=ALU.add)
                    cs3 = v3(cs[:], 64)
                    tot = cs3[:, :, 63:64]
                    S.op("dve", "tensor_tensor", [cs], [t1], out=v3(t1[:], 64),
                         in0=tot.to_broadcast([128, 32, 64]), in1=cs3, op=ALU.subtract)
                    S.op("act", "activation", [cs], [gam], out=gam[:, hp, :], in_=cs3[:, :, 63], func=AF.Exp,
                         scale=GLA_C)
                    if d == 0:
                        bI, tail = cs, t1
                    else:
                        S.op("dve", "tensor_tensor", [t1, L], [t1], out=t1[:], in0=t1[:], in1=L[:], op=ALU.add)
                        S.op("dve", "tensor_tensor", [cs, L], [cs], out=cs[:], in0=cs[:], in1=L[:], op=ALU.subtract)
                        bI, tail = t1, cs
                    S.op("act", "activation", [bI], [E], out=E[:], in_=bI[:], func=AF.Exp, scale=GLA_C)
                    S.op("dve", "scalar_tensor_tensor", [qT, E], [qd], out=qd[:, hp, :], in0=qT[:, hp, :], scalar=0.125,
                         in1=E[:], op0=ALU.mult, op1=ALU.mult)
                    S.op("act", "activation", [bI], [E], out=E[:], in_=bI[:], func=AF.Exp, scale=-GLA_C)
                    S.op("dve", "tensor_tensor", [kT, E], [kd], out=kd[:, hp, :], in0=kT[:, hp, :], in1=E[:], op=ALU.mult)
                    S.op("act", "activation", [tail], [E], out=E[:], in_=tail[:], func=AF.Exp, scale=GLA_C)
                    S.op("dve", "tensor_tensor", [kT, E], [ktT], out=ktT[:, hp, :], in0=kT[:, hp, :], in1=E[:],
                         op=ALU.mult)
                for blk in range(nb):
                    psT = C.psT[C.rr % 2]
                    C.rr += 1
                    for hp in range(2):
                        S.op("pe", "transpose", [ktT, C.ident], [psT], out=psT[:, hp, :],
                             in_=ktT[:, hp, blk * 128:(blk + 1) * 128], identity=C.ident[:])
                    S.op("act", "activation", [psT], [kt], out=v3(kt[:, blk, :]), in_=psT[:, 0:2, :], func=AF.Copy)
                S.op("dve", "memset", [], [Hf], ap=Hf[:], constant=0.0)
                S.op("dve", "memset", [], [Hb], ap=Hb[:], constant=0.0)
                mask = C.mask_f if d == 0 else C.mask_b
                blocks = range(nb) if d == 0 else range(nb - 1, -1, -1)
                for bi, blk in enumerate(blocks):
                    bc = slice(blk * 128, (blk + 1) * 128)
                    bf_ = bi % 2
                    sT = sTs[bf_]
                    for h in range(4):
                        hp, par, base = h // 2, h % 2, (h % 2) * 64
                        pS = v3(bS[par][:, 0:256])
                        S.op("pe", "matmul", [kd, qd], [bS[par]], out=pS[:, hp, :], lhsT=kd[base:base + 64, hp, bc],
                             rhs=qd[base:base + 64, hp, bc], start=True, stop=True)
                    for par in range(2):
                        pS = v3(bS[par][:, 0:256])
                        S.op("dve", "tensor_tensor", [bS[par], mask], [(sT, par)], out=sT[:, par, :, :], in0=pS,
                             in1=mask[:].to_broadcast([128, 2, 128]), op=ALU.mult)
                    for h in range(4):
                        hp, par = h // 2, h % 2
                        pO = v3(bO[par][:, 0:256])
                        S.op("pe", "matmul", [vt, (sT, par)], [bO[par]], out=pO[:, hp, :],
                             lhsT=vt[:, blk, h * 128:(h + 1) * 128], rhs=sT[:, par, hp, :], start=(hp == 0), stop=False,
                             skip_group_check=True)
                    halves = (0, 1) if d == 0 else (1, 0)
                    for half in halves:
                        c = blk * 2 + half
                        cc = slice(c * 64, (c + 1) * 64)
                        pb = slice(half * 64, (half + 1) * 64)
                        pU = v3(bU[half][:, 0:256])
                        for h in range(4):
                            hp, par, base = h // 2, h % 2, (h % 2) * 64
                            pO = v3(bO[par][:, 0:256])
                            S.op("pe", "matmul", [Hb, qd], [bO[par]], out=pO[:, hp, half * 64:(half + 1) * 64],
                                 lhsT=Hb[base:base + 64, hp, :], rhs=qd[base:base + 64, hp, cc],
                                 start=False, stop=True, skip_group_check=True)
                        for h in range(4):
                            hp, base = h // 2, (h % 2) * 64
                            S.op("pe", "matmul", [kt, vt], [bU[half]], out=pU[base:base + 64, hp, :],
                                 lhsT=kt[pb, blk, h * 64:(h + 1) * 64], rhs=vt[pb, blk, h * 128:(h + 1) * 128],
                                 start=True, stop=True)
                        S.op("pool", "tensor_tensor", [Hf, gam], [Ht], out=Ht[:], in0=Hf[:],
                             in1=gam[:, :, c:c + 1].to_broadcast([128, 2, 128]), op=ALU.mult)
                        S.op("dve", "tensor_tensor", [Ht, bU[half]], [Hf], out=Hf[:], in0=Ht[:], in1=pU, op=ALU.add)
                        S.op("act", "activation", [Hf], [Hb], out=Hb[:], in_=Hf[:], func=AF.Copy)
                    for par in range(2):
                        pO = v3(bO[par][:, 0:256])
                        if d == 0:
                            S.op("act", "activation", [bO[par]], [oacc], out=oacc4[:, :, par, bc], in_=pO,
                                 func=AF.Copy)
                        else:
                            S.op("dve", "tensor_tensor", [bO[par], oacc], [oacc], out=oacc4[:, :, par, bc], in0=pO,
                                 in1=oacc4[:, :, par, bc], op=ALU.add)
            k = 0
            for h in range(4):
                for q4 in range(4):
                    c4 = slice(q4 * 512, (q4 + 1) * 512)
                    S.op("act", "activation", [oacc], [sq], out=sq[:], in_=oacc[:, h, c4], func=AF.Square)
                    S.op("pe", "matmul", [C.ones_f, sq], [psL], out=psL[:], lhsT=C.ones_f[:, 0:128], rhs=sq[:],
                         start=True, stop=True)
                    S.op("act", "activation", [psL, C.eps_c], [rs], out=rs[:], in_=psL[:], func=AF.Sqrt,
                         scale=1.0 / 128, bias=C.eps_c[:, 0:1])
                    S.op("dve", "reciprocal", [rs], [rs], out=rs[:], in_=rs[:])
                    S.op("act", "activation", [rT], [sr], out=sr[:], in_=rT[:, h, c4], func=AF.Silu)
                    S.op("dve", "scalar_tensor_tensor", [oacc, gvec, rs], [rs], out=rs[:], in0=oacc[:, h, c4],
                         scalar=gvec[:, 4 + h:5 + h], in1=rs[:], op0=ALU.mult, op1=ALU.mult)
                    ys = yst[k % 2]
                    k += 1
                    S.op("dve", "tensor_tensor", [rs, sr], [ys], out=ys[:], in0=rs[:], in1=sr[:], op=ALU.mult)
                    S.dma("sp", yT.ap()[h * 128:(h + 1) * 128, tb + q4 * 512:tb + (q4 + 1) * 512], ys[:],
                          reads=[ys], writes=[(yT, "g%d_%d_%d" % (b, h, q4))])


def fnet_phase(S, C, zT, yT, A, nseq=2):
    with S.phase():
        Ws = S.sbuf("Ws", [128, 2, 16, SEQ], BF16)
        Wc = S.sbuf("Wc", [128, 256], BF16)
        uT = S.sbuf("uT", [128, 4, SEQ], BF16)
        As = [S.sbuf("As%d" % i, [128, 16, 256], BF16) for i in range(2)]
        yst = [S.sbuf("ystf%d" % i, [128, 512], BF16) for i in range(2)]
        pA = [S.psum("pA%d" % i, [128, 512], F32) for i in range(2)]
        pY = [S.psum("pY%d" % i, [128, 512], F32) for i in range(2)]
        S.dma("sp", Wc[:], A["dftC"], writes=[Wc])
        for t in range(2):
            for hlf in range(2):
                S.dma("sp", Ws[:, t, hlf * 8:(hlf + 1) * 8, :],
                      A["dftS"][t, hlf * 1024:(hlf + 1) * 1024, :].rearrange("(n p) s -> p n s", p=128),
                      writes=[(Ws, "%d_%d" % (t, hlf))])
        k = 0
        for b in range(nseq):
            tb = b * SEQ
            S.dma("sp", uT[:], zT.ap()[ZR_FN:ZR_FN + 512, tb:tb + SEQ].rearrange("(c p) t -> p c t", p=128),
                  reads=[zT], writes=[uT])
            for g in range(4):
                a_s = As[g % 2]
                for n in range(16):
                    p = pA[n % 2]
                    S.op("pe", "matmul", [uT, Wc], [p], out=p[:, 0:256], lhsT=uT[:, g, n * 128:(n + 1) * 128], rhs=Wc[:],
                         start=True, stop=True)
                    if n % 2:
                        S.op("act", "activation", [p], [a_s], out=a_s[:, n, :], in_=p[:, 0:256], func=AF.Copy)
                    else:
                        S.op("dve", "tensor_copy", [p], [a_s], out=a_s[:, n, :], in_=p[:, 0:256])
                for q4 in range(4):
                    p = pY[q4 % 2]
                    i = 0
                    for t in range(2):
                        for n in range(16):
                            S.op("pe", "matmul", [a_s, (Ws, "%d_%d" % (t, n // 8))], [p], out=p[:], lhsT=a_s[:, n, t * 128:(t + 1) * 128],
                                 rhs=Ws[:, t, n, q4 * 512:(q4 + 1) * 512], start=(i == 0), stop=(i == 31))
                            i += 1
                    ys = yst[k % 2]
                    k += 1
                    S.op("act", "activation", [p], [ys], out=ys[:], in_=p[:], func=AF.Copy)
                    S.dma("sp", yT.ap()[512 + g * 128:512 + (g + 1) * 128, tb + q4 * 512:tb + (q4 + 1) * 512], ys[:],
                          reads=[ys], writes=[(yT, "f%d_%d_%d" % (b, g, q4))])


def merge_phase(S, C, xin, xout, yT, gcol_ap, win_ap, pw_aps, wout_ap, ntiles=T // TT):
    with S.phase():
        wgt = S.sbuf("wgt", [128, 8, 3072], BF16)
        pw = [S.sbuf("pw%d" % i, [128, 4, D], BF16) for i in range(3)]
        wo = S.sbuf("wo", [128, 8, D], BF16)
        xt = [S.sbuf("xt%d" % i, [128, TT // 128, D], F32) for i in range(2)]
        ot = [S.sbuf("ot%d" % i, [128, TT // 128, D], F32) for i in range(2)]
        hT = [S.sbuf("hT%d" % i, [128, 8, TT], BF16) for i in range(2)]
        ysb = [S.sbuf("ysb%d" % i, [128, 12, TT], BF16) for i in range(2)]
        sg = S.sbuf("sgm", [128, 24, TT], BF16)
        macc = S.sbuf("macc", [128, 8, TT], F32)
        mtmp = [S.sbuf("mtmp%d" % i, [128, TT], F32) for i in range(2)]
        mT = S.sbuf("mT", [128, 8, TT], BF16)
        pG = [S.psum("pG%d" % i, [128, 512], F32) for i in range(2)]
        pP = [S.psum("pP%d" % i, [128, 512], F32) for i in range(2)]
        pX = [S.psum("pX%d" % i, [128, 512], F32) for i in range(2)]
        load_gain(S, C, 0, gcol_ap)
        for i in range(3):
            S.dma("pool", wgt[:, :, i * 1024:(i + 1) * 1024],
                  win_ap[:, 3840 + i * 1024:3840 + (i + 1) * 1024].rearrange("(c p) n -> p c n", p=128),
                  writes=[(wgt, i)])
            S.dma("pool", pw[i][:], pw_aps[i].rearrange("(c p) n -> p c n", p=128), writes=[pw[i]])
        S.dma("pool", wo[:], wout_ap.rearrange("(c p) n -> p c n", p=128), writes=[wo])
        nsub = TT // 128
        k = 0
        def ld(it):
            load_x(S, xin, it * TT, xt[it % 2])
            S.dma("sp", ysb[it % 2][:], yT.ap()[:, it * TT:(it + 1) * TT].rearrange("(c p) t -> p c t", p=128),
                  reads=[yT], writes=[ysb[it % 2]])

        ld(0)
        norm_tile(S, C, None, 0, 0, xt[0], hT[0], 0)
        for it in range(ntiles):
            t0 = it * TT
            x_t, h_t, o_t, y_t = xt[it % 2], hT[it % 2], ot[it % 2], ysb[it % 2]
            if it + 1 < ntiles:
                ld(it + 1)
            nxt = {}
            for gi in range(24):
                p = pG[gi % 2]
                for c in range(8):
                    S.op("pe", "matmul", [(wgt, gi // 8), h_t], [p], out=p[:, 0:TT], lhsT=wgt[:, c, gi * 128:(gi + 1) * 128],
                         rhs=h_t[:, c, :], start=(c == 0), stop=(c == 7))
                S.op("act", "activation", [p], [(sg, gi)], out=sg[:, gi, :], in_=p[:, 0:TT], func=AF.Sigmoid)
            for dc in range(8):
                if it + 1 < ntiles:
                    if dc == 1:
                        nxt["xs"] = norm_stats(S, C, xt[(it + 1) % 2])
                    if dc == 5:
                        norm_transpose(S, C, nxt["xs"], 0, hT[(it + 1) % 2], 0)
                for i in range(3):
                    p = pP[k % 2]
                    k += 1
                    for kc in range(4):
                        S.op("pe", "matmul", [pw[i], y_t], [p], out=p[:, 0:TT], lhsT=pw[i][:, kc, dc * 128:(dc + 1) * 128],
                             rhs=y_t[:, i * 4 + kc, :], start=(kc == 0), stop=(kc == 3))
                    if i == 0:
                        S.op("dve", "tensor_tensor", [p, (sg, dc)], [(macc, dc)], out=macc[:, dc, :], in0=p[:, 0:TT],
                             in1=sg[:, dc, :], op=ALU.mult)
                    else:
                        mt = mtmp[i % 2]
                        S.op("dve", "tensor_tensor", [p, (sg, i * 8 + dc)], [mt], out=mt[:], in0=p[:, 0:TT],
                             in1=sg[:, i * 8 + dc, :], op=ALU.mult)
                        if i == 1:
                            S.op("pool", "tensor_tensor", [(macc, dc), mt], [(macc, dc)], out=macc[:, dc, :],
                                 in0=macc[:, dc, :], in1=mt[:], op=ALU.add)
                        else:
                            S.op("dve", "tensor_tensor", [(macc, dc), mt], [(mT, dc)], out=mT[:, dc, :],
                                 in0=macc[:, dc, :], in1=mt[:], op=ALU.add)
            for s in range(nsub):
                for hf in range(2):
                    p = pX[hf]
                    for dc in range(8):
                        S.op("pe", "matmul", [(mT, dc), wo], [p], out=p[:], lhsT=mT[:, dc, s * 128:(s + 1) * 128],
                             rhs=wo[:, dc, hf * 512:(hf + 1) * 512], start=(dc == 0), stop=(dc == 7))
                    S.op("dve", "tensor_tensor", [p, x_t], [o_t], out=o_t[:, s, hf * 512:(hf + 1) * 512], in0=p[:],
                         in1=x_t[:, s, hf * 512:(hf + 1) * 512], op=ALU.add)
            S.dma("sp", xout.ap()[t0:t0 + TT, :].rearrange("(s p) d -> p s d", p=128), o_t[:],
                  reads=[o_t], writes=[xout])


def final_norm_phase(S, C, xin, out, grow_ap):
    with S.phase():
        gb = S.sbuf("gbf", [128, D], F32)
        xt = [S.sbuf("xt%d" % i, [128, TT // 128, D], F32) for i in range(2)]
        ot = [S.sbuf("ot%d" % i, [128, TT // 128, D], F32) for i in range(2)]
        S.dma("sp", gb[:], grow_ap, writes=[gb])
        for it in range(T // TT):
            t0 = it * TT
            x_t, o_t = xt[it % 2], ot[it % 2]
            S.dma("sp", x_t[:], xin.ap()[t0:t0 + TT, :].rearrange("(s p) d -> p s d", p=128), reads=[xin], writes=[x_t])
            for s in range(TT // 128):
                ss = C.ss[C.rr % 4]
                rstd = C.rstd[C.rr % 4]
                C.rr += 1
                S.op("act", "activation", [x_t], [C.junk, ss], out=C.junk[:], in_=x_t[:, s, :], func=AF.Square,
                     accum_out=ss[:])
                S.op("dve", "tensor_scalar", [ss], [rstd], out=rstd[:], in0=ss[:], scalar1=1.0 / D, scalar2=EPS,
                     op0=ALU.mult, op1=ALU.add)
                S.op("pool", "tensor_tensor", [rstd, C.mhalf], [rstd], out=rstd[:], in0=rstd[:], in1=C.mhalf[:],
                     op=ALU.pow)
                S.op("dve", "scalar_tensor_tensor", [x_t, rstd, gb], [o_t], out=o_t[:, s, :], in0=x_t[:, s, :],
                     scalar=rstd[:, 0:1], in1=gb[:], op0=ALU.mult, op1=ALU.mult)
            S.dma("sp", out.ap()[t0:t0 + TT, :].rearrange("(s p) d -> p s d", p=128), o_t[:], reads=[o_t], writes=[out])


RW_C = -0.6065306597126334
import os as _os
RW_STAGE = int(_os.environ.get("K_RW_STAGE", "9"))
RW_CUT = int(_os.environ.get("K_RW_CUT", "9"))
RV = {"w0_f": 0, "w0_b": 1, "a0_f": 2, "a0_b": 3, "k_k": 4, "k_a": 5, "r_k": 6, "ln_g": 7, "ln_b": 8}


def rwkv_phase(S, C, zT, zv, yT, A, nseq=2):
    HS = 512
    NB = SEQ // 128
    with S.phase():
        load_rwkv_consts(S, C)
        w2 = [S.sbuf("w2_%d" % i, [32, 512], BF16) for i in range(2)]
        a2 = [S.sbuf("a2_%d" % i, [32, 512], BF16) for i in range(2)]
        g2 = S.sbuf("g2", [96, 512], BF16)
        rvec = S.sbuf("rvec", [128, 36], F32)
        omka = S.sbuf("omka", [128, 4], F32)
        lnb_eps = S.sbuf("lneps", [128, 1], F32)
        lo = S.sbuf("lo", [96, SEQ], BF16)
        th = [S.sbuf("th%d" % i, [32, SEQ], BF16) for i in range(2)]
        ad = [S.sbuf("ad%d" % i, [32, SEQ], BF16) for i in range(2)]
        sgd = S.sbuf("sgd", [96, SEQ], BF16)
        vt = S.sbuf("vtr", [128, NB, 256], BF16)
        rk = S.sbuf("rk", [128, 2, SEQ], BF16)
        XS = [[S.sbuf("X%d_%d" % (g, i), [128, HS], F32) for i in range(7)] for g in range(2)]
        X = XS[0]
        KR = S.sbuf("KR", [128, 2, NB, 2, 128], BF16)
        BT = S.sbuf("BT", [128, 2, SEQ], BF16)
        KT = S.sbuf("KT", [128, 2, SEQ], BF16)
        tlT = [S.sbuf("tlT%d" % i, [128, SEQ], BF16) for i in range(2)]
        tl = [S.sbuf("tl%d" % i, [128, NB, 256], BF16) for i in range(2)]
        gam = S.sbuf("gamr", [128, 2, NB], F32)
        yacc = S.sbuf("yacc", [128, 2, SEQ], F32)
        bonus = S.sbuf("bonus", [128, 2, SEQ], BF16)
        krz = [S.sbuf("krz%d" % i, [128, 4, 256], BF16) for i in range(2)]
        W4 = [S.sbuf("W4_%d" % i, [128, 4, 4, 128], BF16) for i in range(2)]
        Bbm = [S.sbuf("Bbm%d" % i, [128, 4, 128], BF16) for i in range(2)]
        Mt = [S.sbuf("Mt%d" % i, [128, 4, 128], BF16) for i in range(2)]
        Nt = [S.sbuf("Nt%d" % i, [128, 4, 128], BF16) for i in range(2)]
        Sb = [S.sbuf("Sb%d" % i, [128, 4, 128], BF16) for i in range(2)]
        Gs = S.sbuf("Gs", [128, 4, 64], BF16)
        Ps = S.sbuf("Ps", [128, 4, 64], BF16)
        Hf = S.sbuf("Hfr", [128, 2, 64], F32)
        Ht = S.sbuf("Htr", [128, 2, 64], F32)
        Hb = S.sbuf("Hbr", [128, 2, 64], BF16)
        yst = [S.sbuf("ystr%d" % i, [128, 512], BF16) for i in range(2)]
        bk = [S.psum("rb%d" % i, [128, 512], F32) for i in range(6)]
        bQ, bR, bG, bY, bH = (bk[0], bk[1]), bk[2], bk[3], bk[4], bk[5]
        for i, nm in enumerate(("w2_f", "w2_b")):
            S.dma("pool", w2[i][:], A[nm], writes=[w2[i]])
        for i, nm in enumerate(("a2_f", "a2_b")):
            S.dma("pool", a2[i][:], A[nm], writes=[a2[i]])
        S.dma("pool", g2[:], A["g2"], writes=[g2])
        S.dma("sp", rvec[:], A["rvec"], writes=[rvec])
        S.op("dve", "tensor_scalar", [rvec], [omka], out=omka[:], in0=rvec[:, RV["k_a"] * 4:RV["k_a"] * 4 + 4],
             scalar1=-1.0, scalar2=1.0, op0=ALU.mult, op1=ALU.add)
        S.op("dve", "memset", [], [lnb_eps], ap=lnb_eps[:], constant=64e-5)

        def rv(name, hp):
            c = RV[name] * 4 + hp
            return rvec[:, c:c + 1]

        for b in range(nseq):
            if RW_CUT < 1:
                continue
            tb = b * SEQ
            za = zT.ap()
            for i in range(2):
                S.dma("sp", lo[0:32, :], za[ZR_LORA + 32 * i:ZR_LORA + 32 * i + 32, tb:tb + SEQ], reads=[zT], writes=[lo])
                S.op("act", "activation", [lo], [th[i]], out=th[i][:], in_=lo[0:32, :], func=AF.Tanh)
            for i in range(2):
                S.dma("sp", ad[i][:], za[ZR_LORA + 64 + 32 * i:ZR_LORA + 96 + 32 * i, tb:tb + SEQ], reads=[zT],
                      writes=[ad[i]])
            S.dma("sp", lo[:], za[ZR_GD:ZR_GD + 96, tb:tb + SEQ], reads=[zT], writes=[lo])
            S.op("act", "activation", [lo], [sgd], out=sgd[:], in_=lo[:], func=AF.Sigmoid)
            for grp in range(2):
                if RW_CUT < 2:
                    continue
                S.dma("sp", vt[:], zv.ap()[tb:tb + SEQ, 512 + grp * 256:512 + (grp + 1) * 256]
                      .rearrange("(n p) d -> p n d", p=128), reads=[zv], writes=[vt])
                for d in range(2):
                    for hl in range(2):
                        hp = grp * 2 + hl
                        if d == 0:
                            S.dma("sp", rk[:, 0, :], za[ZR_RR + hp * 128:ZR_RR + (hp + 1) * 128, tb:tb + SEQ], reads=[zT],
                                  writes=[(rk, 0)])
                            S.dma("sp", rk[:, 1, :], za[ZR_RK + hp * 128:ZR_RK + (hp + 1) * 128, tb:tb + SEQ], reads=[zT],
                                  writes=[(rk, 1)])
                        elif hl == 0 or True:
                            S.dma("sp", rk[:, 0, :], za[ZR_RR + hp * 128:ZR_RR + (hp + 1) * 128, tb:tb + SEQ], reads=[zT],
                                  writes=[(rk, 0)])
                            S.dma("sp", rk[:, 1, :], za[ZR_RK + hp * 128:ZR_RK + (hp + 1) * 128, tb:tb + SEQ], reads=[zT],
                                  writes=[(rk, 1)])
                        def prep_gen(hs, Xs, bnk):
                            c0 = hs * HS
                            hc = slice(c0, c0 + HS)
                            nbh = HS // 128
                            bsl = slice(hs * nbh, (hs + 1) * nbh)
                            SG, CS, A3, A2, KP, BE, SB = Xs
                            S.op("pe", "matmul", [w2[d], th[d]], [bnk], out=bnk[:], lhsT=w2[d][:, hp * 128:(hp + 1) * 128],
                                 rhs=th[d][:, hc], start=True, stop=True)
                            S.op("act", "activation", [bnk, rvec], [SG], out=SG[:], in_=bnk[:], func=AF.Sigmoid,
                                 bias=rv("w0_f" if d == 0 else "w0_b", hp))
                            yield
                            S.op("dve", "tensor_tensor_scan", [C.rmask128, SG], [CS], out=CS[:], data0=C.rmask128[:, 0:HS],
                                 data1=SG[:], initial=0.0, op0=ALU.mult, op1=ALU.add)
                            yield
                            cs3 = v3(CS[:], 128)
                            S.op("dve", "tensor_tensor", [CS], [A3], out=v3(A3[:], 128),
                                 in0=cs3[:, :, 127:128].to_broadcast([128, nbh, 128]), in1=cs3, op=ALU.subtract)
                            S.op("act", "activation", [CS], [gam], out=gam[:, hl, bsl], in_=cs3[:, :, 127], func=AF.Exp,
                                 scale=RW_C)
                            S.op("pool", "tensor_tensor", [CS, SG], [A2], out=A2[:], in0=CS[:], in1=SG[:], op=ALU.subtract)
                            yield
                            if d == 0:
                                bI, bE, tail = CS, A2, A3
                            else:
                                S.op("dve", "tensor_tensor", [A3, SG], [CS], out=CS[:], in0=A3[:], in1=SG[:], op=ALU.add)
                                bI, bE, tail = CS, A3, A2
                                yield
                            AV = SG
                            S.op("pe", "matmul", [a2[d], ad[d]], [bnk], out=bnk[:], lhsT=a2[d][:, hp * 128:(hp + 1) * 128],
                                 rhs=ad[d][:, hc], start=True, stop=True)
                            S.op("act", "activation", [bnk, rvec], [AV], out=AV[:], in_=bnk[:], func=AF.Sigmoid,
                                 bias=rv("a0_f" if d == 0 else "a0_b", hp))
                            yield
                            S.op("act", "activation", [(rk, 1), rvec], [BE], out=BE[:], in_=rk[:, 1, hc], func=AF.Square,
                                 scale=rv("k_k", hp))
                            yield
                            S.op("pe", "matmul", [C.onesblk, BE], [bnk], out=bnk[:], lhsT=C.onesblk[:], rhs=BE[:],
                                 start=True, stop=True)
                            S.op("act", "activation", [bnk], [SB], out=SB[:], in_=bnk[:], func=AF.Sqrt)
                            yield
                            S.op("dve", "tensor_scalar", [SB], [SB], out=SB[:], in0=SB[:], scalar1=1e-6, scalar2=None,
                                 op0=ALU.max)
                            S.op("dve", "reciprocal", [SB], [SB], out=SB[:], in_=SB[:])
                            yield
                            S.op("dve", "scalar_tensor_tensor", [(rk, 1), rvec, SB], [KP], out=KP[:], in0=rk[:, 1, hc],
                                 scalar=rv("k_k", hp), in1=SB[:], op0=ALU.mult, op1=ALU.mult)
                            yield
                            S.op("dve", "tensor_tensor", [KP, AV], [BE], out=BE[:], in0=KP[:], in1=AV[:], op=ALU.mult)
                            yield
                            S.op("dve", "tensor_scalar", [AV, rvec, omka], [AV], out=AV[:], in0=AV[:], scalar1=rv("k_a", hp),
                                 scalar2=omka[:, hp:hp + 1], op0=ALU.mult, op1=ALU.add)
                            yield
                            S.op("pool", "tensor_tensor", [(rk, 1), AV], [AV], out=AV[:], in0=rk[:, 1, hc], in1=AV[:],
                                 op=ALU.mult)
                            yield
                            KD = AV
                            if d == 0:
                                S.op("dve", "scalar_tensor_tensor", [(rk, 0), rvec, KD], [SB], out=SB[:], in0=rk[:, 0, hc],
                                     scalar=rv("r_k", hp), in1=KD[:], op0=ALU.mult, op1=ALU.mult)
                                yield
                                S.op("pe", "matmul", [C.onesblk, SB], [bnk], out=bnk[:], lhsT=C.onesblk[:], rhs=SB[:],
                                     start=True, stop=True)
                                psT = C.psT[C.rr % 2]
                                C.rr += 1
                                for j in range(4):
                                    blk = c0 // 128 + j
                                    S.op("pe", "transpose", [vt, C.ident], [psT], out=psT[:, j, :],
                                         in_=vt[:, blk, hl * 128:(hl + 1) * 128], identity=C.ident[:])
                                S.op("act", "activation", [bnk], [SB], out=SB[:], in_=bnk[:], func=AF.Copy)
                                yield
                                S.op("dve", "tensor_tensor", [psT, SB], [(bonus, hl)], out=v3(bonus[:, hl, hc]),
                                     in0=psT[:, 0:4, :], in1=v3(SB[:]), op=ALU.mult)
                                yield
                            S.op("act", "activation", [bE], [bE], out=bE[:], in_=bE[:], func=AF.Exp, scale=RW_C)
                            yield
                            S.op("act", "activation", [bI], [bI], out=bI[:], in_=bI[:], func=AF.Exp, scale=RW_C)
                            yield
                            S.op("act", "activation", [tail], [tail], out=tail[:], in_=tail[:], func=AF.Exp, scale=RW_C)
                            yield
                            S.op("dve", "tensor_tensor", [KP, bE], [(KR, hl)], out=KR[:, hl, bsl, 0, :], in0=v3(KP[:]),
                                 in1=v3(bE[:]), op=ALU.mult)
                            yield
                            rdec = bI if d == 0 else bE
                            S.op("dve", "tensor_tensor", [(rk, 0), rdec], [(KR, hl)], out=KR[:, hl, bsl, 1, :],
                                 in0=v3(rk[:, 0, hc]), in1=v3(rdec[:]), op=ALU.mult)
                            yield
                            S.op("pool", "tensor_tensor", [BE, tail], [(tlT[0], hs)], out=tlT[0][:, hc], in0=BE[:],
                                 in1=tail[:], op=ALU.mult)
                            yield
                            S.op("pool", "tensor_tensor", [KD, tail], [(tlT[1], hs)], out=tlT[1][:, hc], in0=KD[:],
                                 in1=tail[:], op=ALU.mult)
                            yield
                            S.op("dve", "reciprocal", [bI], [bI], out=bI[:], in_=bI[:])
                            yield
                            S.op("dve", "tensor_tensor", [BE, bI], [(BT, hl)], out=BT[:, hl, hc], in0=BE[:], in1=bI[:],
                                 op=ALU.mult)
                            yield
                            S.op("pool", "tensor_tensor", [KD, bI], [(KT, hl)], out=KT[:, hl, hc], in0=KD[:], in1=bI[:],
                                 op=ALU.mult)
                            yield

                        for h0 in range(0, SEQ // HS, 2):
                            gens = [prep_gen(h0 + g_, XS[g_], bk[g_]) for g_ in range(2)]
                            while gens:
                                for g in list(gens):
                                    try:
                                        next(g)
                                    except StopIteration:
                                        gens.remove(g)
                        if RW_CUT < 6:
                            continue
                        for blk in range(NB):
                            psT = C.psT[C.rr % 2]
                            C.rr += 1
                            for i in range(2):
                                S.op("pe", "transpose", [tlT[i], C.ident], [psT], out=psT[:, i, :],
                                     in_=tlT[i][:, blk * 128:(blk + 1) * 128], identity=C.ident[:])
                            for i in range(2):
                                if blk % 2:
                                    S.op("act", "activation", [psT], [(tl[i], hl)], out=tl[i][:, blk, hl * 128:(hl + 1) * 128],
                                         in_=psT[:, i, :], func=AF.Copy)
                                else:
                                    S.op("dve", "tensor_copy", [psT], [(tl[i], hl)], out=tl[i][:, blk, hl * 128:(hl + 1) * 128],
                                         in_=psT[:, i, :])
                    if RW_STAGE < 2:
                        continue
                    S.op("dve", "memset", [], [Hf], ap=Hf[:], constant=0.0)
                    S.op("dve", "memset", [], [Hb], ap=Hb[:], constant=0.0)
                    m4 = C.m4_f if d == 0 else C.m4_b
                    mB = C.mB_f if d == 0 else C.mB_b
                    blocks = range(NB) if d == 0 else range(NB - 1, -1, -1)
                    blocks = list(blocks)

                    def pre_inv(bi, blk):
                        th = []
                        bc = slice(blk * 128, (blk + 1) * 128)
                        kz, w4, bbm, sb = krz[bi % 2], W4[bi % 2], Bbm[bi % 2], Sb[bi % 2]

                        def t_pre(h4):
                            def f():
                                hl, par = h4 // 2, h4 % 2
                                q = bQ[h4 % 2]
                                if h4 % 2:
                                    S.op("act", "activation", [(KR, hl), C.pmask], [(kz, h4)], out=kz[:, h4, :],
                                         in_=KR[:, hl, blk, :, :].rearrange("p a b -> p (a b)"), func=AF.Copy,
                                         scale=C.pmask[:, par:par + 1])
                                else:
                                    S.op("pool", "tensor_scalar", [(KR, hl), C.pmask], [(kz, h4)], out=kz[:, h4, :],
                                         in0=KR[:, hl, blk, :, :].rearrange("p a b -> p (a b)"),
                                         scalar1=C.pmask[:, par:par + 1], scalar2=None, op0=ALU.mult)
                                S.op("pe", "matmul", [(BT, hl), (kz, h4)], [q], out=q[:, 0:128], lhsT=BT[:, hl, bc],
                                     rhs=kz[:, h4, 0:128], start=True, stop=True)
                                S.op("pe", "matmul", [(BT, hl), (kz, h4)], [q], out=q[:, 128:256], lhsT=kz[:, h4, 0:128],
                                     rhs=BT[:, hl, bc], start=False, stop=True, skip_group_check=True)
                                S.op("pe", "matmul", [(KT, hl), (kz, h4)], [q], out=q[:, 256:512], lhsT=KT[:, hl, bc],
                                     rhs=kz[:, h4, :], start=False, stop=True, skip_group_check=True)
                                S.op("pe", "matmul", [(BT, hl), (kz, h4)], [bR], out=bR[:, h4 * 128:(h4 + 1) * 128],
                                     lhsT=BT[:, hl, bc], rhs=kz[:, h4, 128:256], start=(h4 == 0), stop=True,
                                     skip_group_check=True)
                                S.op("dve", "tensor_tensor", [q, m4], [(w4, h4)], out=w4[:, h4, :, :], in0=v3(q[:]),
                                     in1=m4[:], op=ALU.mult)
                            return f

                        for h4 in range(4):
                            th.append(t_pre(h4))

                        def t_s0():
                            S.op("act", "activation", [bR], [bbm], out=bbm[:], in_=v3(bR[:]), func=AF.Copy)
                            S.op("pool", "tensor_tensor", [bbm, mB], [bbm], out=bbm[:], in0=bbm[:],
                                 in1=mB[:].to_broadcast([128, 4, 128]), op=ALU.mult)
                            S.op("dve", "tensor_tensor", [w4, C.ident_f], [sb], out=sb[:], in0=w4[:, :, 0, :],
                                 in1=C.ident_f[:].to_broadcast([128, 4, 128]), op=ALU.add)
                        th.append(t_s0)
                        st_ = {"Mp": w4[:, :, 0, :], "Np": w4[:, :, 1, :], "Mr": [w4], "Nr": [w4]}

                        def t_lvA(lv):
                            def f():
                                Mn, Nn = Mt[lv % 2], Nt[lv % 2]
                                Mp, Np = st_["Mp"], st_["Np"]
                                rd = st_["Mr"] + st_["Nr"]
                                for h4 in range(4):
                                    if lv < 6:
                                        S.op("pe", "matmul", rd, [bQ[0]], out=bQ[0][:, h4 * 128:(h4 + 1) * 128],
                                             lhsT=Np[:, h4, :], rhs=Mp[:, h4, :], start=(h4 == 0), stop=True,
                                             skip_group_check=True)
                                    S.op("pe", "matmul", rd, [bQ[1]], out=bQ[1][:, h4 * 128:(h4 + 1) * 128],
                                         lhsT=Mp[:, h4, :], rhs=Np[:, h4, :], start=(h4 == 0), stop=True,
                                         skip_group_check=True)
                                if lv < 6:
                                    S.op("act", "activation", [bQ[0]], [Mn], out=Mn[:], in_=v3(bQ[0][:]), func=AF.Copy)
                                S.op("dve", "tensor_copy", [bQ[1]], [Nn], out=Nn[:], in_=v3(bQ[1][:]))
                                st_["Mp"], st_["Np"], st_["Mr"], st_["Nr"] = Mn[:], Nn[:], [Mn], [Nn]
                            return f

                        def t_lvB(lv):
                            def f():
                                Nn = Nt[lv % 2]
                                for h4 in range(4):
                                    S.op("pe", "matmul", [Nn, sb], [bR], out=bR[:, h4 * 128:(h4 + 1) * 128],
                                         lhsT=Nn[:, h4, :], rhs=sb[:, h4, :], start=(h4 == 0), stop=True,
                                         skip_group_check=True)
                                S.op("dve", "tensor_tensor", [sb, bR], [sb], out=sb[:], in0=sb[:], in1=v3(bR[:]),
                                     op=ALU.add)
                            return f

                        for lv in range(1, 7):
                            th.append(t_lvA(lv))
                            th.append(t_lvB(lv))
                        return th

                    def chain(bi, blk):
                        th = []
                        bc = slice(blk * 128, (blk + 1) * 128)
                        kz, w4, bbm, sb = krz[bi % 2], W4[bi % 2], Bbm[bi % 2], Sb[bi % 2]
                        gG = bG[:, 0:256].rearrange("p (a b) -> p a b", b=64)
                        gP = bG[:, 256:512].rearrange("p (a b) -> p a b", b=64)
                        yv = bY[:, 0:256].rearrange("p (a b) -> p a b", b=128)
                        hv = bH[:, 0:128].rearrange("p (a b) -> p a b", b=64)

                        def t_g():
                            for h4 in range(4):
                                hl = h4 // 2
                                S.op("pe", "matmul", [(kz, h4), Hb], [bG], out=gG[:, h4, :], lhsT=kz[:, h4, 0:128],
                                     rhs=Hb[:, hl, :], start=(h4 == 0), stop=False, skip_group_check=True)
                                S.op("pe", "matmul", [(w4, h4), vt], [bG], out=gG[:, h4, :], lhsT=w4[:, h4, 2, :],
                                     rhs=vt[:, blk, h4 * 64:(h4 + 1) * 64], start=False, stop=True, skip_group_check=True)
                            S.op("act", "activation", [bG], [Gs], out=Gs[:], in_=gG, func=AF.Copy)

                        def t_p():
                            for h4 in range(4):
                                S.op("pe", "matmul", [sb, Gs], [bG], out=gP[:, h4, :], lhsT=sb[:, h4, :], rhs=Gs[:, h4, :],
                                     start=False, stop=True, skip_group_check=True)
                            S.op("act", "activation", [bG], [Ps], out=Ps[:], in_=gP, func=AF.Copy, scale=-1.0)

                        def t_h():
                            for h4 in range(4):
                                hl, par = h4 // 2, h4 % 2
                                po = slice(par * 64, (par + 1) * 64)
                                vh = vt[:, blk, h4 * 64:(h4 + 1) * 64]
                                S.op("pe", "matmul", [(tl[1], hl), vt], [bH], out=hv[po, hl, :],
                                     lhsT=tl[1][:, blk, h4 * 64:(h4 + 1) * 64], rhs=vh, start=(h4 < 2), stop=False,
                                     skip_group_check=True)
                                S.op("pe", "matmul", [(tl[0], hl), Ps], [bH], out=hv[po, hl, :],
                                     lhsT=tl[0][:, blk, h4 * 64:(h4 + 1) * 64], rhs=Ps[:, h4, :], start=False, stop=True,
                                     skip_group_check=True)

                        def t_y():
                            for h4 in range(4):
                                hl, par = h4 // 2, h4 % 2
                                po = slice(par * 64, (par + 1) * 64)
                                vh = vt[:, blk, h4 * 64:(h4 + 1) * 64]
                                S.op("pe", "matmul", [Hb, (kz, h4)], [bY], out=yv[po, hl, :], lhsT=Hb[:, hl, :],
                                     rhs=kz[:, h4, 128:256], start=(h4 < 2), stop=False, skip_group_check=True)
                                S.op("pe", "matmul", [vt, (w4, h4)], [bY], out=yv[po, hl, :], lhsT=vh, rhs=w4[:, h4, 3, :],
                                     start=False, stop=False, skip_group_check=True)
                                S.op("pe", "matmul", [Ps, bbm], [bY], out=yv[po, hl, :], lhsT=Ps[:, h4, :],
                                     rhs=bbm[:, h4, :], start=False, stop=True, skip_group_check=True)

                        def t_upd():
                            S.op("pool", "tensor_tensor", [Hf, gam], [Ht], out=Ht[:], in0=Hf[:],
                                 in1=gam[:, :, blk:blk + 1].to_broadcast([128, 2, 64]), op=ALU.mult)
                            S.op("dve", "tensor_tensor", [Ht, bH], [Hf], out=Hf[:], in0=Ht[:], in1=hv, op=ALU.add)
                            S.op("act", "activation", [Hf], [Hb], out=Hb[:], in_=Hf[:], func=AF.Copy)
                            if d == 0:
                                S.op("act", "activation", [bY], [yacc], out=yacc[:, :, bc], in_=yv, func=AF.Copy)
                            else:
                                S.op("dve", "tensor_tensor", [bY, yacc], [yacc], out=yacc[:, :, bc], in0=yv,
                                     in1=yacc[:, :, bc], op=ALU.add)

                        return [t_g, t_p, t_y, t_h, t_upd]

                    for f in pre_inv(0, blocks[0]):
                        f()
                    for bi, blk in enumerate(blocks):
                        ch = chain(bi, blk)
                        nx = pre_inv(bi + 1, blocks[bi + 1]) if bi + 1 < NB else []
                        pos = {1: 0, 4: 1, 7: 2, 9: 3, 11: 4}
                        for i, f in enumerate(nx):
                            f()
                            if i in pos:
                                ch[pos[i]]()
                        if not nx:
                            for f in ch:
                                f()
                if RW_STAGE < 3:
                    continue
                k = 0
                for hl in range(2):
                    hp = grp * 2 + hl
                    YC, SQ, RS = X[0], X[1], X[2]
                    for q in range(4):
                        qc = slice(q * 512, (q + 1) * 512)
                        S.op("pe", "matmul", [C.onesblk, yacc], [bG], out=bG[:], lhsT=C.onesblk[:], rhs=yacc[:, hl, qc],
                             start=True, stop=True)
                        S.op("dve", "scalar_tensor_tensor", [bG, yacc], [YC], out=YC[:, 0:512], in0=bG[:], scalar=-1.0 / 64,
                             in1=yacc[:, hl, qc], op0=ALU.mult, op1=ALU.add)
                        S.op("act", "activation", [YC], [SQ], out=SQ[:, 0:512], in_=YC[:, 0:512], func=AF.Square)
                        S.op("pe", "matmul", [C.onesblk, SQ], [bY], out=bY[:], lhsT=C.onesblk[:], rhs=SQ[:, 0:512],
                             start=True, stop=True)
                        S.op("act", "activation", [bY, lnb_eps], [RS], out=RS[:, 0:512], in_=bY[:], func=AF.Sqrt,
                             scale=1.0 / 64, bias=lnb_eps[:, 0:1])
                        S.op("dve", "reciprocal", [RS], [RS], out=RS[:, 0:512], in_=RS[:, 0:512])
                        S.op("dve", "tensor_tensor", [YC, RS], [YC], out=YC[:, 0:512], in0=YC[:, 0:512], in1=RS[:, 0:512],
                             op=ALU.mult)
                        S.op("dve", "tensor_scalar", [YC, rvec], [YC], out=YC[:, 0:512], in0=YC[:, 0:512],
                             scalar1=rv("ln_g", hp), scalar2=rv("ln_b", hp), op0=ALU.mult, op1=ALU.add)
                        S.op("pool", "tensor_tensor", [YC, (bonus, hl)], [YC], out=YC[:, 0:512], in0=YC[:, 0:512],
                             in1=bonus[:, hl, qc], op=ALU.add)
                        S.op("pe", "matmul", [g2, sgd], [bH], out=bH[:], lhsT=g2[:, hp * 128:(hp + 1) * 128], rhs=sgd[:, qc],
                             start=True, stop=True)
                        ys = yst[k % 2]
                        k += 1
                        S.op("dve", "tensor_tensor", [YC, bH], [ys], out=ys[:], in0=YC[:, 0:512], in1=bH[:], op=ALU.mult)
                        S.dma("sp", yT.ap()[1024 + hp * 128:1024 + (hp + 1) * 128, tb + q * 512:tb + (q + 1) * 512], ys[:],
                              reads=[ys], writes=[(yT, "r%d_%d_%d" % (b, hp, q))])


NCORES = 8
DEPTH = 2
_bf = ml_dtypes.bfloat16

CONST_SPECS = (("ident_bf", [128, 128], BF16), ("rmask", [128, 2048], F32), ("mask_f", [128, 128], F32),
               ("mask_b", [128, 128], F32), ("rmask128", [128, 1024], F32), ("onesblk", [128, 128], F32),
               ("pmask", [128, 2], F32), ("ident_f", [128, 128], F32), ("m4_f", [128, 4, 128], F32),
               ("m4_b", [128, 4, 128], F32), ("mB_f", [128, 128], F32), ("mB_b", [128, 128], F32),
               ("dftS", [2, 2048, 2048], BF16), ("dftC", [128, 256], BF16))
W_SPECS = (("ffn1_gate", [DEPTH, D, DFF]), ("ffn1_up", [DEPTH, D, DFF]), ("ffn1_down", [DEPTH, DFF, D]),
           ("w_in", [DEPTH, D, 6912]), ("gla_up_f", [DEPTH, 16, 256]), ("gla_up_b", [DEPTH, 16, 256]),
           ("rwkv_w2_f", [DEPTH, 32, 512]), ("rwkv_w2_b", [DEPTH, 32, 512]), ("rwkv_a2_f", [DEPTH, 32, 512]),
           ("rwkv_a2_b", [DEPTH, 32, 512]), ("rwkv_g2", [DEPTH, 96, 512]), ("proj_gla", [DEPTH, 512, D]),
           ("proj_fnet", [DEPTH, 512, D]), ("proj_rwkv", [DEPTH, 512, D]), ("w_out", [DEPTH, D, D]),
           ("ffn2_gate", [DEPTH, D, DFF]), ("ffn2_up", [DEPTH, D, DFF]), ("ffn2_down", [DEPTH, DFF, D]),
           ("gcols", [DEPTH, 3, 128, 8]), ("mub", [DEPTH, 128, 1792]), ("gvec", [DEPTH, 128, 8]),
           ("rvec", [DEPTH, 128, 36]), ("grow", [128, D]))


def host_constants():
    ii = np.arange(128)
    same = (ii[:, None] // 64) == (ii[None, :] // 64)
    LT = (ii[:, None] < ii[None, :]).astype(np.float32)
    GT = (ii[:, None] > ii[None, :]).astype(np.float32)
    LE = (ii[:, None] <= ii[None, :]).astype(np.float32)
    rm = np.ones((128, 2048), np.float32)
    rm[:, ::64] = 0
    rm128 = np.ones((128, 1024), np.float32)
    rm128[:, ::128] = 0
    ob = np.zeros((128, 128), np.float32)
    ob[:64, :64] = 1
    ob[64:, 64:] = 1
    pm = np.zeros((128, 2), np.float32)
    pm[:64, 0] = 1
    pm[64:, 1] = 1
    s = np.arange(2048)
    ang = 2 * np.pi * ((np.outer(s, s) % 2048).astype(np.float64)) / 2048
    c = np.arange(128)
    angc = 2 * np.pi * ((np.outer(c, c) % 128).astype(np.float64)) / 128
    sc = 1.0 / np.sqrt(2048 * 128)
    return {
        "ident_bf": np.eye(128, dtype=_bf), "rmask": rm,
        "mask_f": (same & (ii[:, None] <= ii[None, :])).astype(np.float32),
        "mask_b": (same & (ii[:, None] > ii[None, :])).astype(np.float32),
        "rmask128": rm128, "onesblk": ob, "pmask": pm, "ident_f": np.eye(128, dtype=np.float32),
        "m4_f": np.ascontiguousarray(np.stack([-LT, -GT, LT, LE], axis=1)),
        "m4_b": np.ascontiguousarray(np.stack([-GT, -LT, GT, GT], axis=1)),
        "mB_f": LE, "mB_b": GT,
        "dftS": np.stack([np.cos(ang), -np.sin(ang)]).astype(_bf),
        "dftC": (np.concatenate([np.cos(angc), np.sin(angc)], axis=1) * sc).astype(_bf),
    }


def host_layouts(inp):
    L = DEPTH
    f32 = np.float32
    gcols = np.zeros((L, 3, 128, 8), f32)
    mub = np.zeros((L, 128, 1792), f32)
    gvec = np.zeros((L, 128, 8), f32)
    rvec = np.zeros((L, 128, 36), f32)
    for l in range(L):
        for i, nm in enumerate(("ffn1_norm", "mix_norm", "ffn2_norm")):
            gcols[l, i] = np.asarray(inp[nm][l], f32).reshape(8, 128).T
        mu = np.asarray(inp["rwkv_mu"][l], f32)
        muz = np.zeros(1792, f32)
        muz[0:512] = mu[0:512]
        muz[512:1024] = mu[512:1024]
        muz[1024:1152] = mu[1536:1664]
        muz[1152:1248] = mu[1664:1760]
        muz[1280:1792] = mu[1024:1536]
        mub[l] = np.broadcast_to(muz, (128, 1792))
        gvec[l, :, 0:2] = np.asarray(inp["gla_bias_f"][l], f32).reshape(2, 128).T
        gvec[l, :, 2:4] = np.asarray(inp["gla_bias_b"][l], f32).reshape(2, 128).T
        gvec[l, :, 4:8] = np.asarray(inp["gla_norm"][l], f32).reshape(4, 128).T
        for i, nm in enumerate(("rwkv_w0_f", "rwkv_w0_b", "rwkv_a0_f", "rwkv_a0_b", "rwkv_k_k", "rwkv_k_a", "rwkv_r_k",
                                "rwkv_ln_g", "rwkv_ln_b")):
            rvec[l, :, i * 4:(i + 1) * 4] = np.asarray(inp[nm][l], f32).reshape(4, 128).T
    grow = np.ascontiguousarray(np.broadcast_to(np.asarray(inp["final_norm"], f32), (128, D)))
    return {"gcols": gcols, "mub": mub, "gvec": gvec, "rvec": rvec, "grow": grow}


def build_program():
    nc = bass.Bass("TRN2", target_bir_lowering=False)
    x = nc.dram_tensor("x", [T, D], F32, kind="ExternalInput")
    out = nc.dram_tensor("out", [T, D], F32, kind="ExternalOutput")
    cn = {}
    for name, shp, dt in CONST_SPECS:
        cn[name] = nc.dram_tensor(name, shp, dt, kind="ExternalInput").ap()
    W = {}
    for name, shp in W_SPECS:
        W[name] = nc.dram_tensor(name, shp, F32, kind="ExternalInput").ap()
    xs = [Buf(nc.dram_tensor("xres%d" % i, [T, D], F32), "xres%d" % i) for i in range(3)]
    import os
    dk = {"kind": "ExternalOutput"} if os.environ.get("K_DUMP") else {}
    zT = Buf(nc.dram_tensor("zT", [ZROWS, T], BF16, **dk), "zT")
    zv = Buf(nc.dram_tensor("zv", [T, 1024], BF16, **dk), "zv")
    yT = Buf(nc.dram_tensor("yT", [1536, T], BF16, **dk), "yT")
    xin = Buf(x, "x")
    outb = Buf(out, "out")
    with contextlib.ExitStack() as st:
        S = Sched(nc, st)
        C = Ctx()
        setup_consts(S, C, cn)
        cur = xin
        import os
        NPH = int(os.environ.get("K_NPH", "99"))
        for l in range(DEPTH):
            if l * 7 + 0 >= NPH:
                break
            ffn_phase(S, C, cur, xs[0], W["gcols"][l, 0], W["ffn1_gate"][l], W["ffn1_up"][l], W["ffn1_down"][l])
            if l * 7 + 1 >= NPH:
                break
            mixin_phase(S, C, xs[0], zT, zv, W["gcols"][l, 1], W["w_in"][l], W["mub"][l])
            if l * 7 + 2 >= NPH:
                break
            gla_phase(S, C, zT, zv, yT, {"up_f": W["gla_up_f"][l], "up_b": W["gla_up_b"][l], "gvec": W["gvec"][l]})
            if l * 7 + 3 >= NPH:
                break
            fnet_phase(S, C, zT, yT, cn)
            if l * 7 + 4 >= NPH:
                break
            if os.environ.get("K_SKIP_R0") and l == 0:
                pass
            else:
              rwkv_phase(S, C, zT, zv, yT, {"w2_f": W["rwkv_w2_f"][l], "w2_b": W["rwkv_w2_b"][l],
                                          "a2_f": W["rwkv_a2_f"][l], "a2_b": W["rwkv_a2_b"][l],
                                          "g2": W["rwkv_g2"][l], "rvec": W["rvec"][l]})
            if l * 7 + 5 >= NPH:
                break
            merge_phase(S, C, xs[0], xs[1], yT, W["gcols"][l, 1], W["w_in"][l],
                        [W["proj_gla"][l], W["proj_fnet"][l], W["proj_rwkv"][l]], W["w_out"][l])
            if l * 7 + 6 >= NPH:
                break
            ffn_phase(S, C, xs[1], xs[2], W["gcols"][l, 2], W["ffn2_gate"][l], W["ffn2_up"][l], W["ffn2_down"][l])
            cur = xs[2]
        final_norm_phase(S, C, cur, outb, W["grow"])
        S.emit()
    return nc


_CACHE = {}


def kernel(**inputs):
    inp = {k: np.asarray(v) for k, v in inputs.items()}
    if "nc" not in _CACHE:
        _CACHE["nc"] = build_program()
        _CACHE["consts"] = host_constants()
    nc = _CACHE["nc"]
    shared = dict(_CACHE["consts"])
    shared.update(host_layouts(inp))
    for name, _ in W_SPECS:
        if name not in shared:
            shared[name] = np.ascontiguousarray(inp[name], dtype=np.float32)
    x = np.ascontiguousarray(inp["x"], dtype=np.float32).reshape(NCORES, T, D)
    in_maps = []
    for c in range(NCORES):
        m = dict(shared)
        m["x"] = x[c]
        in_maps.append(m)
    res = run_bass_kernel_spmd(nc, in_maps, core_ids=list(range(NCORES)))
    out = np.stack([np.asarray(r["out"]) for r in res.results], axis=0)
    return out.reshape(16, 2048, D).astype(np.float32)
```
